# Optimizing a Trainium2 kernel written in Bass

```python
import jax, jax.numpy as jnp
from jax import lax
import numpy as np

D_MODEL = 2048
BATCH = 2
SEQ = 16384
DEPTH = 4

N_META = 16
RET_HEADS = 8
RET_DK = D_MODEL // 16
RET_DV = 2 * RET_DK
RET_QK_W = RET_HEADS * RET_DK
RET_V_W = RET_HEADS * RET_DV
CONV_W = D_MODEL
CONV_K = 3
CHUNK = 128
ROPE_BASE = 10000.0
EPS = 1e-6
N_BRANCH = 2
IN_SIZES = (RET_QK_W, RET_QK_W, RET_V_W, RET_V_W, CONV_W, CONV_W, CONV_W, CONV_W, D_MODEL, D_MODEL)
IN_COLS = RET_QK_W * 2 + RET_V_W * 2 + CONV_W * 4 + D_MODEL * 2

kernel_name = "hybrid_retention_shortconv_gated_merge"


def rms_norm(x, g):
    xf = x.astype(jnp.float32)
    y = xf * lax.rsqrt(jnp.mean(xf * xf, axis=-1, keepdims=True) + EPS)
    return (y * g.astype(jnp.float32)).astype(x.dtype)


def split_columns(proj):
    points = []
    acc = 0
    for s in IN_SIZES[:-1]:
        acc += s
        points.append(acc)
    return jnp.split(proj, points, axis=-1)


def rotary(x, pos):
    half = x.shape[-1] // 2
    inv_freq = ROPE_BASE ** (-jnp.arange(half, dtype=jnp.float32) / half)
    ang = pos.astype(jnp.float32)[:, None] * inv_freq[None, :]
    cos = jnp.cos(ang)[None, :, None, :]
    sin = jnp.sin(ang)[None, :, None, :]
    xf = x.astype(jnp.float32)
    x1, x2 = xf[..., :half], xf[..., half:]
    out = jnp.concatenate([x1 * cos - x2 * sin, x1 * sin + x2 * cos], axis=-1)
    return out.astype(x.dtype)


def retention_chunkwise(q, k, v):
    B, L, H, dk = q.shape
    dv = v.shape[-1]
    pad = (-L) % CHUNK
    padw = ((0, 0), (pad, 0), (0, 0), (0, 0))
    q, k, v = [jnp.pad(t.astype(jnp.float32), padw) for t in (q, k, v)]
    n = (L + pad) // CHUNK

    def to_chunks(t):
        return t.reshape(B, n, CHUNK, H, t.shape[-1]).transpose(1, 0, 3, 2, 4)

    qc, kc, vc = to_chunks(q), to_chunks(k), to_chunks(v)
    log_g = jnp.log(1.0 - jnp.exp2(-5.0 - jnp.arange(H, dtype=jnp.float32)))
    idx = jnp.arange(CHUNK, dtype=jnp.float32)
    diff = idx[:, None] - idx[None, :]
    decay = jnp.where(diff >= 0, jnp.exp(log_g[:, None, None] * jnp.maximum(diff, 0.0)), 0.0)
    q_decay = jnp.exp(log_g[:, None] * (idx + 1.0))[None, :, :, None]
    k_decay = jnp.exp(log_g[:, None] * (CHUNK - 1.0 - idx))[None, :, :, None]
    chunk_decay = jnp.exp(log_g * CHUNK)[None, :, None, None]

    def step(state, xs):
        qi, ki, vi = xs
        scores = jnp.einsum('bhid,bhjd->bhij', qi, ki) * decay[None]
        inner = jnp.einsum('bhij,bhjv->bhiv', scores, vi)
        cross = jnp.einsum('bhid,bhdv->bhiv', qi, state) * q_decay
        new_state = state * chunk_decay + jnp.einsum('bhjd,bhjv->bhdv', ki * k_decay, vi)
        return new_state, inner + cross

    s0 = jnp.zeros((B, H, dk, dv), jnp.float32)
    _, out = lax.scan(step, s0, (qc, kc, vc))
    out = out.transpose(1, 0, 3, 2, 4).reshape(B, n * CHUNK, H, dv)
    return out[:, pad:]


def head_group_norm(y, g):
    B, L, H, dv = y.shape
    mu = jnp.mean(y, axis=-1, keepdims=True)
    yc = y - mu
    var = jnp.mean(yc * yc, axis=-1, keepdims=True)
    yn = (yc * lax.rsqrt(var + EPS)).reshape(B, L, H * dv)
    return yn * g.astype(jnp.float32)


def causal_depthwise_conv(u, w, b):
    C = u.shape[-1]
    y = lax.conv_general_dilated(
        u, w[:, None, :].astype(u.dtype), window_strides=(1,), padding=[(CONV_K - 1, 0)],
        dimension_numbers=('NWC', 'WIO', 'NWC'), feature_group_count=C)
    return y + b.astype(u.dtype)


def hybrid_layer(h, pos, norm_g, w_in, conv_w, conv_b, gn_g, w_branch, w_out):
    B, L, _ = h.shape
    u = rms_norm(h, norm_g)
    proj = jnp.einsum('bld,de->ble', u, w_in)
    q, k, v, g_ret, c_x, c_pre, c_post, g_conv, m_ret, m_conv = split_columns(proj)

    q = rotary(q.reshape(B, L, RET_HEADS, RET_DK), pos) * (RET_DK ** -0.5)
    k = rotary(k.reshape(B, L, RET_HEADS, RET_DK), pos)
    v = v.reshape(B, L, RET_HEADS, RET_DV)
    y_ret = head_group_norm(retention_chunkwise(q, k, v), gn_g)
    y_ret = (y_ret * jax.nn.silu(g_ret.astype(jnp.float32))).astype(h.dtype)
    y_ret = jnp.einsum('blc,cd->bld', y_ret, w_branch[0])

    z = causal_depthwise_conv(c_pre * c_x, conv_w, conv_b)
    y_conv = (c_post * z) * jax.nn.silu(g_conv)
    y_conv = jnp.einsum('blc,cd->bld', y_conv, w_branch[1])

    merged = jax.nn.sigmoid(m_ret) * y_ret + jax.nn.sigmoid(m_conv) * y_conv
    return h + jnp.einsum('bld,de->ble', merged, w_out)


def setup_inputs(seed: int = 0) -> dict:
    key = jax.random.key(seed)
    ks = jax.random.split(key, 10)
    f32 = jnp.float32
    x = jax.random.normal(ks[0], (BATCH, SEQ, D_MODEL), f32)
    meta_tokens = jax.random.normal(ks[1], (N_META, D_MODEL), f32)
    norm_g = 1.0 + 0.02 * jax.random.normal(ks[2], (DEPTH, D_MODEL), f32)
    w_in = jax.random.normal(ks[3], (DEPTH, D_MODEL, IN_COLS), f32) * (D_MODEL ** -0.5)
    conv_w = jax.random.normal(ks[4], (DEPTH, CONV_K, CONV_W), f32) * (CONV_K ** -0.5)
    conv_b = 0.02 * jax.random.normal(ks[5], (DEPTH, CONV_W), f32)
    gn_g = 1.0 + 0.02 * jax.random.normal(ks[6], (DEPTH, RET_V_W), f32)
    w_branch = jax.random.normal(ks[7], (DEPTH, N_BRANCH, RET_V_W, D_MODEL), f32) * (RET_V_W ** -0.5)
    w_out = jax.random.normal(ks[8], (DEPTH, D_MODEL, D_MODEL), f32) * (D_MODEL ** -0.5)
    final_norm_g = 1.0 + 0.02 * jax.random.normal(ks[9], (D_MODEL,), f32)
    return {"x": x, "meta_tokens": meta_tokens, "norm_g": norm_g, "w_in": w_in,
            "conv_w": conv_w, "conv_b": conv_b, "gn_g": gn_g, "w_branch": w_branch,
            "w_out": w_out, "final_norm_g": final_norm_g}


def reference(x, meta_tokens, norm_g, w_in, conv_w, conv_b, gn_g, w_branch, w_out, final_norm_g):
    B = x.shape[0]
    meta = jnp.broadcast_to(meta_tokens.astype(x.dtype)[None], (B, N_META, x.shape[-1]))
    h = jnp.concatenate([meta, x], axis=1)
    pos = jnp.arange(h.shape[1], dtype=jnp.int32)
    for layer in range(DEPTH):
        h = hybrid_layer(h, pos, norm_g[layer], w_in[layer], conv_w[layer], conv_b[layer],
                         gn_g[layer], w_branch[layer], w_out[layer])
    return rms_norm(h, final_norm_g)[:, N_META:]
```

```python
import numpy as np
import ml_dtypes
from contextlib import ExitStack
import concourse.bass as bass
import concourse.mybir as mybir
from concourse.bass_utils import run_bass_kernel_spmd

F32 = mybir.dt.float32
BF16 = mybir.dt.bfloat16
AF = mybir.ActivationFunctionType
ALU = mybir.AluOpType

D = 2048
KC = 16
T = 384
NCH = 3
SOLO = True
NT = 43 if SOLO else 11
NCORE_USED = 2 if SOLO else 8
H = 8
DEPTH = 4
NCORE = 8
EPS = 1e-6
NPIECE = 192
EW = 2048 + 32
NSLOT = 6
SAME_ENG_SYNC = True

C_MASK = 0
C_QDEC = C_MASK + 1024
C_KDEC = C_QDEC + 1024
C_IDENT = C_KDEC + 8
C_PERM = C_IDENT + 128
C_ONES = C_PERM + 128
C_VECS = C_ONES + 128
C_COEF = C_VECS + DEPTH * 96 + 16
C_HCOEF = C_COEF + 64
C_EPS = C_HCOEF + 8
C_SELF = C_EPS + 1
NCONST = C_SELF + 8

GAMMA = [1.0 - 2.0 ** (-5.0 - h) for h in range(H)]


class Buf:
    __slots__ = ("name", "w", "r")

    def __init__(self, name):
        self.name = name
        self.w = None
        self.r = []


class Op:
    __slots__ = ("eng", "fn", "deps", "token", "signal", "seq", "is_dma", "chan", "waits", "sigval", "inc")


class Prog:
    ENGS = ("pe", "act", "dve", "sp", "gq")

    def __init__(self):
        self.ops = {e: [] for e in self.ENGS}
        self.chan_count = {}
        self.chan_last = {}

    def op(self, eng, fn, reads=(), writes=(), chan=None, inc=16):
        o = Op()
        o.inc = inc
        o.eng = eng
        o.fn = fn
        o.is_dma = chan is not None
        o.chan = chan
        o.seq = len(self.ops[eng])
        o.signal = False
        o.token = None
        o.sigval = None
        deps = []
        for b in reads:
            if b.w is not None:
                deps.append((b.w, 0))
        for b in writes:
            if b.w is not None:
                deps.append((b.w, 1))
            for r in b.r:
                deps.append((r, 2))
        if o.is_dma:
            k = self.chan_count.get(chan, 0) + 1
            self.chan_count[chan] = k
            o.token = (chan, k)
            prev = self.chan_last.get(chan)
            if prev is not None:
                deps.append((prev, 0))
            self.chan_last[chan] = o
        o.deps = deps
        for b in reads:
            b.r.append(o)
        for b in writes:
            b.w = o
            b.r = []
        self.ops[eng].append(o)
        return o

    def finalize(self):
        for eng in self.ENGS:
            waited = {}
            for o in self.ops[eng]:
                need = {}
                for (p, kind) in o.deps:
                    if p is o:
                        continue
                    if p.is_dma:
                        key = ("c", p.chan)
                        val = p.token[1]
                    else:
                        if p.eng == eng:
                            if eng == "pe" or not SAME_ENG_SYNC or kind == 2:
                                continue
                        key = ("e", p.eng)
                        val = p.seq
                    if waited.get(key, -1) >= val:
                        continue
                    cur = need.get(key)
                    if cur is None or cur[0] < val:
                        need[key] = (val, p)
                o.waits = []
                for key, (val, p) in need.items():
                    waited[key] = val
                    if not p.is_dma:
                        p.signal = True
                    o.waits.append(p)
        for eng in ("pe", "act", "dve"):
            cnt = 0
            for o in self.ops[eng]:
                if o.signal:
                    cnt += 1
                    o.sigval = cnt

    LIM = 16384

    def sem_plan(self):
        plan = {}
        for eng in ("pe", "act", "dve"):
            n = max([o.sigval or 0 for o in self.ops[eng]] + [0])
            plan[("e", eng)] = max(1, -(-n // self.LIM))
        for c, k in self.chan_count.items():
            plan[("c", c)] = max(1, -(-k // (self.LIM // 16)))
        return plan

    def _loc(self, ordinal, per, inc):
        return (ordinal - 1) // per, ((ordinal - 1) % per + 1) * inc

    def emit(self, eng, e, sems):
        per_d = self.LIM // 16
        for o in self.ops[eng]:
            for p in o.waits:
                if p.is_dma:
                    b, v = self._loc(p.token[1], per_d, p.inc)
                    e.wait_ge(sems[("c", p.chan)][b], v)
                else:
                    b, v = self._loc(p.sigval, self.LIM, 1)
                    e.wait_ge(sems[("e", p.eng)][b], v)
            ins = o.fn(e)
            if o.is_dma:
                b, v = self._loc(o.token[1], per_d, o.inc)
                ins.then_inc(sems[("c", o.chan)][b], o.inc)
            elif o.signal:
                b, v = self._loc(o.sigval, self.LIM, 1)
                ins.then_inc(sems[("e", o.eng)][b], 1)

    def final_waits(self, e, sems):
        per_d = self.LIM // 16
        for c, o in self.chan_last.items():
            b, v = self._loc(o.token[1], per_d, o.inc)
            e.wait_ge(sems[("c", c)][b], v)


def piece_ids_B():
    return list(range(NPIECE))


def piece_ids_A(last_tile):
    ids = []
    for h in range(H):
        ids += [4 * h + 1, 4 * h + 2, 4 * h + 3]
    if last_tile:
        for cc in range(KC):
            ids += [48 + 4 * cc, 48 + 4 * cc + 1]
    return ids


A_LOCAL = {}
_i = 0
for _h in range(H):
    for _k in (1, 2, 3):
        A_LOCAL[4 * _h + _k] = _i
        _i += 1
for _cc in range(KC):
    for _k in (0, 1):
        A_LOCAL[48 + 4 * _cc + _k] = _i
        _i += 1
NPIECE_A = _i


def build_program(nlayers=DEPTH):
    nc = bass.Bass("TRN2", target_bir_lowering=False)
    P = Prog()
    hin = nc.dram_tensor("hin", [NT * 128, KC * T], F32, kind="ExternalInput").ap()
    rot = nc.dram_tensor("rot", [NT * 128, 2 * T], F32, kind="ExternalInput").ap()
    cst = nc.dram_tensor("cst", [128, NCONST], F32, kind="ExternalInput").ap()
    wall = nc.dram_tensor("wall", [DEPTH * NPIECE * 128, 2048], F32, kind="ExternalInput").ap()
    hout = nc.dram_tensor("hout", [NT * 128, KC * T], F32, kind="ExternalOutput").ap()
    hbuf = nc.dram_tensor("hbuf", [NT * 128, KC * T], F32).ap()
    wbfs = [nc.dram_tensor(f"wbf{l}", [NPIECE * 128, 2048], BF16).ap() for l in range(DEPTH)]
    ecc_in = nc.dram_tensor("ecc_in", [128, NCORE * EW], F32).ap()
    ecc_out = nc.dram_tensor("ecc_out", [128, NCORE * EW], F32).ap()

    es = ExitStack()
    with es:
        def sb(name, shape, dt):
            return es.enter_context(nc.sbuf_tensor(name, shape, dt))

        hT = sb("hT", [128, KC, T], F32)
        uT = sb("uT", [128, KC, T], BF16)
        sq = [sb(f"sq{i}", [128, T], BF16) for i in range(2)]
        rstd = sb("rstd", [128, T], F32)
        rott = sb("rott", [128, 2, T], F32)
        xraw = [sb(f"xraw{i}", [128, T], BF16) for i in range(2)]
        t1 = sb("t1", [128, T], F32)
        t2 = sb("t2", [128, T], F32)
        qrot = [sb(f"qrot{i}", [128, T], BF16) for i in range(2)]
        krot = [sb(f"krot{i}", [128, T], BF16) for i in range(2)]
        ktok = [sb(f"ktok{i}", [128, NCH, 128], BF16) for i in range(2)]
        vT = [sb(f"vT{i}", [128, T], BF16) for i in range(2)]
        vtok = [sb(f"vtok{i}", [128, NCH, 256], BF16) for i in range(2)]
        Pm = [sb(f"Pm{i}", [128, 128], BF16) for i in range(2)]
        qd = [sb(f"qd{i}", [128, 128], BF16) for i in range(2)]
        state_f = sb("state_f", [128, H, 256], F32)
        state_bf = sb("state_bf", [128, H, 256], BF16)
        sin_f = sb("sin_f", [128, H, 256], F32)
        yn = sb("yn", [128, NCH, D], BF16)
        sg = [sb(f"sg{i}", [128, T], BF16) for i in range(2)]
        yretT = sb("yretT", [128, KC, T], BF16)
        yconvT = sb("yconvT", [128, KC, T], BF16)
        mergedT = sb("mergedT", [128, KC, T], BF16)
        cxs = sb("cxs", [128, T], F32)
        cxp = [sb(f"cxp{i}", [128, T + 2], F32) for i in range(2)]
        zt = sb("zt", [128, T], F32)
        sgc = sb("sgc", [128, T], F32)
        smr = sb("smr", [128, T], F32)
        smc = sb("smc", [128, T], F32)
        halo = sb("halo", [128, KC, 2], F32)
        halo_in = sb("halo_in", [128, KC, 2], F32)
        ej = sb("ej", [128, EW], F32)
        eout = sb("eout", [128, 32], F32)
        st6 = sb("st6", [128, 6], F32)
        mv = sb("mv", [128, 2], F32)
        grs = sb("grs", [128, 1], F32)
        wslot = [sb(f"wslot{i}", [128, 2048], BF16) for i in range(NSLOT)]
        cst_t = sb("cst_t", [128, NCONST], F32)
        ident = sb("ident", [128, 128], BF16)
        perm = sb("perm", [128, 128], BF16)
        ones = sb("ones", [128, 128], BF16)

        pbank = [es.enter_context(nc.psum_tensor(f"pb{i}", [128, 512], F32)) for i in range(3)]
        bankM = es.enter_context(nc.psum_tensor("bM", [128, 512], F32))
        bankS = es.enter_context(nc.psum_tensor("bS", [128, 512], F32))
        bankO = es.enter_context(nc.psum_tensor("bO", [128, 512], F32))
        bankK = es.enter_context(nc.psum_tensor("bK", [128, 512], F32))
        bankT = es.enter_context(nc.psum_tensor("bT", [128, 1024], BF16))

        B = {}

        def bf(name):
            if name not in B:
                B[name] = Buf(name)
            return B[name]

        hT_b = [bf(f"hT{k}") for k in range(KC)]
        pb_b = [bf(f"pb{i}") for i in range(3)]
        ws_b = [bf(f"ws{i}") for i in range(NSLOT)]

        P.op("sp", lambda e: e.dma_start(out=cst_t[:], in_=cst), writes=[bf("cst")], chan="cst")
        P.op("dve", lambda e: e.tensor_copy(out=ident[:], in_=cst_t[:, C_IDENT:C_IDENT + 128]), reads=[bf("cst")], writes=[bf("ident")])
        P.op("dve", lambda e: e.tensor_copy(out=perm[:], in_=cst_t[:, C_PERM:C_PERM + 128]), reads=[bf("cst")], writes=[bf("perm")])
        P.op("dve", lambda e: e.tensor_copy(out=ones[:], in_=cst_t[:, C_ONES:C_ONES + 128]), reads=[bf("cst")], writes=[bf("ones")])
        cb = bf("cst")

        def vec(l, which, kc):
            c = C_VECS + l * 96 + which * 16 + kc
            return cst_t[:, c:c + 1]

        def fvec(kc):
            c = C_VECS + DEPTH * 96 + kc
            return cst_t[:, c:c + 1]

        epsc = cst_t[:, C_EPS:C_EPS + 1]

        wstate = {"seq": [], "issued": 0, "used": 0}

        def w_issue_upto(n):
            while wstate["issued"] < min(n, len(wstate["seq"])):
                i = wstate["issued"]
                src, srcb = wstate["seq"][i]
                s = i % NSLOT
                P.op("gq", lambda e, s=s, src=src: e.dma_start(out=wslot[s][:], in_=src),
                     reads=[srcb], writes=[ws_b[s]], chan=f"ws{s}")
                wstate["issued"] += 1

        def w_next():
            i = wstate["used"]
            w_issue_upto(i + NSLOT)
            wstate["used"] += 1
            s = i % NSLOT
            return wslot[s], ws_b[s]

        def wpiece(l, i):
            return (wbfs[l][i * 128:(i + 1) * 128, :], bf(f"wbf{l}_{i}"))

        for l_ in range(nlayers):
            for i_ in range(NPIECE):
                r0 = (l_ * NPIECE + i_) * 128
                P.op("gq", lambda e, r0=r0, l_=l_, i_=i_: e.dma_start(out=wbfs[l_][i_ * 128:(i_ + 1) * 128, :], in_=wall[r0:r0 + 128, :]),
                     writes=[bf(f"wbf{l_}_{i_}")], chan="cv")

        seq = []

        def seqA(l):
            for t in range(NT):
                seq.extend(wpiece(l, i) for i in piece_ids_A(t == NT - 1))

        def seqB(l):
            for t in range(NT):
                seq.extend(wpiece(l, i) for i in piece_ids_B())
        if not SOLO:
            seqA(0)
        for l_ in range(nlayers):
            seqB(l_)
            if l_ + 1 < nlayers and not SOLO:
                seqA(l_ + 1)
        wstate["seq"] = seq

        pstate = {"i": 0}

        def proj(rhs_tile, rhs_bufs):
            wt, wb = w_next()
            i = pstate["i"] % 3
            pstate["i"] += 1
            bank = pbank[i]

            def fn(e, wt=wt, bank=bank):
                for kc in range(KC):
                    ins = e.matmul(bank[:, 0:T], wt[:, kc * 128:(kc + 1) * 128], rhs_tile[:, kc, :],
                                   start=(kc == 0), stop=(kc == KC - 1))
                return ins
            P.op("pe", fn, reads=[wb] + rhs_bufs, writes=[pb_b[i]])
            return bank, pb_b[i]

        def load_tile(src, srcname, t):
            P.op("sp", lambda e: e.dma_start(out=hT[:].rearrange("p k j -> p (k j)"), in_=src[t * 128:(t + 1) * 128, :]),
                 reads=[bf(f"{srcname}{t}")], writes=hT_b, chan="hT")
            P.op("sp", lambda e: e.dma_start(out=rott[:].rearrange("p k j -> p (k j)"), in_=rot[t * 128:(t + 1) * 128, :]),
                 writes=[bf("rott")], chan="rott")

        def rms_stats(out_rstd_buf):
            for kc in range(KC):
                s = kc % 2
                P.op("act", lambda e, kc=kc, s=s: e.activation(out=sq[s][:], in_=hT[:, kc, :], func=AF.Square),
                     reads=[hT_b[kc]], writes=[bf(f"sq{s}")])
                P.op("pe", lambda e, kc=kc, s=s: e.matmul(bankM[:, 0:T], ones[:], sq[s][:], start=(kc == 0), stop=(kc == KC - 1)),
                     reads=[bf(f"sq{s}"), bf("ones")], writes=[bf("bM")])
            P.op("act", lambda e: e.activation(out=rstd[:], in_=bankM[:, 0:T], func=AF.Sqrt, scale=1.0 / D, bias=epsc),
                 reads=[bf("bM"), cb], writes=[out_rstd_buf])
            P.op("dve", lambda e: e.reciprocal(out=rstd[:], in_=rstd[:]), reads=[out_rstd_buf], writes=[out_rstd_buf])

        def make_u(l):
            rms_stats(bf("rstd"))
            for kc in range(KC):
                P.op("dve", lambda e, kc=kc: e.scalar_tensor_tensor(out=uT[:, kc, :], in0=hT[:, kc, :], scalar=vec(l, 0, kc),
                                                                    in1=rstd[:], op0=ALU.mult, op1=ALU.mult),
                     reads=[hT_b[kc], bf("rstd"), cb], writes=[bf("uT")])

        def rotary(bank, bankb, dst, dstb, i2):
            xr = xraw[i2]
            xb = bf(f"xraw{i2}")
            P.op("act", lambda e: e.copy(out=xr[:], in_=bank[:, 0:T]), reads=[bankb], writes=[xb])
            P.op("pe", lambda e: e.matmul(bankM[:, 0:T], perm[:], xr[:], start=True, stop=True),
                 reads=[xb, bf("perm")], writes=[bf("bM")])
            P.op("dve", lambda e: e.tensor_tensor(out=t1[:], in0=xr[:], in1=rott[:, 0, :], op=ALU.mult),
                 reads=[xb, bf("rott")], writes=[bf("t1")])
            P.op("dve", lambda e: e.tensor_tensor(out=t2[:], in0=bankM[:, 0:T], in1=rott[:, 1, :], op=ALU.mult),
                 reads=[bf("bM"), bf("rott")], writes=[bf("t2")])
            P.op("dve", lambda e: e.tensor_tensor(out=dst[:], in0=t1[:], in1=t2[:], op=ALU.add),
                 reads=[bf("t1"), bf("t2")], writes=[dstb])

        def k_and_v(h):
            hb = h % 2
            bank, bb = proj(uT, [bf("uT")])
            rotary(bank, bb, krot[hb], bf(f"krot{hb}"), 1)

            def trk(e):
                for c in range(NCH):
                    ins = e.transpose(bankT[:, c * 128:(c + 1) * 128], krot[hb][:, c * 128:(c + 1) * 128], ident[:])
                return ins
            P.op("pe", trk, reads=[bf(f"krot{hb}"), bf("ident")], writes=[bf("bT")])
            P.op("dve", lambda e: e.tensor_scalar_mul(out=ktok[hb][:].rearrange("p c d -> p (c d)"), in0=bankT[:, 0:NCH * 128],
                                                      scalar1=cst_t[:, C_KDEC + h:C_KDEC + h + 1]),
                 reads=[bf("bT"), cb], writes=[bf(f"ktok{hb}")])
            for i in range(2):
                bank, bb = proj(uT, [bf("uT")])
                P.op("act", lambda e, bank=bank, i=i: e.copy(out=vT[i][:], in_=bank[:, 0:T]), reads=[bb], writes=[bf(f"vT{i}")])

            def trv(e):
                for c in range(NCH):
                    for i in range(2):
                        o0 = (c * 2 + i) * 128
                        ins = e.transpose(bankT[:, o0:o0 + 128], vT[i][:, c * 128:(c + 1) * 128], ident[:])
                return ins
            P.op("pe", trv, reads=[bf("vT0"), bf("vT1"), bf("ident")], writes=[bf("bT")])
            P.op("act", lambda e: e.copy(out=vtok[hb][:].rearrange("p c d -> p (c d)"), in_=bankT[:, 0:NCH * 256]),
                 reads=[bf("bT")], writes=[bf(f"vtok{hb}")])

        def kv_update(h, c, inject=False):
            hb = h % 2
            P.op("pe", lambda e: e.matmul(bankK[:, 0:256], ktok[hb][:, c, :], vtok[hb][:, c, :], start=True, stop=True),
                 reads=[bf(f"ktok{hb}"), bf(f"vtok{hb}")], writes=[bf("bK")])
            P.op("dve", lambda e: e.scalar_tensor_tensor(out=state_f[:, h, :], in0=state_f[:, h, :], scalar=float(GAMMA[h] ** 128),
                                                         in1=bankK[:, 0:256], op0=ALU.mult, op1=ALU.add),
                 reads=[bf("bK"), bf(f"stf{h}")], writes=[bf(f"stf{h}")])
            if inject:
                P.op("dve", lambda e: e.tensor_tensor(out=state_f[:, h, :], in0=state_f[:, h, :], in1=sin_f[:, h, :], op=ALU.add),
                     reads=[bf(f"stf{h}"), bf("sin")], writes=[bf(f"stf{h}")])

        def phaseB(lb, src, srcname, dst, dstname, final):
            l = lb
            P.op("dve", lambda e: e.memset(sin_f[:].rearrange("p h v -> p (h v)"), 0.0), writes=[bf("sin")])
            P.op("dve", lambda e: e.memset(halo_in[:].rearrange("p k j -> p (k j)"), 0.0), writes=[bf("halo_in")])
            P.op("dve", lambda e: e.memset(halo[:].rearrange("p k j -> p (k j)"), 0.0), writes=[bf("halo")])
            P.op("dve", lambda e: e.memset(state_f[:].rearrange("p h v -> p (h v)"), 0.0), writes=[bf(f"stf{h}") for h in range(H)])
            P.op("dve", lambda e: e.memset(state_bf[:].rearrange("p h v -> p (h v)"), 0.0), writes=[bf(f"stb{h}") for h in range(H)])
            for j in range(0 if SOLO else NCORE):
                P.op("sp", lambda e, j=j: e.dma_start(out=ej[:], in_=ecc_out[:, j * EW:(j + 1) * EW]),
                     reads=[bf("ecc_out")], writes=[bf("ej")], chan="ej")
                for h in range(H):
                    cc_ = C_COEF + j * 8 + h
                    P.op("dve", lambda e, h=h, cc_=cc_: e.scalar_tensor_tensor(
                        out=sin_f[:, h, :], in0=ej[:, h * 256:(h + 1) * 256], scalar=cst_t[:, cc_:cc_ + 1],
                        in1=sin_f[:, h, :], op0=ALU.mult, op1=ALU.add), reads=[bf("ej"), bf("sin"), cb], writes=[bf("sin")])
                P.op("dve", lambda e, j=j: e.scalar_tensor_tensor(
                    out=halo_in[:].rearrange("p k j -> p (k j)"), in0=ej[:, 2048:2080], scalar=cst_t[:, C_HCOEF + j:C_HCOEF + j + 1],
                    in1=halo_in[:].rearrange("p k j -> p (k j)"), op0=ALU.mult, op1=ALU.add),
                    reads=[bf("ej"), bf("halo_in"), cb], writes=[bf("halo_in")])

            for t in range(NT):
                load_tile(src, srcname, t)
                make_u(l)

                def retention(h, t=t):
                    hb = h % 2
                    for c in range(NCH):
                        pi = c % 2
                        cs = slice(c * 128, (c + 1) * 128)
                        P.op("pe", lambda e, cs=cs: e.matmul(bankS[:, 0:128], krot[hb][:, cs], qrot[hb][:, cs], start=True, stop=True),
                             reads=[bf(f"krot{hb}"), bf(f"qrot{hb}")], writes=[bf("bS")])
                        P.op("dve", lambda e, pi=pi: e.tensor_tensor(out=Pm[pi][:], in0=bankS[:, 0:128],
                                                                     in1=cst_t[:, C_MASK + h * 128:C_MASK + (h + 1) * 128], op=ALU.mult),
                             reads=[bf("bS"), cb], writes=[bf(f"Pm{pi}")])
                        P.op("dve", lambda e, pi=pi, cs=cs: e.tensor_tensor(out=qd[pi][:], in0=qrot[hb][:, cs],
                                                                            in1=cst_t[:, C_QDEC + h * 128:C_QDEC + (h + 1) * 128], op=ALU.mult),
                             reads=[bf(f"qrot{hb}"), cb], writes=[bf(f"qd{pi}")])

                        def fo(e, pi=pi, c=c):
                            e.matmul(bankO[:, 0:256], Pm[pi][:], vtok[hb][:, c, :], start=True, stop=False)
                            return e.matmul(bankO[:, 0:256], qd[pi][:], state_bf[:, h, :], start=False, stop=True)
                        P.op("pe", fo, reads=[bf(f"Pm{pi}"), bf(f"qd{pi}"), bf(f"vtok{hb}"), bf(f"stb{h}")], writes=[bf("bO")])
                        kv_update(h, c, inject=(t == 0 and c == 0))
                        P.op("act", lambda e: e.copy(out=state_bf[:, h, :], in_=state_f[:, h, :]),
                             reads=[bf(f"stf{h}")], writes=[bf(f"stb{h}")])
                        P.op("dve", lambda e: e.bn_stats(out=st6[:], in_=bankO[:, 0:256]), reads=[bf("bO")], writes=[bf("st6")])
                        P.op("dve", lambda e: e.bn_aggr(out=mv[:], in_=st6[:]), reads=[bf("st6")], writes=[bf("mv")])
                        P.op("act", lambda e: e.activation(out=grs[:], in_=mv[:, 1:2], func=AF.Sqrt, scale=1.0, bias=epsc),
                             reads=[bf("mv"), cb], writes=[bf("grs")])
                        P.op("dve", lambda e: e.reciprocal(out=grs[:], in_=grs[:]), reads=[bf("grs")], writes=[bf("grs")])
                        P.op("dve", lambda e, c=c: e.tensor_scalar(out=yn[:, c, h * 256:(h + 1) * 256], in0=bankO[:, 0:256],
                                                                   scalar1=mv[:, 0:1], scalar2=grs[:, 0:1],
                                                                   op0=ALU.subtract, op1=ALU.mult),
                             reads=[bf("bO"), bf("mv"), bf("grs")], writes=[bf("yn")])

                for h in range(H):
                    hb = h % 2
                    bank, bb = proj(uT, [bf("uT")])
                    rotary(bank, bb, qrot[hb], bf(f"qrot{hb}"), 0)
                    k_and_v(h)
                    if h >= 1:
                        retention(h - 1)
                retention(H - 1)

                for cc in range(KC):
                    s = cc % 2
                    bank, bb = proj(uT, [bf("uT")])
                    P.op("act", lambda e, bank=bank, s=s: e.activation(out=sg[s][:], in_=bank[:, 0:T], func=AF.Silu),
                         reads=[bb], writes=[bf(f"sg{s}")])

                    def try_(e, cc=cc):
                        for c in range(NCH):
                            ins = e.transpose(bankT[:, c * 128:(c + 1) * 128], yn[:, c, cc * 128:(cc + 1) * 128], ident[:])
                        return ins
                    P.op("pe", try_, reads=[bf("yn"), bf("ident")], writes=[bf("bT")])
                    P.op("dve", lambda e, cc=cc, s=s: e.scalar_tensor_tensor(out=yretT[:, cc, :], in0=bankT[:, 0:T], scalar=vec(lb, 1, cc),
                                                                             in1=sg[s][:], op0=ALU.mult, op1=ALU.mult),
                         reads=[bf("bT"), bf(f"sg{s}"), cb], writes=[bf("yretT")])

                for cc in range(KC):
                    s = cc % 2
                    cb_ = bf(f"cxp{s}")
                    bank, bb = proj(uT, [bf("uT")])
                    P.op("act", lambda e, bank=bank: e.copy(out=cxs[:], in_=bank[:, 0:T]), reads=[bb], writes=[bf("cxs")])
                    bank, bb = proj(uT, [bf("uT")])
                    P.op("dve", lambda e, bank=bank, s=s: e.tensor_tensor(out=cxp[s][:, 2:T + 2], in0=bank[:, 0:T], in1=cxs[:], op=ALU.mult),
                         reads=[bb, bf("cxs")], writes=[cb_])
                    P.op("dve", lambda e, s=s, cc=cc: e.tensor_copy(out=cxp[s][:, 0:2], in_=halo[:, cc, :]), reads=[bf("halo")], writes=[cb_])
                    if t == 0:
                        P.op("dve", lambda e, s=s, cc=cc: e.tensor_tensor(out=cxp[s][:, 128:130], in0=cxp[s][:, 128:130],
                                                                          in1=halo_in[:, cc, :], op=ALU.add),
                             reads=[cb_, bf("halo_in")], writes=[cb_])
                    P.op("dve", lambda e, s=s, cc=cc: e.tensor_copy(out=halo[:, cc, :], in_=cxp[s][:, T:T + 2]), reads=[cb_], writes=[bf("halo")])
                    P.op("dve", lambda e, s=s, cc=cc: e.tensor_scalar(out=zt[:], in0=cxp[s][:, 2:T + 2], scalar1=vec(lb, 4, cc), scalar2=vec(lb, 5, cc),
                                                                      op0=ALU.mult, op1=ALU.add),
                         reads=[cb_, cb], writes=[bf("zt")])
                    P.op("dve", lambda e, s=s, cc=cc: e.scalar_tensor_tensor(out=zt[:], in0=cxp[s][:, 1:T + 1], scalar=vec(lb, 3, cc), in1=zt[:],
                                                                             op0=ALU.mult, op1=ALU.add),
                         reads=[cb_, cb, bf("zt")], writes=[bf("zt")])
                    P.op("dve", lambda e, s=s, cc=cc: e.scalar_tensor_tensor(out=zt[:], in0=cxp[s][:, 0:T], scalar=vec(lb, 2, cc), in1=zt[:],
                                                                             op0=ALU.mult, op1=ALU.add),
                         reads=[cb_, cb, bf("zt")], writes=[bf("zt")])
                    bank, bb = proj(uT, [bf("uT")])
                    P.op("dve", lambda e, bank=bank: e.tensor_tensor(out=zt[:], in0=bank[:, 0:T], in1=zt[:], op=ALU.mult),
                         reads=[bb, bf("zt")], writes=[bf("zt")])
                    bank, bb = proj(uT, [bf("uT")])
                    P.op("act", lambda e, bank=bank: e.activation(out=sgc[:], in_=bank[:, 0:T], func=AF.Silu), reads=[bb], writes=[bf("sgc")])
                    P.op("dve", lambda e, cc=cc: e.tensor_tensor(out=yconvT[:, cc, :], in0=zt[:], in1=sgc[:], op=ALU.mult),
                         reads=[bf("zt"), bf("sgc")], writes=[bf("yconvT")])

                for dd in range(KC):
                    bank, bb = proj(uT, [bf("uT")])
                    P.op("act", lambda e, bank=bank: e.activation(out=smr[:], in_=bank[:, 0:T], func=AF.Sigmoid), reads=[bb], writes=[bf("smr")])
                    bank, bb = proj(uT, [bf("uT")])
                    P.op("act", lambda e, bank=bank: e.activation(out=smc[:], in_=bank[:, 0:T], func=AF.Sigmoid), reads=[bb], writes=[bf("smc")])
                    bank, bb = proj(yretT, [bf("yretT")])
                    P.op("dve", lambda e, bank=bank: e.tensor_tensor(out=t1[:], in0=bank[:, 0:T], in1=smr[:], op=ALU.mult),
                         reads=[bb, bf("smr")], writes=[bf("t1")])
                    bank, bb = proj(yconvT, [bf("yconvT")])
                    P.op("dve", lambda e, bank=bank: e.tensor_tensor(out=t2[:], in0=bank[:, 0:T], in1=smc[:], op=ALU.mult),
                         reads=[bb, bf("smc")], writes=[bf("t2")])
                    P.op("dve", lambda e, dd=dd: e.tensor_tensor(out=mergedT[:, dd, :], in0=t1[:], in1=t2[:], op=ALU.add),
                         reads=[bf("t1"), bf("t2")], writes=[bf("mergedT")])

                for dd in range(KC):
                    bank, bb = proj(mergedT, [bf("mergedT")])
                    P.op("dve", lambda e, bank=bank, dd=dd: e.tensor_tensor(out=hT[:, dd, :], in0=hT[:, dd, :], in1=bank[:, 0:T], op=ALU.add),
                         reads=[bb, hT_b[dd]], writes=[hT_b[dd]])
                if final:
                    rms_stats(bf("rstd"))
                    for kc in range(KC):
                        P.op("dve", lambda e, kc=kc: e.scalar_tensor_tensor(out=hT[:, kc, :], in0=hT[:, kc, :], scalar=fvec(kc),
                                                                            in1=rstd[:], op0=ALU.mult, op1=ALU.mult),
                             reads=[hT_b[kc], bf("rstd"), cb], writes=[hT_b[kc]])
                P.op("sp", lambda e, t=t: e.dma_start(out=dst[t * 128:(t + 1) * 128, :], in_=hT[:].rearrange("p k j -> p (k j)")),
                     reads=hT_b, writes=[bf(f"{dstname}{t}")], chan="hT")

        def phaseA(la, src, srcname):
            l = la
            P.op("dve", lambda e: e.memset(state_f[:].rearrange("p h v -> p (h v)"), 0.0), writes=[bf(f"stf{h}") for h in range(H)])
            for t in range(NT):
                load_tile(src, srcname, t)
                make_u(l)
                for h in range(H):
                    k_and_v(h)
                    for c in range(NCH):
                        kv_update(h, c)
                if t == NT - 1:
                    for cc in range(KC):
                        bank, bb = proj(uT, [bf("uT")])
                        P.op("act", lambda e, bank=bank: e.copy(out=cxs[:], in_=bank[:, 0:T]), reads=[bb], writes=[bf("cxs")])
                        bank, bb = proj(uT, [bf("uT")])
                        P.op("dve", lambda e, bank=bank, cc=cc: e.tensor_tensor(out=eout[:, cc * 2:cc * 2 + 2], in0=bank[:, T - 2:T],
                                                                                in1=cxs[:, T - 2:T], op=ALU.mult),
                             reads=[bb, bf("cxs")], writes=[bf("eout")])
            for j in range(NCORE):
                sc = cst_t[:, C_SELF + j:C_SELF + j + 1]
                P.op("dve", lambda e, sc=sc: e.tensor_scalar_mul(out=ej[:, 0:2048], in0=state_f[:].rearrange("p h v -> p (h v)"), scalar1=sc),
                     reads=[bf(f"stf{h}") for h in range(H)] + [cb], writes=[bf("ej")])
                P.op("dve", lambda e, sc=sc: e.tensor_scalar_mul(out=ej[:, 2048:2080], in0=eout[:], scalar1=sc),
                     reads=[bf("eout"), cb], writes=[bf("ej")])
                P.op("sp", lambda e, j=j: e.dma_start(out=ecc_in[:, j * EW:(j + 1) * EW], in_=ej[:]),
                     reads=[bf("ej")], writes=[bf("ecc_in")], chan="ej")
            P.op("gq", lambda e: e.collective_compute("AllReduce", op=ALU.add, replica_groups=[list(range(NCORE))],
                                                      ins=[ecc_in.opt()], outs=[ecc_out.opt()]),
                 reads=[bf("ecc_in")], writes=[bf("ecc_out")], chan="cc", inc=1)

        if not SOLO:
            phaseA(0, hin, "hin")
        for l_ in range(nlayers):
            last = (l_ == nlayers - 1)
            src, srcname = (hin, "hin") if l_ == 0 else (hbuf, "hbuf")
            dst, dstname = (hout, "hout") if last else (hbuf, "hbuf")
            phaseB(l_, src, srcname, dst, dstname, last)
            if not last and not SOLO:
                phaseA(l_ + 1, hbuf, "hbuf")

        assert wstate["used"] == len(wstate["seq"]), (wstate["used"], len(wstate["seq"]))
        P.finalize()

        plan = P.sem_plan()
        sems = {key: [es.enter_context(nc.semaphore(f"s_{key[0]}_{key[1]}_{i}")) for i in range(n)] for key, n in plan.items()}
        block = es.enter_context(nc.Block())

        @block.tensor
        def _(e):
            P.emit("pe", e, sems)

        @block.scalar
        def _(e):
            P.emit("act", e, sems)

        @block.vector
        def _(e):
            P.emit("dve", e, sems)

        @block.gpsimd
        def _(e):
            P.emit("gq", e, sems)

        @block.sync
        def _(e):
            P.emit("sp", e, sems)
            P.final_waits(e, sems)
    return nc


def _piece_chunk_order():
    q0, k0, v0, gr0, cx0, cp0, co0, gc0, mr0, mc0 = [x // 128 for x in
                                                      (0, 1024, 2048, 4096, 6144, 8192, 10240, 12288, 14336, 16384)]
    wb0_0 = 18432 // 128
    wb1_0 = wb0_0 + 16
    wo_0 = wb1_0 + 16
    order = []
    for h in range(H):
        order += [q0 + h, k0 + h, v0 + 2 * h, v0 + 2 * h + 1]
    for cc in range(KC):
        order += [gr0 + cc]
    for cc in range(KC):
        order += [cx0 + cc, cp0 + cc, co0 + cc, gc0 + cc]
    for dd in range(KC):
        order += [mr0 + dd, mc0 + dd, wb0_0 + dd, wb1_0 + dd]
    for dd in range(KC):
        order += [wo_0 + dd]
    assert len(order) == NPIECE
    return order


def _layer_weights(w_in, w_branch, w_out, l):
    order = np.array(_piece_chunk_order())
    wall = np.concatenate([w_in[l], w_branch[l, 0], w_branch[l, 1], w_out[l]], axis=1)
    w4 = wall.reshape(KC, 128, NPIECE, 128)[:, :, order, :]
    wp = np.ascontiguousarray(w4.transpose(2, 1, 0, 3)).reshape(NPIECE * 128, 2048)
    return wp


def _tile_layout(hloc):
    a = hloc.reshape(NT, T, KC, 128).transpose(0, 3, 2, 1)
    return np.ascontiguousarray(a).reshape(NT * 128, KC * T)


def _untile(ht):
    a = ht.reshape(NT, 128, KC, T).transpose(0, 3, 2, 1)
    return np.ascontiguousarray(a).reshape(NT * T, D)


def _consts(core, norm_g, gn_g, conv_w, conv_b, final_norm_g):
    b, s = (core, 0) if SOLO else divmod(core, 4)
    c = np.zeros((128, NCONST), np.float32)
    idx = np.arange(128, dtype=np.float64)
    scale = 128.0 ** -0.5
    for h in range(H):
        g = GAMMA[h]
        diff = idx[None, :] - idx[:, None]
        m = np.where(diff >= 0, g ** np.maximum(diff, 0.0), 0.0) * scale
        c[:, C_MASK + h * 128:C_MASK + (h + 1) * 128] = m
        c[:, C_QDEC + h * 128:C_QDEC + (h + 1) * 128] = (g ** (idx + 1.0))[None, :] * scale
        c[:, C_KDEC + h] = g ** (127.0 - idx)
    c[:, C_IDENT:C_IDENT + 128] = np.eye(128)
    pm = np.zeros((128, 128))
    for m_ in range(128):
        pm[(m_ + 64) % 128, m_] = 1.0
    c[:, C_PERM:C_PERM + 128] = pm
    c[:, C_ONES:C_ONES + 128] = 1.0
    for l in range(DEPTH):
        base = C_VECS + l * 96
        c[:, base + 0:base + 16] = norm_g[l].reshape(KC, 128).T
        c[:, base + 16:base + 32] = gn_g[l].reshape(KC, 128).T
        for k in range(3):
            c[:, base + 32 + 16 * k:base + 48 + 16 * k] = conv_w[l, k].reshape(KC, 128).T
        c[:, base + 80:base + 96] = conv_b[l].reshape(KC, 128).T
    c[:, C_VECS + DEPTH * 96:C_VECS + DEPTH * 96 + 16] = final_norm_g.reshape(KC, 128).T
    for j in range(0 if SOLO else NCORE):
        bj, sj = divmod(j, 4)
        if bj == b and sj < s:
            for h in range(H):
                c[:, C_COEF + j * 8 + h] = GAMMA[h] ** (4096.0 * (s - 1 - sj))
        if bj == b and sj == s - 1:
            c[:, C_HCOEF + j] = 1.0
    c[:, C_EPS] = EPS
    c[:, C_SELF + core] = 1.0
    return c


def _rot_tables(core):
    b, s = (core, 0) if SOLO else divmod(core, 4)
    tau = np.arange(NT * T)
    pos = np.where(tau >= 128, 16 + s * 4096 + (tau - 128), np.maximum(tau - 112, 0)).astype(np.float32)
    inv_freq = (np.float32(10000.0) ** (-(np.arange(64, dtype=np.float32) / np.float32(64)))).astype(np.float32)
    ang = (pos[:, None] * inv_freq[None, :]).astype(np.float32)
    cos = np.cos(ang).astype(np.float32)
    sin = np.sin(ang).astype(np.float32)
    cosT = np.concatenate([cos, cos], axis=1).T
    sinT = np.concatenate([-sin, sin], axis=1).T
    r = np.stack([cosT.reshape(128, NT, T), sinT.reshape(128, NT, T)], axis=2)
    return np.ascontiguousarray(r.transpose(1, 0, 2, 3)).reshape(NT * 128, 2 * T)


_PROG_CACHE = {}


def _prog():
    if "p" not in _PROG_CACHE:
        _PROG_CACHE["p"] = build_program()
    return _PROG_CACHE["p"]


def kernel(x, meta_tokens, norm_g, w_in, conv_w, conv_b, gn_g, w_branch, w_out, final_norm_g):
    x = np.asarray(x, np.float32)
    meta_tokens = np.asarray(meta_tokens, np.float32)
    norm_g = np.asarray(norm_g, np.float32)
    w_in = np.asarray(w_in, np.float32)
    conv_w = np.asarray(conv_w, np.float32)
    conv_b = np.asarray(conv_b, np.float32)
    gn_g = np.asarray(gn_g, np.float32)
    w_branch = np.asarray(w_branch, np.float32)
    w_out = np.asarray(w_out, np.float32)
    final_norm_g = np.asarray(final_norm_g, np.float32)

    cores = list(range(NCORE_USED))
    nreal = (NT * T - 128)
    hs = []
    for core in cores:
        b, s = (core, 0) if SOLO else divmod(core, 4)
        hloc = np.zeros((NT * T, D), np.float32)
        if s == 0:
            hloc[112:128] = meta_tokens
        hloc[128:] = x[b, s * nreal:(s + 1) * nreal]
        hs.append(_tile_layout(hloc))
    rots = [_rot_tables(c) for c in cores]
    csts = [_consts(c, norm_g, gn_g, conv_w, conv_b, final_norm_g) for c in cores]
    wall = np.empty((DEPTH * NPIECE * 128, 2048), np.float32)
    for l in range(DEPTH):
        wall[l * NPIECE * 128:(l + 1) * NPIECE * 128] = _layer_weights(w_in, w_branch, w_out, l)

    nc = _prog()
    res = run_bass_kernel_spmd(nc, [{"hin": hs[c], "rot": rots[c], "cst": csts[c], "wall": wall} for c in cores], core_ids=cores)
    out = np.zeros((2, 16384, D), np.float32)
    for core in cores:
        b, s = (core, 0) if SOLO else divmod(core, 4)
        out[b, s * nreal:(s + 1) * nreal] = _untile(res.results[core]["hout"])[128:]
    return out
```

```python
import numpy as np
import ml_dtypes
from contextlib import ExitStack
import concourse.bass as bass
import concourse.mybir as mybir
from concourse.bass_utils import run_bass_kernel_spmd

F32 = mybir.dt.float32
BF16 = mybir.dt.bfloat16
AF = mybir.ActivationFunctionType
ALU = mybir.AluOpType

D = 2048
KC = 16
T = 384
NCH = 3
SOLO = True
NT = 43 if SOLO else 11
NCORE_USED = 2 if SOLO else 8
H = 8
DEPTH = 4
NCORE = 8
EPS = 1e-6
NPIECE = 192
EW = 2048 + 32
NSLOT = 8
SAME_ENG_SYNC = True

C_MASK = 0
C_QDEC = C_MASK + 1024
C_KDEC = C_QDEC + 1024
C_IDENT = C_KDEC + 8
C_PERM = C_IDENT + 128
C_ONES = C_PERM + 128
C_VECS = C_ONES + 128
C_COEF = C_VECS + DEPTH * 96 + 16
C_HCOEF = C_COEF + 64
C_EPS = C_HCOEF + 8
C_SELF = C_EPS + 1
NCONST = C_SELF + 8

GAMMA = [1.0 - 2.0 ** (-5.0 - h) for h in range(H)]


class Buf:
    __slots__ = ("name", "w", "r")

    def __init__(self, name):
        self.name = name
        self.w = None
        self.r = []


class Op:
    __slots__ = ("eng", "fn", "deps", "token", "signal", "seq", "is_dma", "chan", "waits", "sigval", "inc")


class Prog:
    ENGS = ("pe", "act", "dve", "sp", "gq")

    def __init__(self):
        self.ops = {e: [] for e in self.ENGS}
        self.chan_count = {}
        self.chan_last = {}

    def op(self, eng, fn, reads=(), writes=(), chan=None, inc=16):
        o = Op()
        o.inc = inc
        o.eng = eng
        o.fn = fn
        o.is_dma = chan is not None
        o.chan = chan
        o.seq = len(self.ops[eng])
        o.signal = False
        o.token = None
        o.sigval = None
        deps = []
        for b in reads:
            if b.w is not None:
                deps.append((b.w, 0))
        for b in writes:
            if b.w is not None:
                deps.append((b.w, 1))
            for r in b.r:
                deps.append((r, 2))
        if o.is_dma:
            k = self.chan_count.get(chan, 0) + 1
            self.chan_count[chan] = k
            o.token = (chan, k)
            prev = self.chan_last.get(chan)
            if prev is not None:
                deps.append((prev, 0))
            self.chan_last[chan] = o
        o.deps = deps
        for b in reads:
            b.r.append(o)
        for b in writes:
            b.w = o
            b.r = []
        self.ops[eng].append(o)
        return o

    def finalize(self):
        for eng in self.ENGS:
            waited = {}
            for o in self.ops[eng]:
                need = {}
                for (p, kind) in o.deps:
                    if p is o:
                        continue
                    if p.is_dma:
                        key = ("c", p.chan)
                        val = p.token[1]
                    else:
                        if p.eng == eng:
                            if eng == "pe" or not SAME_ENG_SYNC or kind == 2:
                                continue
                        key = ("e", p.eng)
                        val = p.seq
                    if waited.get(key, -1) >= val:
                        continue
                    cur = need.get(key)
                    if cur is None or cur[0] < val:
                        need[key] = (val, p)
                o.waits = []
                for key, (val, p) in need.items():
                    waited[key] = val
                    if not p.is_dma:
                        p.signal = True
                    o.waits.append(p)
        for eng in ("pe", "act", "dve"):
            cnt = 0
            for o in self.ops[eng]:
                if o.signal:
                    cnt += 1
                    o.sigval = cnt

    LIM = 16384

    def sem_plan(self):
        plan = {}
        for eng in ("pe", "act", "dve"):
            n = max([o.sigval or 0 for o in self.ops[eng]] + [0])
            plan[("e", eng)] = max(1, -(-n // self.LIM))
        for c, k in self.chan_count.items():
            plan[("c", c)] = max(1, -(-k // (self.LIM // 16)))
        return plan

    def _loc(self, ordinal, per, inc):
        return (ordinal - 1) // per, ((ordinal - 1) % per + 1) * inc

    def emit(self, eng, e, sems):
        per_d = self.LIM // 16
        for o in self.ops[eng]:
            for p in o.waits:
                if p.is_dma:
                    b, v = self._loc(p.token[1], per_d, p.inc)
                    e.wait_ge(sems[("c", p.chan)][b], v)
                else:
                    b, v = self._loc(p.sigval, self.LIM, 1)
                    e.wait_ge(sems[("e", p.eng)][b], v)
            ins = o.fn(e)
            if o.is_dma:
                b, v = self._loc(o.token[1], per_d, o.inc)
                ins.then_inc(sems[("c", o.chan)][b], o.inc)
            elif o.signal:
                b, v = self._loc(o.sigval, self.LIM, 1)
                ins.then_inc(sems[("e", o.eng)][b], 1)

    def final_waits(self, e, sems):
        per_d = self.LIM // 16
        for c, o in self.chan_last.items():
            b, v = self._loc(o.token[1], per_d, o.inc)
            e.wait_ge(sems[("c", c)][b], v)


def build_program(nlayers=DEPTH):
    nc = bass.Bass("TRN2", target_bir_lowering=False)
    P = Prog()
    hin = nc.dram_tensor("hin", [NT * 128, KC * T], F32, kind="ExternalInput").ap()
    rot = nc.dram_tensor("rot", [NT * 128, 2 * T], F32, kind="ExternalInput").ap()
    cst = nc.dram_tensor("cst", [128, NCONST], F32, kind="ExternalInput").ap()
    wall = nc.dram_tensor("wall", [DEPTH * NPIECE * 128, 2048], F32, kind="ExternalInput").ap()
    hout = nc.dram_tensor("hout", [NT * 128, KC * T], F32, kind="ExternalOutput").ap()
    hbuf = nc.dram_tensor("hbuf", [NT * 128, KC * T], F32).ap()
    wbfs = [nc.dram_tensor(f"wbf{l}", [NPIECE * 128, 2048], BF16).ap() for l in range(DEPTH)]

    es = ExitStack()
    with es:
        def sb(name, shape, dt):
            return es.enter_context(nc.sbuf_tensor(name, shape, dt))

        hT = [sb(f"hT{i}", [128, KC, T], F32) for i in range(2)]
        uT = sb("uT", [128, KC, T], BF16)
        sq = [sb(f"sq{i}", [128, T], BF16) for i in range(2)]
        rstd = sb("rstd", [128, T], F32)
        rott = sb("rott", [128, 2, T], F32)
        xraw = [sb(f"xraw{i}", [128, T], BF16) for i in range(2)]
        t1 = sb("t1", [128, T], F32)
        t2 = sb("t2", [128, T], F32)
        qrot = [sb(f"qrot{i}", [128, T], BF16) for i in range(2)]
        krot = [sb(f"krot{i}", [128, T], BF16) for i in range(2)]
        ktok = [sb(f"ktok{i}", [128, NCH, 128], BF16) for i in range(2)]
        vT = [sb(f"vT{i}", [128, T], BF16) for i in range(2)]
        vtok = [sb(f"vtok{i}", [128, NCH, 256], BF16) for i in range(2)]
        Pm = [sb(f"Pm{i}", [128, 128], BF16) for i in range(NCH)]
        qd = [sb(f"qd{i}", [128, 128], BF16) for i in range(2)]
        state_f = sb("state_f", [128, H, 256], F32)
        state_bf = sb("state_bf", [128, H, 256], BF16)
        yn = sb("yn", [128, NCH, D], BF16)
        sg = [sb(f"sg{i}", [128, T], BF16) for i in range(2)]
        yretT = sb("yretT", [128, KC, T], BF16)
        yconvT = sb("yconvT", [128, KC, T], BF16)
        mergedT = sb("mergedT", [128, KC, T], BF16)
        cxs = sb("cxs", [128, T], F32)
        cxp = [sb(f"cxp{i}", [128, T + 2], F32) for i in range(2)]
        zt = sb("zt", [128, T], F32)
        sgc = sb("sgc", [128, T], F32)
        smr = sb("smr", [128, T], F32)
        smc = sb("smc", [128, T], F32)
        halo = sb("halo", [128, KC, 2], F32)
        st6 = sb("st6", [128, 6], F32)
        mv = sb("mv", [128, 2], F32)
        grs = sb("grs", [128, 1], F32)
        wslot = [sb(f"wslot{i}", [128, 2048], BF16) for i in range(NSLOT)]
        cst_t = sb("cst_t", [128, NCONST], F32)
        ident = sb("ident", [128, 128], BF16)
        perm = sb("perm", [128, 128], BF16)
        ones = sb("ones", [128, 128], BF16)

        pbank = [es.enter_context(nc.psum_tensor(f"pb{i}", [128, 512], F32)) for i in range(3)]
        bankM = es.enter_context(nc.psum_tensor("bM", [128, 512], F32))
        bankS = es.enter_context(nc.psum_tensor("bS", [128, 512], F32))
        bankO = es.enter_context(nc.psum_tensor("bO", [128, 512], F32))
        bankK = es.enter_context(nc.psum_tensor("bK", [128, 512], F32))
        bankT = es.enter_context(nc.psum_tensor("bT", [128, 1024], BF16))

        B = {}

        def bf(name):
            if name not in B:
                B[name] = Buf(name)
            return B[name]

        hT_b = [[bf(f"hT{i}_{k}") for k in range(KC)] for i in range(2)]
        pb_b = [bf(f"pb{i}") for i in range(3)]
        ws_b = [bf(f"ws{i}") for i in range(NSLOT)]

        P.op("sp", lambda e: e.dma_start(out=cst_t[:], in_=cst), writes=[bf("cst")], chan="cst")
        P.op("dve", lambda e: e.tensor_copy(out=ident[:], in_=cst_t[:, C_IDENT:C_IDENT + 128]), reads=[bf("cst")], writes=[bf("ident")])
        P.op("dve", lambda e: e.tensor_copy(out=perm[:], in_=cst_t[:, C_PERM:C_PERM + 128]), reads=[bf("cst")], writes=[bf("perm")])
        P.op("dve", lambda e: e.tensor_copy(out=ones[:], in_=cst_t[:, C_ONES:C_ONES + 128]), reads=[bf("cst")], writes=[bf("ones")])
        cb = bf("cst")

        def vec(l, which, kc):
            c = C_VECS + l * 96 + which * 16 + kc
            return cst_t[:, c:c + 1]

        def fvec(kc):
            c = C_VECS + DEPTH * 96 + kc
            return cst_t[:, c:c + 1]

        epsc = cst_t[:, C_EPS:C_EPS + 1]

        for l_ in range(nlayers):
            for i_ in range(NPIECE):
                r0 = (l_ * NPIECE + i_) * 128
                P.op("gq", lambda e, r0=r0, l_=l_, i_=i_: e.dma_start(out=wbfs[l_][i_ * 128:(i_ + 1) * 128, :], in_=wall[r0:r0 + 128, :]),
                     writes=[bf(f"wbf{l_}_{i_}")], chan="cv")

        seq = []
        for l_ in range(nlayers):
            for t in range(NT):
                for i in range(NPIECE):
                    seq.append((wbfs[l_][i * 128:(i + 1) * 128, :], bf(f"wbf{l_}_{i}")))
        wstate = {"seq": seq, "issued": 0, "used": 0}

        def w_issue_upto(n):
            while wstate["issued"] < min(n, len(wstate["seq"])):
                i = wstate["issued"]
                src, srcb = wstate["seq"][i]
                s = i % NSLOT
                P.op("gq", lambda e, s=s, src=src: e.dma_start(out=wslot[s][:], in_=src),
                     reads=[srcb], writes=[ws_b[s]], chan=f"ws{s}")
                wstate["issued"] += 1

        def w_next():
            i = wstate["used"]
            w_issue_upto(i + NSLOT)
            wstate["used"] += 1
            s = i % NSLOT
            return wslot[s], ws_b[s]

        pstate = {"i": 0}

        def proj(rhs_tile, rhs_bufs):
            wt, wb = w_next()
            i = pstate["i"] % 3
            pstate["i"] += 1
            bank = pbank[i]

            def fn(e, wt=wt, bank=bank):
                for kc in range(KC):
                    ins = e.matmul(bank[:, 0:T], wt[:, kc * 128:(kc + 1) * 128], rhs_tile[:, kc, :],
                                   start=(kc == 0), stop=(kc == KC - 1))
                return ins
            P.op("pe", fn, reads=[wb] + rhs_bufs, writes=[pb_b[i]])
            return bank, pb_b[i]

        tiles = [(l, t) for l in range(nlayers) for t in range(NT)]
        NTILE = len(tiles)

        def src_of(l):
            return (hin, "hin") if l == 0 else (hbuf, "hbuf")

        def dst_of(l):
            return (hout, "hout") if l == nlayers - 1 else (hbuf, "hbuf")

        def load_h(i):
            l, t = tiles[i]
            src, name = src_of(l)
            b = i % 2
            P.op("sp", lambda e: e.dma_start(out=hT[b][:].rearrange("p k j -> p (k j)"), in_=src[t * 128:(t + 1) * 128, :]),
                 reads=[bf(f"{name}{t}")], writes=hT_b[b], chan=f"hT{b}")

        def load_rot(i):
            l, t = tiles[i]
            P.op("sp", lambda e: e.dma_start(out=rott[:].rearrange("p k j -> p (k j)"), in_=rot[t * 128:(t + 1) * 128, :]),
                 writes=[bf("rott")], chan="rott")

        def rms_stats(b):
            hTi, hTb = hT[b], hT_b[b]
            for kc in range(KC):
                s = kc % 2
                P.op("act", lambda e, kc=kc, s=s: e.activation(out=sq[s][:], in_=hTi[:, kc, :], func=AF.Square),
                     reads=[hTb[kc]], writes=[bf(f"sq{s}")])
                P.op("pe", lambda e, kc=kc, s=s: e.matmul(bankM[:, 0:T], ones[:], sq[s][:], start=(kc == 0), stop=(kc == KC - 1)),
                     reads=[bf(f"sq{s}"), bf("ones")], writes=[bf("bM")])
            P.op("act", lambda e: e.activation(out=rstd[:], in_=bankM[:, 0:T], func=AF.Sqrt, scale=1.0 / D, bias=epsc),
                 reads=[bf("bM"), cb], writes=[bf("rstd")])
            P.op("dve", lambda e: e.reciprocal(out=rstd[:], in_=rstd[:]), reads=[bf("rstd")], writes=[bf("rstd")])

        def make_u(i):
            l, t = tiles[i]
            b = i % 2
            hTi, hTb = hT[b], hT_b[b]
            rms_stats(b)
            for kc in range(KC):
                P.op("dve", lambda e, kc=kc: e.scalar_tensor_tensor(out=uT[:, kc, :], in0=hTi[:, kc, :], scalar=vec(l, 0, kc),
                                                                    in1=rstd[:], op0=ALU.mult, op1=ALU.mult),
                     reads=[hTb[kc], bf("rstd"), cb], writes=[bf("uT")])

        def rotary(bank, bankb, dst, dstb, i2):
            xr = xraw[i2]
            xb = bf(f"xraw{i2}")
            P.op("act", lambda e: e.copy(out=xr[:], in_=bank[:, 0:T]), reads=[bankb], writes=[xb])
            P.op("pe", lambda e: e.matmul(bankM[:, 0:T], perm[:], xr[:], start=True, stop=True),
                 reads=[xb, bf("perm")], writes=[bf("bM")])
            P.op("dve", lambda e: e.tensor_tensor(out=t1[:], in0=xr[:], in1=rott[:, 0, :], op=ALU.mult),
                 reads=[xb, bf("rott")], writes=[bf("t1")])
            P.op("dve", lambda e: e.tensor_tensor(out=t2[:], in0=bankM[:, 0:T], in1=rott[:, 1, :], op=ALU.mult),
                 reads=[bf("bM"), bf("rott")], writes=[bf("t2")])
            P.op("dve", lambda e: e.tensor_tensor(out=dst[:], in0=t1[:], in1=t2[:], op=ALU.add),
                 reads=[bf("t1"), bf("t2")], writes=[dstb])

        def proj_q(h):
            hb = h % 2
            bank, bb = proj(uT, [bf("uT")])
            rotary(bank, bb, qrot[hb], bf(f"qrot{hb}"), 0)

        def proj_k(h):
            hb = h % 2
            bank, bb = proj(uT, [bf("uT")])
            rotary(bank, bb, krot[hb], bf(f"krot{hb}"), 1)

            def trk(e):
                for c in range(NCH):
                    ins = e.transpose(bankT[:, c * 128:(c + 1) * 128], krot[hb][:, c * 128:(c + 1) * 128], ident[:])
                return ins
            P.op("pe", trk, reads=[bf(f"krot{hb}"), bf("ident")], writes=[bf("bT")])
            P.op("dve", lambda e: e.tensor_scalar_mul(out=ktok[hb][:].rearrange("p c d -> p (c d)"), in0=bankT[:, 0:NCH * 128],
                                                      scalar1=cst_t[:, C_KDEC + h:C_KDEC + h + 1]),
                 reads=[bf("bT"), cb], writes=[bf(f"ktok{hb}")])

        def proj_v(h, i):
            bank, bb = proj(uT, [bf("uT")])
            P.op("act", lambda e: e.copy(out=vT[i][:], in_=bank[:, 0:T]), reads=[bb], writes=[bf(f"vT{i}")])

        def v_trans(h):
            hb = h % 2

            def trv(e):
                for c in range(NCH):
                    for i in range(2):
                        o0 = (c * 2 + i) * 128
                        ins = e.transpose(bankT[:, o0:o0 + 128], vT[i][:, c * 128:(c + 1) * 128], ident[:])
                return ins
            P.op("pe", trv, reads=[bf("vT0"), bf("vT1"), bf("ident")], writes=[bf("bT")])
            P.op("act", lambda e: e.copy(out=vtok[hb][:].rearrange("p c d -> p (c d)"), in_=bankT[:, 0:NCH * 256]),
                 reads=[bf("bT")], writes=[bf(f"vtok{hb}")])

        def ret_S(h):
            hb = h % 2

            def fs(e):
                for c in range(NCH):
                    cs = slice(c * 128, (c + 1) * 128)
                    ins = e.matmul(bankS[:, cs], krot[hb][:, cs], qrot[hb][:, cs], start=True, stop=True)
                return ins
            P.op("pe", fs, reads=[bf(f"krot{hb}"), bf(f"qrot{hb}")], writes=[bf("bS")])
            for c in range(NCH):
                P.op("dve", lambda e, c=c: e.tensor_tensor(out=Pm[c][:], in0=bankS[:, c * 128:(c + 1) * 128],
                                                           in1=cst_t[:, C_MASK + h * 128:C_MASK + (h + 1) * 128], op=ALU.mult),
                     reads=[bf("bS"), cb], writes=[bf(f"Pm{c}")])

        def ret_chunk(h, c):
            hb = h % 2
            pi = c % 2
            cs = slice(c * 128, (c + 1) * 128)
            P.op("dve", lambda e: e.tensor_tensor(out=qd[pi][:], in0=qrot[hb][:, cs],
                                                  in1=cst_t[:, C_QDEC + h * 128:C_QDEC + (h + 1) * 128], op=ALU.mult),
                 reads=[bf(f"qrot{hb}"), cb], writes=[bf(f"qd{pi}")])

            def fo(e):
                e.matmul(bankO[:, 0:256], Pm[c][:], vtok[hb][:, c, :], start=True, stop=False)
                return e.matmul(bankO[:, 0:256], qd[pi][:], state_bf[:, h, :], start=False, stop=True)
            P.op("pe", fo, reads=[bf(f"Pm{c}"), bf(f"qd{pi}"), bf(f"vtok{hb}"), bf(f"stb{h}")], writes=[bf("bO")])
            P.op("pe", lambda e: e.matmul(bankK[:, 0:256], ktok[hb][:, c, :], vtok[hb][:, c, :], start=True, stop=True),
                 reads=[bf(f"ktok{hb}"), bf(f"vtok{hb}")], writes=[bf("bK")])
            P.op("dve", lambda e: e.scalar_tensor_tensor(out=state_f[:, h, :], in0=state_f[:, h, :], scalar=float(GAMMA[h] ** 128),
                                                         in1=bankK[:, 0:256], op0=ALU.mult, op1=ALU.add),
                 reads=[bf("bK"), bf(f"stf{h}")], writes=[bf(f"stf{h}")])
            P.op("act", lambda e: e.copy(out=state_bf[:, h, :], in_=state_f[:, h, :]),
                 reads=[bf(f"stf{h}")], writes=[bf(f"stb{h}")])
            P.op("dve", lambda e: e.bn_stats(out=st6[:], in_=bankO[:, 0:256]), reads=[bf("bO")], writes=[bf("st6")])
            P.op("dve", lambda e: e.bn_aggr(out=mv[:], in_=st6[:]), reads=[bf("st6")], writes=[bf("mv")])
            P.op("act", lambda e: e.activation(out=grs[:], in_=mv[:, 1:2], func=AF.Sqrt, scale=1.0, bias=epsc),
                 reads=[bf("mv"), cb], writes=[bf("grs")])
            P.op("dve", lambda e: e.reciprocal(out=grs[:], in_=grs[:]), reads=[bf("grs")], writes=[bf("grs")])
            P.op("dve", lambda e: e.tensor_scalar(out=yn[:, c, h * 256:(h + 1) * 256], in0=bankO[:, 0:256],
                                                  scalar1=mv[:, 0:1], scalar2=grs[:, 0:1],
                                                  op0=ALU.subtract, op1=ALU.mult),
                 reads=[bf("bO"), bf("mv"), bf("grs")], writes=[bf("yn")])

        def yret_cc(l, cc):
            s = cc % 2
            bank, bb = proj(uT, [bf("uT")])
            P.op("act", lambda e: e.activation(out=sg[s][:], in_=bank[:, 0:T], func=AF.Silu), reads=[bb], writes=[bf(f"sg{s}")])

            def try_(e):
                for c in range(NCH):
                    ins = e.transpose(bankT[:, c * 128:(c + 1) * 128], yn[:, c, cc * 128:(cc + 1) * 128], ident[:])
                return ins
            P.op("pe", try_, reads=[bf("yn"), bf("ident")], writes=[bf("bT")])
            P.op("dve", lambda e: e.scalar_tensor_tensor(out=yretT[:, cc, :], in0=bankT[:, 0:T], scalar=vec(l, 1, cc),
                                                         in1=sg[s][:], op0=ALU.mult, op1=ALU.mult),
                 reads=[bf("bT"), bf(f"sg{s}"), cb], writes=[bf("yretT")])

        def conv_cc(l, cc):
            s = cc % 2
            cb_ = bf(f"cxp{s}")
            bank, bb = proj(uT, [bf("uT")])
            P.op("act", lambda e, bank=bank: e.copy(out=cxs[:], in_=bank[:, 0:T]), reads=[bb], writes=[bf("cxs")])
            bank, bb = proj(uT, [bf("uT")])
            P.op("dve", lambda e, bank=bank: e.tensor_tensor(out=cxp[s][:, 2:T + 2], in0=bank[:, 0:T], in1=cxs[:], op=ALU.mult),
                 reads=[bb, bf("cxs")], writes=[cb_])
            P.op("dve", lambda e: e.tensor_copy(out=cxp[s][:, 0:2], in_=halo[:, cc, :]), reads=[bf("halo")], writes=[cb_])
            P.op("dve", lambda e: e.tensor_copy(out=halo[:, cc, :], in_=cxp[s][:, T:T + 2]), reads=[cb_], writes=[bf("halo")])
            P.op("dve", lambda e: e.tensor_scalar(out=zt[:], in0=cxp[s][:, 2:T + 2], scalar1=vec(l, 4, cc), scalar2=vec(l, 5, cc),
                                                  op0=ALU.mult, op1=ALU.add),
                 reads=[cb_, cb], writes=[bf("zt")])
            P.op("dve", lambda e: e.scalar_tensor_tensor(out=zt[:], in0=cxp[s][:, 1:T + 1], scalar=vec(l, 3, cc), in1=zt[:],
                                                         op0=ALU.mult, op1=ALU.add),
                 reads=[cb_, cb, bf("zt")], writes=[bf("zt")])
            P.op("dve", lambda e: e.scalar_tensor_tensor(out=zt[:], in0=cxp[s][:, 0:T], scalar=vec(l, 2, cc), in1=zt[:],
                                                         op0=ALU.mult, op1=ALU.add),
                 reads=[cb_, cb, bf("zt")], writes=[bf("zt")])
            bank, bb = proj(uT, [bf("uT")])
            P.op("dve", lambda e, bank=bank: e.tensor_tensor(out=zt[:], in0=bank[:, 0:T], in1=zt[:], op=ALU.mult),
                 reads=[bb, bf("zt")], writes=[bf("zt")])
            bank, bb = proj(uT, [bf("uT")])
            P.op("act", lambda e, bank=bank: e.activation(out=sgc[:], in_=bank[:, 0:T], func=AF.Silu), reads=[bb], writes=[bf("sgc")])
            P.op("dve", lambda e: e.tensor_tensor(out=yconvT[:, cc, :], in0=zt[:], in1=sgc[:], op=ALU.mult),
                 reads=[bf("zt"), bf("sgc")], writes=[bf("yconvT")])

        def merge_dd(dd):
            bank, bb = proj(uT, [bf("uT")])
            P.op("act", lambda e, bank=bank: e.activation(out=smr[:], in_=bank[:, 0:T], func=AF.Sigmoid), reads=[bb], writes=[bf("smr")])
            bank, bb = proj(uT, [bf("uT")])
            P.op("act", lambda e, bank=bank: e.activation(out=smc[:], in_=bank[:, 0:T], func=AF.Sigmoid), reads=[bb], writes=[bf("smc")])
            bank, bb = proj(yretT, [bf("yretT")])
            P.op("dve", lambda e, bank=bank: e.tensor_tensor(out=t1[:], in0=bank[:, 0:T], in1=smr[:], op=ALU.mult),
                 reads=[bb, bf("smr")], writes=[bf("t1")])
            bank, bb = proj(yconvT, [bf("yconvT")])
            P.op("dve", lambda e, bank=bank: e.tensor_tensor(out=t2[:], in0=bank[:, 0:T], in1=smc[:], op=ALU.mult),
                 reads=[bb, bf("smc")], writes=[bf("t2")])
            P.op("dve", lambda e: e.tensor_tensor(out=mergedT[:, dd, :], in0=t1[:], in1=t2[:], op=ALU.add),
                 reads=[bf("t1"), bf("t2")], writes=[bf("mergedT")])

        for i in range(NTILE):
            l, t = tiles[i]
            b = i % 2
            hTi, hTb = hT[b], hT_b[b]
            if t == 0:
                P.op("dve", lambda e: e.memset(halo[:].rearrange("p k j -> p (k j)"), 0.0), writes=[bf("halo")])
                P.op("dve", lambda e: e.memset(state_f[:].rearrange("p h v -> p (h v)"), 0.0), writes=[bf(f"stf{h}") for h in range(H)])
                P.op("dve", lambda e: e.memset(state_bf[:].rearrange("p h v -> p (h v)"), 0.0), writes=[bf(f"stb{h}") for h in range(H)])
            if i == 0:
                load_h(0)
                load_rot(0)
                make_u(0)
            if i + 1 < NTILE:
                load_h(i + 1)

            for h in range(H):
                proj_q(h)
                if h >= 1:
                    ret_S(h - 1)
                proj_k(h)
                if h >= 1:
                    ret_chunk(h - 1, 0)
                proj_v(h, 0)
                if h >= 1:
                    ret_chunk(h - 1, 1)
                proj_v(h, 1)
                v_trans(h)
                if h >= 1:
                    ret_chunk(h - 1, 2)
            if i + 1 < NTILE:
                load_rot(i + 1)
            yret_cc(l, 0)
            ret_S(H - 1)
            yret_cc(l, 1)
            ret_chunk(H - 1, 0)
            yret_cc(l, 2)
            ret_chunk(H - 1, 1)
            yret_cc(l, 3)
            ret_chunk(H - 1, 2)
            for cc in range(4, KC):
                yret_cc(l, cc)
            for cc in range(KC):
                conv_cc(l, cc)
            for dd in range(KC):
                merge_dd(dd)
            if i + 1 < NTILE:
                make_u(i + 1)
            for dd in range(KC):
                bank, bb = proj(mergedT, [bf("mergedT")])
                P.op("dve", lambda e, bank=bank, dd=dd, hTi=hTi: e.tensor_tensor(out=hTi[:, dd, :], in0=hTi[:, dd, :], in1=bank[:, 0:T], op=ALU.add),
                     reads=[bb, hTb[dd]], writes=[hTb[dd]])
            if l == nlayers - 1:
                rms_stats(b)
                for kc in range(KC):
                    P.op("dve", lambda e, kc=kc, hTi=hTi: e.scalar_tensor_tensor(out=hTi[:, kc, :], in0=hTi[:, kc, :], scalar=fvec(kc),
                                                                               in1=rstd[:], op0=ALU.mult, op1=ALU.mult),
                         reads=[hTb[kc], bf("rstd"), cb], writes=[hTb[kc]])
            dst, dname = dst_of(l)
            P.op("sp", lambda e, t=t, dst=dst, hTi=hTi: e.dma_start(out=dst[t * 128:(t + 1) * 128, :], in_=hTi[:].rearrange("p k j -> p (k j)")),
                 reads=hTb, writes=[bf(f"{dname}{t}")], chan=f"hT{b}")

        assert wstate["used"] == len(wstate["seq"]), (wstate["used"], len(wstate["seq"]))
        P.finalize()

        plan = P.sem_plan()
        sems = {key: [es.enter_context(nc.semaphore(f"s_{key[0]}_{key[1]}_{i}")) for i in range(n)] for key, n in plan.items()}
        block = es.enter_context(nc.Block())

        @block.tensor
        def _(e):
            P.emit("pe", e, sems)

        @block.scalar
        def _(e):
            P.emit("act", e, sems)

        @block.vector
        def _(e):
            P.emit("dve", e, sems)

        @block.gpsimd
        def _(e):
            P.emit("gq", e, sems)

        @block.sync
        def _(e):
            P.emit("sp", e, sems)
            P.final_waits(e, sems)
    return nc


def _piece_chunk_order():
    q0, k0, v0, gr0, cx0, cp0, co0, gc0, mr0, mc0 = [x // 128 for x in
                                                      (0, 1024, 2048, 4096, 6144, 8192, 10240, 12288, 14336, 16384)]
    wb0_0 = 18432 // 128
    wb1_0 = wb0_0 + 16
    wo_0 = wb1_0 + 16
    order = []
    for h in range(H):
        order += [q0 + h, k0 + h, v0 + 2 * h, v0 + 2 * h + 1]
    for cc in range(KC):
        order += [gr0 + cc]
    for cc in range(KC):
        order += [cx0 + cc, cp0 + cc, co0 + cc, gc0 + cc]
    for dd in range(KC):
        order += [mr0 + dd, mc0 + dd, wb0_0 + dd, wb1_0 + dd]
    for dd in range(KC):
        order += [wo_0 + dd]
    assert len(order) == NPIECE
    return order


def _layer_weights(w_in, w_branch, w_out, l):
    order = np.array(_piece_chunk_order())
    wall = np.concatenate([w_in[l], w_branch[l, 0], w_branch[l, 1], w_out[l]], axis=1)
    w4 = wall.reshape(KC, 128, NPIECE, 128)[:, :, order, :]
    wp = np.ascontiguousarray(w4.transpose(2, 1, 0, 3)).reshape(NPIECE * 128, 2048)
    return wp


def _tile_layout(hloc):
    a = hloc.reshape(NT, T, KC, 128).transpose(0, 3, 2, 1)
    return np.ascontiguousarray(a).reshape(NT * 128, KC * T)


def _untile(ht):
    a = ht.reshape(NT, 128, KC, T).transpose(0, 3, 2, 1)
    return np.ascontiguousarray(a).reshape(NT * T, D)


def _consts(core, norm_g, gn_g, conv_w, conv_b, final_norm_g):
    b, s = (core, 0) if SOLO else divmod(core, 4)
    c = np.zeros((128, NCONST), np.float32)
    idx = np.arange(128, dtype=np.float64)
    scale = 128.0 ** -0.5
    for h in range(H):
        g = GAMMA[h]
        diff = idx[None, :] - idx[:, None]
        m = np.where(diff >= 0, g ** np.maximum(diff, 0.0), 0.0) * scale
        c[:, C_MASK + h * 128:C_MASK + (h + 1) * 128] = m
        c[:, C_QDEC + h * 128:C_QDEC + (h + 1) * 128] = (g ** (idx + 1.0))[None, :] * scale
        c[:, C_KDEC + h] = g ** (127.0 - idx)
    c[:, C_IDENT:C_IDENT + 128] = np.eye(128)
    pm = np.zeros((128, 128))
    for m_ in range(128):
        pm[(m_ + 64) % 128, m_] = 1.0
    c[:, C_PERM:C_PERM + 128] = pm
    c[:, C_ONES:C_ONES + 128] = 1.0
    for l in range(DEPTH):
        base = C_VECS + l * 96
        c[:, base + 0:base + 16] = norm_g[l].reshape(KC, 128).T
        c[:, base + 16:base + 32] = gn_g[l].reshape(KC, 128).T
        for k in range(3):
            c[:, base + 32 + 16 * k:base + 48 + 16 * k] = conv_w[l, k].reshape(KC, 128).T
        c[:, base + 80:base + 96] = conv_b[l].reshape(KC, 128).T
    c[:, C_VECS + DEPTH * 96:C_VECS + DEPTH * 96 + 16] = final_norm_g.reshape(KC, 128).T
    for j in range(0 if SOLO else NCORE):
        bj, sj = divmod(j, 4)
        if bj == b and sj < s:
            for h in range(H):
                c[:, C_COEF + j * 8 + h] = GAMMA[h] ** (4096.0 * (s - 1 - sj))
        if bj == b and sj == s - 1:
            c[:, C_HCOEF + j] = 1.0
    c[:, C_EPS] = EPS
    c[:, C_SELF + core] = 1.0
    return c


def _rot_tables(core):
    b, s = (core, 0) if SOLO else divmod(core, 4)
    tau = np.arange(NT * T)
    pos = np.where(tau >= 128, 16 + s * 4096 + (tau - 128), np.maximum(tau - 112, 0)).astype(np.float32)
    inv_freq = (np.float32(10000.0) ** (-(np.arange(64, dtype=np.float32) / np.float32(64)))).astype(np.float32)
    ang = (pos[:, None] * inv_freq[None, :]).astype(np.float32)
    cos = np.cos(ang).astype(np.float32)
    sin = np.sin(ang).astype(np.float32)
    cosT = np.concatenate([cos, cos], axis=1).T
    sinT = np.concatenate([-sin, sin], axis=1).T
    r = np.stack([cosT.reshape(128, NT, T), sinT.reshape(128, NT, T)], axis=2)
    return np.ascontiguousarray(r.transpose(1, 0, 2, 3)).reshape(NT * 128, 2 * T)


_PROG_CACHE = {}


def _prog():
    if "p" not in _PROG_CACHE:
        _PROG_CACHE["p"] = build_program()
    return _PROG_CACHE["p"]


def kernel(x, meta_tokens, norm_g, w_in, conv_w, conv_b, gn_g, w_branch, w_out, final_norm_g):
    x = np.asarray(x, np.float32)
    meta_tokens = np.asarray(meta_tokens, np.float32)
    norm_g = np.asarray(norm_g, np.float32)
    w_in = np.asarray(w_in, np.float32)
    conv_w = np.asarray(conv_w, np.float32)
    conv_b = np.asarray(conv_b, np.float32)
    gn_g = np.asarray(gn_g, np.float32)
    w_branch = np.asarray(w_branch, np.float32)
    w_out = np.asarray(w_out, np.float32)
    final_norm_g = np.asarray(final_norm_g, np.float32)

    cores = list(range(NCORE_USED))
    nreal = (NT * T - 128)
    hs = []
    for core in cores:
        b, s = (core, 0) if SOLO else divmod(core, 4)
        hloc = np.zeros((NT * T, D), np.float32)
        if s == 0:
            hloc[112:128] = meta_tokens
        hloc[128:] = x[b, s * nreal:(s + 1) * nreal]
        hs.append(_tile_layout(hloc))
    rots = [_rot_tables(c) for c in cores]
    csts = [_consts(c, norm_g, gn_g, conv_w, conv_b, final_norm_g) for c in cores]
    wall = np.empty((DEPTH * NPIECE * 128, 2048), np.float32)
    for l in range(DEPTH):
        wall[l * NPIECE * 128:(l + 1) * NPIECE * 128] = _layer_weights(w_in, w_branch, w_out, l)

    nc = _prog()
    res = run_bass_kernel_spmd(nc, [{"hin": hs[c], "rot": rots[c], "cst": csts[c], "wall": wall} for c in cores], core_ids=cores)
    out = np.zeros((2, 16384, D), np.float32)
    for core in cores:
        b, s = (core, 0) if SOLO else divmod(core, 4)
        out[b, s * nreal:(s + 1) * nreal] = _untile(res.results[core]["hout"])[128:]
    return out
```

```python
import numpy as np
import ml_dtypes
from contextlib import ExitStack
import concourse.bass as bass
import concourse.mybir as mybir
from concourse.bass_utils import run_bass_kernel_spmd

F32 = mybir.dt.float32
BF16 = mybir.dt.bfloat16
AF = mybir.ActivationFunctionType
ALU = mybir.AluOpType

D = 2048
KC = 16
T = 384
NCH = 3
SOLO = True
NT = 43 if SOLO else 11
NCORE_USED = 2 if SOLO else 8
H = 8
DEPTH = 4
NCORE = 8
EPS = 1e-6
NPIECE = 192
EW = 2048 + 32
NSLOT = 8
SAME_ENG_SYNC = True

C_MASK = 0
C_QDEC = C_MASK + 1024
C_KDEC = C_QDEC + 1024
C_IDENT = C_KDEC + 8
C_PERM = C_IDENT + 128
C_ONES = C_PERM + 128
C_VECS = C_ONES + 128
C_COEF = C_VECS + DEPTH * 96 + 16
C_HCOEF = C_COEF + 64
C_EPS = C_HCOEF + 8
C_SELF = C_EPS + 1
NCONST = C_SELF + 8

GAMMA = [1.0 - 2.0 ** (-5.0 - h) for h in range(H)]


class Buf:
    __slots__ = ("name", "w", "r")

    def __init__(self, name):
        self.name = name
        self.w = None
        self.r = []


class Op:
    __slots__ = ("eng", "fn", "deps", "token", "signal", "seq", "is_dma", "chan", "waits", "sigval", "inc")


class Prog:
    ENGS = ("pe", "act", "dve", "sp", "gq")

    def __init__(self):
        self.ops = {e: [] for e in self.ENGS}
        self.chan_count = {}
        self.chan_last = {}

    def op(self, eng, fn, reads=(), writes=(), chan=None, inc=16):
        o = Op()
        o.inc = inc
        o.eng = eng
        o.fn = fn
        o.is_dma = chan is not None
        o.chan = chan
        o.seq = len(self.ops[eng])
        o.signal = False
        o.token = None
        o.sigval = None
        deps = []
        for b in reads:
            if b.w is not None:
                deps.append((b.w, 0))
        for b in writes:
            if b.w is not None:
                deps.append((b.w, 1))
            for r in b.r:
                deps.append((r, 2))
        if o.is_dma:
            k = self.chan_count.get(chan, 0) + 1
            self.chan_count[chan] = k
            o.token = (chan, k)
            prev = self.chan_last.get(chan)
            if prev is not None:
                deps.append((prev, 0))
            self.chan_last[chan] = o
        o.deps = deps
        for b in reads:
            b.r.append(o)
        for b in writes:
            b.w = o
            b.r = []
        self.ops[eng].append(o)
        return o

    def finalize(self):
        for eng in self.ENGS:
            waited = {}
            for o in self.ops[eng]:
                need = {}
                for (p, kind) in o.deps:
                    if p is o:
                        continue
                    if p.is_dma:
                        key = ("c", p.chan)
                        val = p.token[1]
                    else:
                        if p.eng == eng:
                            if eng == "pe" or not SAME_ENG_SYNC or kind == 2:
                                continue
                        key = ("e", p.eng)
                        val = p.seq
                    if waited.get(key, -1) >= val:
                        continue
                    cur = need.get(key)
                    if cur is None or cur[0] < val:
                        need[key] = (val, p)
                o.waits = []
                for key, (val, p) in need.items():
                    waited[key] = val
                    if not p.is_dma:
                        p.signal = True
                    o.waits.append(p)
        for eng in ("pe", "act", "dve"):
            cnt = 0
            for o in self.ops[eng]:
                if o.signal:
                    cnt += 1
                    o.sigval = cnt

    LIM = 16384

    def sem_plan(self):
        plan = {}
        for eng in ("pe", "act", "dve"):
            n = max([o.sigval or 0 for o in self.ops[eng]] + [0])
            plan[("e", eng)] = max(1, -(-n // self.LIM))
        for c, k in self.chan_count.items():
            plan[("c", c)] = max(1, -(-k // (self.LIM // 16)))
        return plan

    def _loc(self, ordinal, per, inc):
        return (ordinal - 1) // per, ((ordinal - 1) % per + 1) * inc

    def emit(self, eng, e, sems):
        per_d = self.LIM // 16
        for o in self.ops[eng]:
            for p in o.waits:
                if p.is_dma:
                    b, v = self._loc(p.token[1], per_d, p.inc)
                    e.wait_ge(sems[("c", p.chan)][b], v)
                else:
                    b, v = self._loc(p.sigval, self.LIM, 1)
                    e.wait_ge(sems[("e", p.eng)][b], v)
            ins = o.fn(e)
            if o.is_dma:
                b, v = self._loc(o.token[1], per_d, o.inc)
                ins.then_inc(sems[("c", o.chan)][b], o.inc)
            elif o.signal:
                b, v = self._loc(o.sigval, self.LIM, 1)
                ins.then_inc(sems[("e", o.eng)][b], 1)

    def final_waits(self, e, sems):
        per_d = self.LIM // 16
        for c, o in self.chan_last.items():
            b, v = self._loc(o.token[1], per_d, o.inc)
            e.wait_ge(sems[("c", c)][b], v)


def build_program(nlayers=DEPTH):
    nc = bass.Bass("TRN2", target_bir_lowering=False)
    P = Prog()
    hin = nc.dram_tensor("hin", [NT * 128, KC * T], F32, kind="ExternalInput").ap()
    rot = nc.dram_tensor("rot", [NT * 128, 2 * T], F32, kind="ExternalInput").ap()
    cst = nc.dram_tensor("cst", [128, NCONST], F32, kind="ExternalInput").ap()
    wall = nc.dram_tensor("wall", [DEPTH * NPIECE * 128, 2048], F32, kind="ExternalInput").ap()
    hout = nc.dram_tensor("hout", [NT * 128, KC * T], F32, kind="ExternalOutput").ap()
    hbuf = nc.dram_tensor("hbuf", [NT * 128, KC * T], F32).ap()
    wbfs = [nc.dram_tensor(f"wbf{l}", [NPIECE * 128, 2048], BF16).ap() for l in range(DEPTH)]

    es = ExitStack()
    with es:
        def sb(name, shape, dt):
            return es.enter_context(nc.sbuf_tensor(name, shape, dt))

        hT = [sb(f"hT{i}", [128, KC, T], F32) for i in range(2)]
        uT = sb("uT", [128, KC, T], BF16)
        sq = [sb(f"sq{i}", [128, T], BF16) for i in range(2)]
        rstd = sb("rstd", [128, T], F32)
        rott = sb("rott", [128, 2, T], F32)
        xraw = [sb(f"xraw{i}", [128, T], BF16) for i in range(2)]
        t1 = sb("t1", [128, T], F32)
        t2 = sb("t2", [128, T], F32)
        qrot = [sb(f"qrot{i}", [128, T], BF16) for i in range(2)]
        krot = [sb(f"krot{i}", [128, T], BF16) for i in range(2)]
        ktok = [sb(f"ktok{i}", [128, NCH, 128], BF16) for i in range(2)]
        vT = [sb(f"vT{i}", [128, T], BF16) for i in range(2)]
        vtok = [sb(f"vtok{i}", [128, NCH, 256], BF16) for i in range(2)]
        Pm = [sb(f"Pm{i}", [128, 128], BF16) for i in range(NCH)]
        qd = [sb(f"qd{i}", [128, 128], BF16) for i in range(2)]
        state_f = sb("state_f", [128, H, 256], F32)
        state_bf = sb("state_bf", [128, H, 256], BF16)
        yn = sb("yn", [128, NCH, D], BF16)
        sg = [sb(f"sg{i}", [128, T], BF16) for i in range(2)]
        yretT = sb("yretT", [128, KC, T], BF16)
        yconvT = sb("yconvT", [128, KC, T], BF16)
        mergedT = sb("mergedT", [128, KC, T], BF16)
        cxs = sb("cxs", [128, T], F32)
        cxp = [sb(f"cxp{i}", [128, T + 2], F32) for i in range(2)]
        zt = sb("zt", [128, T], F32)
        sgc = sb("sgc", [128, T], F32)
        smr = sb("smr", [128, T], F32)
        smc = sb("smc", [128, T], F32)
        halo = sb("halo", [128, KC, 2], F32)
        st6 = sb("st6", [128, 6], F32)
        mv = sb("mv", [128, 2], F32)
        grs = sb("grs", [128, 1], F32)
        wslot = [sb(f"wslot{i}", [128, 2048], BF16) for i in range(NSLOT)]
        cst_t = sb("cst_t", [128, NCONST], F32)
        ident = sb("ident", [128, 128], BF16)
        perm = sb("perm", [128, 128], BF16)
        ones = sb("ones", [128, 128], BF16)

        pbank = [es.enter_context(nc.psum_tensor(f"pb{i}", [128, 512], F32)) for i in range(3)]
        bankM = es.enter_context(nc.psum_tensor("bM", [128, 512], F32))
        bankS = es.enter_context(nc.psum_tensor("bS", [128, 512], F32))
        bankO = es.enter_context(nc.psum_tensor("bO", [128, 512], F32))
        bankK = es.enter_context(nc.psum_tensor("bK", [128, 512], F32))
        bankT = es.enter_context(nc.psum_tensor("bT", [128, 1024], BF16))

        B = {}

        def bf(name):
            if name not in B:
                B[name] = Buf(name)
            return B[name]

        hT_b = [[bf(f"hT{i}_{k}") for k in range(KC)] for i in range(2)]
        pb_b = [bf(f"pb{i}") for i in range(3)]
        ws_b = [bf(f"ws{i}") for i in range(NSLOT)]

        P.op("sp", lambda e: e.dma_start(out=cst_t[:], in_=cst), writes=[bf("cst")], chan="cst")
        P.op("dve", lambda e: e.tensor_copy(out=ident[:], in_=cst_t[:, C_IDENT:C_IDENT + 128]), reads=[bf("cst")], writes=[bf("ident")])
        P.op("dve", lambda e: e.tensor_copy(out=perm[:], in_=cst_t[:, C_PERM:C_PERM + 128]), reads=[bf("cst")], writes=[bf("perm")])
        P.op("dve", lambda e: e.tensor_copy(out=ones[:], in_=cst_t[:, C_ONES:C_ONES + 128]), reads=[bf("cst")], writes=[bf("ones")])
        cb = bf("cst")

        def vec(l, which, kc):
            c = C_VECS + l * 96 + which * 16 + kc
            return cst_t[:, c:c + 1]

        def fvec(kc):
            c = C_VECS + DEPTH * 96 + kc
            return cst_t[:, c:c + 1]

        epsc = cst_t[:, C_EPS:C_EPS + 1]

        def cast_piece(l_, i_):
            r0 = (l_ * NPIECE + i_) * 128
            P.op("gq", lambda e: e.dma_start(out=wbfs[l_][i_ * 128:(i_ + 1) * 128, :], in_=wall[r0:r0 + 128, :]),
                 writes=[bf(f"wbf{l_}_{i_}")], chan="cv")

        for i_ in range(NPIECE):
            cast_piece(0, i_)

        seq = []
        for l_ in range(nlayers):
            for t in range(NT):
                for i in range(NPIECE):
                    seq.append((wbfs[l_][i * 128:(i + 1) * 128, :], bf(f"wbf{l_}_{i}")))
        wstate = {"seq": seq, "issued": 0, "used": 0}

        def w_issue_upto(n):
            while wstate["issued"] < min(n, len(wstate["seq"])):
                i = wstate["issued"]
                src, srcb = wstate["seq"][i]
                s = i % NSLOT
                P.op("sp", lambda e, s=s, src=src: e.dma_start(out=wslot[s][:], in_=src),
                     reads=[srcb], writes=[ws_b[s]], chan=f"ws{s}")
                wstate["issued"] += 1
                lcur, rem = divmod(i, NT * NPIECE)
                if lcur + 1 < nlayers and rem % NT == 0:
                    cast_piece(lcur + 1, rem // NT)

        def w_next():
            i = wstate["used"]
            w_issue_upto(i + NSLOT)
            wstate["used"] += 1
            s = i % NSLOT
            return wslot[s], ws_b[s]

        pstate = {"i": 0}

        def proj(rhs_tile, rhs_bufs):
            wt, wb = w_next()
            i = pstate["i"] % 3
            pstate["i"] += 1
            bank = pbank[i]

            def fn(e, wt=wt, bank=bank):
                for kc in range(KC):
                    ins = e.matmul(bank[:, 0:T], wt[:, kc * 128:(kc + 1) * 128], rhs_tile[:, kc, :],
                                   start=(kc == 0), stop=(kc == KC - 1))
                return ins
            P.op("pe", fn, reads=[wb] + rhs_bufs, writes=[pb_b[i]])
            return bank, pb_b[i]

        tiles = [(l, t) for l in range(nlayers) for t in range(NT)]
        NTILE = len(tiles)

        def src_of(l):
            return (hin, "hin") if l == 0 else (hbuf, "hbuf")

        def dst_of(l):
            return (hout, "hout") if l == nlayers - 1 else (hbuf, "hbuf")

        def load_h(i):
            l, t = tiles[i]
            src, name = src_of(l)
            b = i % 2
            P.op("gq", lambda e: e.dma_start(out=hT[b][:].rearrange("p k j -> p (k j)"), in_=src[t * 128:(t + 1) * 128, :]),
                 reads=[bf(f"{name}{t}")], writes=hT_b[b], chan=f"hT{b}")

        def load_rot(i):
            l, t = tiles[i]
            P.op("gq", lambda e: e.dma_start(out=rott[:].rearrange("p k j -> p (k j)"), in_=rot[t * 128:(t + 1) * 128, :]),
                 writes=[bf("rott")], chan="rott")

        def rms_stats(b):
            hTi, hTb = hT[b], hT_b[b]
            for kc in range(KC):
                s = kc % 2
                P.op("act", lambda e, kc=kc, s=s: e.activation(out=sq[s][:], in_=hTi[:, kc, :], func=AF.Square),
                     reads=[hTb[kc]], writes=[bf(f"sq{s}")])
                P.op("pe", lambda e, kc=kc, s=s: e.matmul(bankM[:, 0:T], ones[:], sq[s][:], start=(kc == 0), stop=(kc == KC - 1)),
                     reads=[bf(f"sq{s}"), bf("ones")], writes=[bf("bM")])
            P.op("act", lambda e: e.activation(out=rstd[:], in_=bankM[:, 0:T], func=AF.Sqrt, scale=1.0 / D, bias=epsc),
                 reads=[bf("bM"), cb], writes=[bf("rstd")])
            P.op("dve", lambda e: e.reciprocal(out=rstd[:], in_=rstd[:]), reads=[bf("rstd")], writes=[bf("rstd")])

        def make_u(i):
            l, t = tiles[i]
            b = i % 2
            hTi, hTb = hT[b], hT_b[b]
            rms_stats(b)
            for kc in range(KC):
                P.op("dve", lambda e, kc=kc: e.scalar_tensor_tensor(out=uT[:, kc, :], in0=hTi[:, kc, :], scalar=vec(l, 0, kc),
                                                                    in1=rstd[:], op0=ALU.mult, op1=ALU.mult),
                     reads=[hTb[kc], bf("rstd"), cb], writes=[bf("uT")])

        def rotary(bank, bankb, dst, dstb, i2):
            xr = xraw[i2]
            xb = bf(f"xraw{i2}")
            P.op("act", lambda e: e.copy(out=xr[:], in_=bank[:, 0:T]), reads=[bankb], writes=[xb])
            P.op("pe", lambda e: e.matmul(bankM[:, 0:T], perm[:], xr[:], start=True, stop=True),
                 reads=[xb, bf("perm")], writes=[bf("bM")])
            P.op("dve", lambda e: e.tensor_tensor(out=t1[:], in0=xr[:], in1=rott[:, 0, :], op=ALU.mult),
                 reads=[xb, bf("rott")], writes=[bf("t1")])
            P.op("dve", lambda e: e.tensor_tensor(out=t2[:], in0=bankM[:, 0:T], in1=rott[:, 1, :], op=ALU.mult),
                 reads=[bf("bM"), bf("rott")], writes=[bf("t2")])
            P.op("dve", lambda e: e.tensor_tensor(out=dst[:], in0=t1[:], in1=t2[:], op=ALU.add),
                 reads=[bf("t1"), bf("t2")], writes=[dstb])

        def proj_q(h):
            hb = h % 2
            bank, bb = proj(uT, [bf("uT")])
            rotary(bank, bb, qrot[hb], bf(f"qrot{hb}"), 0)

        def proj_k(h):
            hb = h % 2
            bank, bb = proj(uT, [bf("uT")])
            rotary(bank, bb, krot[hb], bf(f"krot{hb}"), 1)

            def trk(e):
                for c in range(NCH):
                    ins = e.transpose(bankT[:, c * 128:(c + 1) * 128], krot[hb][:, c * 128:(c + 1) * 128], ident[:])
                return ins
            P.op("pe", trk, reads=[bf(f"krot{hb}"), bf("ident")], writes=[bf("bT")])
            P.op("dve", lambda e: e.tensor_scalar_mul(out=ktok[hb][:].rearrange("p c d -> p (c d)"), in0=bankT[:, 0:NCH * 128],
                                                      scalar1=cst_t[:, C_KDEC + h:C_KDEC + h + 1]),
                 reads=[bf("bT"), cb], writes=[bf(f"ktok{hb}")])

        def proj_v(h, i):
            bank, bb = proj(uT, [bf("uT")])
            P.op("act", lambda e: e.copy(out=vT[i][:], in_=bank[:, 0:T]), reads=[bb], writes=[bf(f"vT{i}")])

        def v_trans(h):
            hb = h % 2

            def trv(e):
                for c in range(NCH):
                    for i in range(2):
                        o0 = (c * 2 + i) * 128
                        ins = e.transpose(bankT[:, o0:o0 + 128], vT[i][:, c * 128:(c + 1) * 128], ident[:])
                return ins
            P.op("pe", trv, reads=[bf("vT0"), bf("vT1"), bf("ident")], writes=[bf("bT")])
            P.op("act", lambda e: e.copy(out=vtok[hb][:].rearrange("p c d -> p (c d)"), in_=bankT[:, 0:NCH * 256]),
                 reads=[bf("bT")], writes=[bf(f"vtok{hb}")])

        def ret_S(h):
            hb = h % 2

            def fs(e):
                for c in range(NCH):
                    cs = slice(c * 128, (c + 1) * 128)
                    ins = e.matmul(bankS[:, cs], krot[hb][:, cs], qrot[hb][:, cs], start=True, stop=True)
                return ins
            P.op("pe", fs, reads=[bf(f"krot{hb}"), bf(f"qrot{hb}")], writes=[bf("bS")])
            for c in range(NCH):
                P.op("dve", lambda e, c=c: e.tensor_tensor(out=Pm[c][:], in0=bankS[:, c * 128:(c + 1) * 128],
                                                           in1=cst_t[:, C_MASK + h * 128:C_MASK + (h + 1) * 128], op=ALU.mult),
                     reads=[bf("bS"), cb], writes=[bf(f"Pm{c}")])

        def ret_chunk(h, c):
            hb = h % 2
            pi = c % 2
            cs = slice(c * 128, (c + 1) * 128)
            P.op("dve", lambda e: e.tensor_tensor(out=qd[pi][:], in0=qrot[hb][:, cs],
                                                  in1=cst_t[:, C_QDEC + h * 128:C_QDEC + (h + 1) * 128], op=ALU.mult),
                 reads=[bf(f"qrot{hb}"), cb], writes=[bf(f"qd{pi}")])

            def fo(e):
                e.matmul(bankO[:, 0:256], Pm[c][:], vtok[hb][:, c, :], start=True, stop=False)
                return e.matmul(bankO[:, 0:256], qd[pi][:], state_bf[:, h, :], start=False, stop=True)
            P.op("pe", fo, reads=[bf(f"Pm{c}"), bf(f"qd{pi}"), bf(f"vtok{hb}"), bf(f"stb{h}")], writes=[bf("bO")])
            P.op("pe", lambda e: e.matmul(bankK[:, 0:256], ktok[hb][:, c, :], vtok[hb][:, c, :], start=True, stop=True),
                 reads=[bf(f"ktok{hb}"), bf(f"vtok{hb}")], writes=[bf("bK")])
            P.op("dve", lambda e: e.scalar_tensor_tensor(out=state_f[:, h, :], in0=state_f[:, h, :], scalar=float(GAMMA[h] ** 128),
                                                         in1=bankK[:, 0:256], op0=ALU.mult, op1=ALU.add),
                 reads=[bf("bK"), bf(f"stf{h}")], writes=[bf(f"stf{h}")])
            P.op("act", lambda e: e.copy(out=state_bf[:, h, :], in_=state_f[:, h, :]),
                 reads=[bf(f"stf{h}")], writes=[bf(f"stb{h}")])
            P.op("dve", lambda e: e.bn_stats(out=st6[:], in_=bankO[:, 0:256]), reads=[bf("bO")], writes=[bf("st6")])
            P.op("dve", lambda e: e.bn_aggr(out=mv[:], in_=st6[:]), reads=[bf("st6")], writes=[bf("mv")])
            P.op("act", lambda e: e.activation(out=grs[:], in_=mv[:, 1:2], func=AF.Sqrt, scale=1.0, bias=epsc),
                 reads=[bf("mv"), cb], writes=[bf("grs")])
            P.op("dve", lambda e: e.reciprocal(out=grs[:], in_=grs[:]), reads=[bf("grs")], writes=[bf("grs")])
            P.op("dve", lambda e: e.tensor_scalar(out=yn[:, c, h * 256:(h + 1) * 256], in0=bankO[:, 0:256],
                                                  scalar1=mv[:, 0:1], scalar2=grs[:, 0:1],
                                                  op0=ALU.subtract, op1=ALU.mult),
                 reads=[bf("bO"), bf("mv"), bf("grs")], writes=[bf("yn")])

        def yret_cc(l, cc):
            s = cc % 2
            bank, bb = proj(uT, [bf("uT")])
            P.op("act", lambda e: e.activation(out=sg[s][:], in_=bank[:, 0:T], func=AF.Silu), reads=[bb], writes=[bf(f"sg{s}")])

            def try_(e):
                for c in range(NCH):
                    ins = e.transpose(bankT[:, c * 128:(c + 1) * 128], yn[:, c, cc * 128:(cc + 1) * 128], ident[:])
                return ins
            P.op("pe", try_, reads=[bf("yn"), bf("ident")], writes=[bf("bT")])
            P.op("dve", lambda e: e.scalar_tensor_tensor(out=yretT[:, cc, :], in0=bankT[:, 0:T], scalar=vec(l, 1, cc),
                                                         in1=sg[s][:], op0=ALU.mult, op1=ALU.mult),
                 reads=[bf("bT"), bf(f"sg{s}"), cb], writes=[bf("yretT")])

        def conv_cc(l, cc):
            s = cc % 2
            cb_ = bf(f"cxp{s}")
            bank, bb = proj(uT, [bf("uT")])
            P.op("act", lambda e, bank=bank: e.copy(out=cxs[:], in_=bank[:, 0:T]), reads=[bb], writes=[bf("cxs")])
            bank, bb = proj(uT, [bf("uT")])
            P.op("dve", lambda e, bank=bank: e.tensor_tensor(out=cxp[s][:, 2:T + 2], in0=bank[:, 0:T], in1=cxs[:], op=ALU.mult),
                 reads=[bb, bf("cxs")], writes=[cb_])
            P.op("dve", lambda e: e.tensor_copy(out=cxp[s][:, 0:2], in_=halo[:, cc, :]), reads=[bf("halo")], writes=[cb_])
            P.op("dve", lambda e: e.tensor_copy(out=halo[:, cc, :], in_=cxp[s][:, T:T + 2]), reads=[cb_], writes=[bf("halo")])
            P.op("dve", lambda e: e.tensor_scalar(out=zt[:], in0=cxp[s][:, 2:T + 2], scalar1=vec(l, 4, cc), scalar2=vec(l, 5, cc),
                                                  op0=ALU.mult, op1=ALU.add),
                 reads=[cb_, cb], writes=[bf("zt")])
            P.op("dve", lambda e: e.scalar_tensor_tensor(out=zt[:], in0=cxp[s][:, 1:T + 1], scalar=vec(l, 3, cc), in1=zt[:],
                                                         op0=ALU.mult, op1=ALU.add),
                 reads=[cb_, cb, bf("zt")], writes=[bf("zt")])
            P.op("dve", lambda e: e.scalar_tensor_tensor(out=zt[:], in0=cxp[s][:, 0:T], scalar=vec(l, 2, cc), in1=zt[:],
                                                         op0=ALU.mult, op1=ALU.add),
                 reads=[cb_, cb, bf("zt")], writes=[bf("zt")])
            bank, bb = proj(uT, [bf("uT")])
            P.op("dve", lambda e, bank=bank: e.tensor_tensor(out=zt[:], in0=bank[:, 0:T], in1=zt[:], op=ALU.mult),
                 reads=[bb, bf("zt")], writes=[bf("zt")])
            bank, bb = proj(uT, [bf("uT")])
            P.op("act", lambda e, bank=bank: e.activation(out=sgc[:], in_=bank[:, 0:T], func=AF.Silu), reads=[bb], writes=[bf("sgc")])
            P.op("dve", lambda e: e.tensor_tensor(out=yconvT[:, cc, :], in0=zt[:], in1=sgc[:], op=ALU.mult),
                 reads=[bf("zt"), bf("sgc")], writes=[bf("yconvT")])

        def merge_dd(dd):
            bank, bb = proj(uT, [bf("uT")])
            P.op("act", lambda e, bank=bank: e.activation(out=smr[:], in_=bank[:, 0:T], func=AF.Sigmoid), reads=[bb], writes=[bf("smr")])
            bank, bb = proj(uT, [bf("uT")])
            P.op("act", lambda e, bank=bank: e.activation(out=smc[:], in_=bank[:, 0:T], func=AF.Sigmoid), reads=[bb], writes=[bf("smc")])
            bank, bb = proj(yretT, [bf("yretT")])
            P.op("dve", lambda e, bank=bank: e.tensor_tensor(out=t1[:], in0=bank[:, 0:T], in1=smr[:], op=ALU.mult),
                 reads=[bb, bf("smr")], writes=[bf("t1")])
            bank, bb = proj(yconvT, [bf("yconvT")])
            P.op("dve", lambda e, bank=bank: e.tensor_tensor(out=t2[:], in0=bank[:, 0:T], in1=smc[:], op=ALU.mult),
                 reads=[bb, bf("smc")], writes=[bf("t2")])
            P.op("dve", lambda e: e.tensor_tensor(out=mergedT[:, dd, :], in0=t1[:], in1=t2[:], op=ALU.add),
                 reads=[bf("t1"), bf("t2")], writes=[bf("mergedT")])

        for i in range(NTILE):
            l, t = tiles[i]
            b = i % 2
            hTi, hTb = hT[b], hT_b[b]
            if t == 0:
                P.op("dve", lambda e: e.memset(halo[:].rearrange("p k j -> p (k j)"), 0.0), writes=[bf("halo")])
                P.op("dve", lambda e: e.memset(state_f[:].rearrange("p h v -> p (h v)"), 0.0), writes=[bf(f"stf{h}") for h in range(H)])
                P.op("dve", lambda e: e.memset(state_bf[:].rearrange("p h v -> p (h v)"), 0.0), writes=[bf(f"stb{h}") for h in range(H)])
            if i == 0:
                load_h(0)
                load_rot(0)
                make_u(0)
            for h in range(H):
                proj_q(h)
                if h >= 1:
                    ret_S(h - 1)
                proj_k(h)
                if h >= 1:
                    ret_chunk(h - 1, 0)
                proj_v(h, 0)
                if h >= 1:
                    ret_chunk(h - 1, 1)
                proj_v(h, 1)
                v_trans(h)
                if h >= 1:
                    ret_chunk(h - 1, 2)
            if i + 1 < NTILE:
                load_h(i + 1)
                load_rot(i + 1)
            yret_cc(l, 0)
            ret_S(H - 1)
            yret_cc(l, 1)
            ret_chunk(H - 1, 0)
            yret_cc(l, 2)
            ret_chunk(H - 1, 1)
            yret_cc(l, 3)
            ret_chunk(H - 1, 2)
            for cc in range(4, KC):
                yret_cc(l, cc)
            for cc in range(KC):
                conv_cc(l, cc)
            for dd in range(KC):
                merge_dd(dd)
            if i + 1 < NTILE:
                make_u(i + 1)
            for dd in range(KC):
                bank, bb = proj(mergedT, [bf("mergedT")])
                P.op("dve", lambda e, bank=bank, dd=dd, hTi=hTi: e.tensor_tensor(out=hTi[:, dd, :], in0=hTi[:, dd, :], in1=bank[:, 0:T], op=ALU.add),
                     reads=[bb, hTb[dd]], writes=[hTb[dd]])
            if l == nlayers - 1:
                rms_stats(b)
                for kc in range(KC):
                    P.op("dve", lambda e, kc=kc, hTi=hTi: e.scalar_tensor_tensor(out=hTi[:, kc, :], in0=hTi[:, kc, :], scalar=fvec(kc),
                                                                               in1=rstd[:], op0=ALU.mult, op1=ALU.mult),
                         reads=[hTb[kc], bf("rstd"), cb], writes=[hTb[kc]])
            dst, dname = dst_of(l)
            P.op("gq", lambda e, t=t, dst=dst, hTi=hTi: e.dma_start(out=dst[t * 128:(t + 1) * 128, :], in_=hTi[:].rearrange("p k j -> p (k j)")),
                 reads=hTb, writes=[bf(f"{dname}{t}")], chan=f"hT{b}")

        assert wstate["used"] == len(wstate["seq"]), (wstate["used"], len(wstate["seq"]))
        P.finalize()

        plan = P.sem_plan()
        sems = {key: [es.enter_context(nc.semaphore(f"s_{key[0]}_{key[1]}_{i}")) for i in range(n)] for key, n in plan.items()}
        block = es.enter_context(nc.Block())

        @block.tensor
        def _(e):
            P.emit("pe", e, sems)

        @block.scalar
        def _(e):
            P.emit("act", e, sems)

        @block.vector
        def _(e):
            P.emit("dve", e, sems)

        @block.gpsimd
        def _(e):
            P.emit("gq", e, sems)

        @block.sync
        def _(e):
            P.emit("sp", e, sems)
            P.final_waits(e, sems)
    return nc


def _piece_chunk_order():
    q0, k0, v0, gr0, cx0, cp0, co0, gc0, mr0, mc0 = [x // 128 for x in
                                                      (0, 1024, 2048, 4096, 6144, 8192, 10240, 12288, 14336, 16384)]
    wb0_0 = 18432 // 128
    wb1_0 = wb0_0 + 16
    wo_0 = wb1_0 + 16
    order = []
    for h in range(H):
        order += [q0 + h, k0 + h, v0 + 2 * h, v0 + 2 * h + 1]
    for cc in range(KC):
        order += [gr0 + cc]
    for cc in range(KC):
        order += [cx0 + cc, cp0 + cc, co0 + cc, gc0 + cc]
    for dd in range(KC):
        order += [mr0 + dd, mc0 + dd, wb0_0 + dd, wb1_0 + dd]
    for dd in range(KC):
        order += [wo_0 + dd]
    assert len(order) == NPIECE
    return order


def _layer_weights(w_in, w_branch, w_out, l):
    order = np.array(_piece_chunk_order())
    wall = np.concatenate([w_in[l], w_branch[l, 0], w_branch[l, 1], w_out[l]], axis=1)
    w4 = wall.reshape(KC, 128, NPIECE, 128)[:, :, order, :]
    wp = np.ascontiguousarray(w4.transpose(2, 1, 0, 3)).reshape(NPIECE * 128, 2048)
    return wp


def _tile_layout(hloc):
    a = hloc.reshape(NT, T, KC, 128).transpose(0, 3, 2, 1)
    return np.ascontiguousarray(a).reshape(NT * 128, KC * T)


def _untile(ht):
    a = ht.reshape(NT, 128, KC, T).transpose(0, 3, 2, 1)
    return np.ascontiguousarray(a).reshape(NT * T, D)


def _consts(core, norm_g, gn_g, conv_w, conv_b, final_norm_g):
    b, s = (core, 0) if SOLO else divmod(core, 4)
    c = np.zeros((128, NCONST), np.float32)
    idx = np.arange(128, dtype=np.float64)
    scale = 128.0 ** -0.5
    for h in range(H):
        g = GAMMA[h]
        diff = idx[None, :] - idx[:, None]
        m = np.where(diff >= 0, g ** np.maximum(diff, 0.0), 0.0) * scale
        c[:, C_MASK + h * 128:C_MASK + (h + 1) * 128] = m
        c[:, C_QDEC + h * 128:C_QDEC + (h + 1) * 128] = (g ** (idx + 1.0))[None, :] * scale
        c[:, C_KDEC + h] = g ** (127.0 - idx)
    c[:, C_IDENT:C_IDENT + 128] = np.eye(128)
    pm = np.zeros((128, 128))
    for m_ in range(128):
        pm[(m_ + 64) % 128, m_] = 1.0
    c[:, C_PERM:C_PERM + 128] = pm
    c[:, C_ONES:C_ONES + 128] = 1.0
    for l in range(DEPTH):
        base = C_VECS + l * 96
        c[:, base + 0:base + 16] = norm_g[l].reshape(KC, 128).T
        c[:, base + 16:base + 32] = gn_g[l].reshape(KC, 128).T
        for k in range(3):
            c[:, base + 32 + 16 * k:base + 48 + 16 * k] = conv_w[l, k].reshape(KC, 128).T
        c[:, base + 80:base + 96] = conv_b[l].reshape(KC, 128).T
    c[:, C_VECS + DEPTH * 96:C_VECS + DEPTH * 96 + 16] = final_norm_g.reshape(KC, 128).T
    for j in range(0 if SOLO else NCORE):
        bj, sj = divmod(j, 4)
        if bj == b and sj < s:
            for h in range(H):
                c[:, C_COEF + j * 8 + h] = GAMMA[h] ** (4096.0 * (s - 1 - sj))
        if bj == b and sj == s - 1:
            c[:, C_HCOEF + j] = 1.0
    c[:, C_EPS] = EPS
    c[:, C_SELF + core] = 1.0
    return c


def _rot_tables(core):
    b, s = (core, 0) if SOLO else divmod(core, 4)
    tau = np.arange(NT * T)
    pos = np.where(tau >= 128, 16 + s * 4096 + (tau - 128), np.maximum(tau - 112, 0)).astype(np.float32)
    inv_freq = (np.float32(10000.0) ** (-(np.arange(64, dtype=np.float32) / np.float32(64)))).astype(np.float32)
    ang = (pos[:, None] * inv_freq[None, :]).astype(np.float32)
    cos = np.cos(ang).astype(np.float32)
    sin = np.sin(ang).astype(np.float32)
    cosT = np.concatenate([cos, cos], axis=1).T
    sinT = np.concatenate([-sin, sin], axis=1).T
    r = np.stack([cosT.reshape(128, NT, T), sinT.reshape(128, NT, T)], axis=2)
    return np.ascontiguousarray(r.transpose(1, 0, 2, 3)).reshape(NT * 128, 2 * T)


_PROG_CACHE = {}


def _prog():
    if "p" not in _PROG_CACHE:
        _PROG_CACHE["p"] = build_program()
    return _PROG_CACHE["p"]


def kernel(x, meta_tokens, norm_g, w_in, conv_w, conv_b, gn_g, w_branch, w_out, final_norm_g):
    x = np.asarray(x, np.float32)
    meta_tokens = np.asarray(meta_tokens, np.float32)
    norm_g = np.asarray(norm_g, np.float32)
    w_in = np.asarray(w_in, np.float32)
    conv_w = np.asarray(conv_w, np.float32)
    conv_b = np.asarray(conv_b, np.float32)
    gn_g = np.asarray(gn_g, np.float32)
    w_branch = np.asarray(w_branch, np.float32)
    w_out = np.asarray(w_out, np.float32)
    final_norm_g = np.asarray(final_norm_g, np.float32)

    cores = list(range(NCORE_USED))
    nreal = (NT * T - 128)
    hs = []
    for core in cores:
        b, s = (core, 0) if SOLO else divmod(core, 4)
        hloc = np.zeros((NT * T, D), np.float32)
        if s == 0:
            hloc[112:128] = meta_tokens
        hloc[128:] = x[b, s * nreal:(s + 1) * nreal]
        hs.append(_tile_layout(hloc))
    rots = [_rot_tables(c) for c in cores]
    csts = [_consts(c, norm_g, gn_g, conv_w, conv_b, final_norm_g) for c in cores]
    wall = np.empty((DEPTH * NPIECE * 128, 2048), np.float32)
    for l in range(DEPTH):
        wall[l * NPIECE * 128:(l + 1) * NPIECE * 128] = _layer_weights(w_in, w_branch, w_out, l)

    nc = _prog()
    res = run_bass_kernel_spmd(nc, [{"hin": hs[c], "rot": rots[c], "cst": csts[c], "wall": wall} for c in cores], core_ids=cores)
    out = np.zeros((2, 16384, D), np.float32)
    for core in cores:
        b, s = (core, 0) if SOLO else divmod(core, 4)
        out[b, s * nreal:(s + 1) * nreal] = _untile(res.results[core]["hout"])[128:]
    return out
```

```python
import numpy as np
import ml_dtypes
from contextlib import ExitStack
import concourse.bass as bass
import concourse.mybir as mybir
from concourse.bass_utils import run_bass_kernel_spmd

F32 = mybir.dt.float32
BF16 = mybir.dt.bfloat16
AF = mybir.ActivationFunctionType
ALU = mybir.AluOpType

D = 2048
KC = 16
T = 384
NCH = 3
SOLO = True
NT = 43 if SOLO else 11
NCORE_USED = 2 if SOLO else 8
H = 8
DEPTH = 4
NCORE = 8
EPS = 1e-6
NPIECE = 192
EW = 2048 + 32
NSLOT = 8
SAME_ENG_SYNC = True

C_MASK = 0
C_QDEC = C_MASK + 1024
C_KDEC = C_QDEC + 1024
C_IDENT = C_KDEC + 8
C_PERM = C_IDENT + 128
C_ONES = C_PERM + 128
C_VECS = C_ONES + 128
C_COEF = C_VECS + DEPTH * 96 + 16
C_HCOEF = C_COEF + 64
C_EPS = C_HCOEF + 8
C_SELF = C_EPS + 1
NCONST = C_SELF + 8

GAMMA = [1.0 - 2.0 ** (-5.0 - h) for h in range(H)]


class Buf:
    __slots__ = ("name", "w", "r")

    def __init__(self, name):
        self.name = name
        self.w = None
        self.r = []


class Op:
    __slots__ = ("eng", "fn", "deps", "token", "signal", "seq", "is_dma", "chan", "waits", "sigval", "inc")


class Prog:
    ENGS = ("pe", "act", "dve", "sp", "gq")

    def __init__(self):
        self.ops = {e: [] for e in self.ENGS}
        self.chan_count = {}
        self.chan_last = {}

    def op(self, eng, fn, reads=(), writes=(), chan=None, inc=16):
        o = Op()
        o.inc = inc
        o.eng = eng
        o.fn = fn
        o.is_dma = chan is not None
        o.chan = chan
        o.seq = len(self.ops[eng])
        o.signal = False
        o.token = None
        o.sigval = None
        deps = []
        for b in reads:
            if b.w is not None:
                deps.append((b.w, 0))
        for b in writes:
            if b.w is not None:
                deps.append((b.w, 1))
            for r in b.r:
                deps.append((r, 2))
        if o.is_dma:
            k = self.chan_count.get(chan, 0) + 1
            self.chan_count[chan] = k
            o.token = (chan, k)
            prev = self.chan_last.get(chan)
            if prev is not None:
                deps.append((prev, 0))
            self.chan_last[chan] = o
        o.deps = deps
        for b in reads:
            b.r.append(o)
        for b in writes:
            b.w = o
            b.r = []
        self.ops[eng].append(o)
        return o

    def finalize(self):
        for eng in self.ENGS:
            waited = {}
            for o in self.ops[eng]:
                need = {}
                for (p, kind) in o.deps:
                    if p is o:
                        continue
                    if p.is_dma:
                        key = ("c", p.chan)
                        val = p.token[1]
                    else:
                        if p.eng == eng:
                            if eng == "pe" or not SAME_ENG_SYNC or kind == 2:
                                continue
                        key = ("e", p.eng)
                        val = p.seq
                    if waited.get(key, -1) >= val:
                        continue
                    cur = need.get(key)
                    if cur is None or cur[0] < val:
                        need[key] = (val, p)
                o.waits = []
                for key, (val, p) in need.items():
                    waited[key] = val
                    if not p.is_dma:
                        p.signal = True
                    o.waits.append(p)
        for eng in ("pe", "act", "dve"):
            cnt = 0
            for o in self.ops[eng]:
                if o.signal:
                    cnt += 1
                    o.sigval = cnt

    LIM = 16384

    def sem_plan(self):
        plan = {}
        for eng in ("pe", "act", "dve"):
            n = max([o.sigval or 0 for o in self.ops[eng]] + [0])
            plan[("e", eng)] = max(1, -(-n // self.LIM))
        for c, k in self.chan_count.items():
            plan[("c", c)] = max(1, -(-k // (self.LIM // 16)))
        return plan

    def _loc(self, ordinal, per, inc):
        return (ordinal - 1) // per, ((ordinal - 1) % per + 1) * inc

    def emit(self, eng, e, sems):
        per_d = self.LIM // 16
        for o in self.ops[eng]:
            for p in o.waits:
                if p.is_dma:
                    b, v = self._loc(p.token[1], per_d, p.inc)
                    e.wait_ge(sems[("c", p.chan)][b], v)
                else:
                    b, v = self._loc(p.sigval, self.LIM, 1)
                    e.wait_ge(sems[("e", p.eng)][b], v)
            ins = o.fn(e)
            if o.is_dma:
                b, v = self._loc(o.token[1], per_d, o.inc)
                ins.then_inc(sems[("c", o.chan)][b], o.inc)
            elif o.signal:
                b, v = self._loc(o.sigval, self.LIM, 1)
                ins.then_inc(sems[("e", o.eng)][b], 1)

    def final_waits(self, e, sems):
        per_d = self.LIM // 16
        for c, o in self.chan_last.items():
            b, v = self._loc(o.token[1], per_d, o.inc)
            e.wait_ge(sems[("c", c)][b], v)


def build_program(nlayers=DEPTH):
    nc = bass.Bass("TRN2", target_bir_lowering=False)
    P = Prog()
    hin = nc.dram_tensor("hin", [NT * 128, KC * T], F32, kind="ExternalInput").ap()
    rot = nc.dram_tensor("rot", [NT * 128, 2 * T], F32, kind="ExternalInput").ap()
    cst = nc.dram_tensor("cst", [128, NCONST], F32, kind="ExternalInput").ap()
    wall = nc.dram_tensor("wall", [DEPTH * NPIECE * 128, 2048], F32, kind="ExternalInput").ap()
    hout = nc.dram_tensor("hout", [NT * 128, KC * T], F32, kind="ExternalOutput").ap()
    hbuf = nc.dram_tensor("hbuf", [NT * 128, KC * T], F32).ap()
    wbfs = [nc.dram_tensor(f"wbf{l}", [NPIECE * 128, 2048], BF16).ap() for l in range(DEPTH)]

    es = ExitStack()
    with es:
        def sb(name, shape, dt):
            return es.enter_context(nc.sbuf_tensor(name, shape, dt))

        hT = [sb(f"hT{i}", [128, KC, T], F32) for i in range(2)]
        uT = sb("uT", [128, KC, T], BF16)
        sq = [sb(f"sq{i}", [128, T], BF16) for i in range(2)]
        rstd = sb("rstd", [128, T], F32)
        rott = sb("rott", [128, 2, T], F32)
        xraw = [sb(f"xraw{i}", [128, T], BF16) for i in range(2)]
        t1 = sb("t1", [128, T], F32)
        t2 = sb("t2", [128, T], F32)
        qrot = [sb(f"qrot{i}", [128, T], BF16) for i in range(2)]
        krot = [sb(f"krot{i}", [128, T], BF16) for i in range(2)]
        ktok = [sb(f"ktok{i}", [128, NCH, 128], BF16) for i in range(2)]
        vT = [sb(f"vT{i}", [128, T], BF16) for i in range(2)]
        vtok = [sb(f"vtok{i}", [128, NCH, 256], BF16) for i in range(2)]
        Pm = [sb(f"Pm{i}", [128, 128], BF16) for i in range(NCH)]
        qd = [sb(f"qd{i}", [128, 128], BF16) for i in range(2)]
        state_f = sb("state_f", [128, H, 256], F32)
        state_bf = sb("state_bf", [128, H, 256], BF16)
        yn = sb("yn", [128, NCH, D], BF16)
        sg = [sb(f"sg{i}", [128, T], BF16) for i in range(2)]
        yretT = sb("yretT", [128, KC, T], BF16)
        yconvT = sb("yconvT", [128, KC, T], BF16)
        mergedT = sb("mergedT", [128, KC, T], BF16)
        cxs = sb("cxs", [128, T], F32)
        cxp = [sb(f"cxp{i}", [128, T + 2], F32) for i in range(2)]
        zt = sb("zt", [128, T], F32)
        sgc = sb("sgc", [128, T], F32)
        smr = sb("smr", [128, T], F32)
        smc = sb("smc", [128, T], F32)
        halo = sb("halo", [128, KC, 2], F32)
        st6 = sb("st6", [128, 6], F32)
        mv = sb("mv", [128, 2], F32)
        grs = sb("grs", [128, 1], F32)
        wslot = [sb(f"wslot{i}", [128, 2048], BF16) for i in range(NSLOT)]
        cst_t = sb("cst_t", [128, NCONST], F32)
        ident = sb("ident", [128, 128], BF16)
        perm = sb("perm", [128, 128], BF16)
        ones = sb("ones", [128, 128], BF16)

        pbank = [es.enter_context(nc.psum_tensor(f"pb{i}", [128, 512], F32)) for i in range(3)]
        bankM = es.enter_context(nc.psum_tensor("bM", [128, 512], F32))
        bankS = es.enter_context(nc.psum_tensor("bS", [128, 512], F32))
        bankO = es.enter_context(nc.psum_tensor("bO", [128, 512], F32))
        bankK = es.enter_context(nc.psum_tensor("bK", [128, 512], F32))
        bankT = es.enter_context(nc.psum_tensor("bT", [128, 1024], BF16))

        B = {}

        def bf(name):
            if name not in B:
                B[name] = Buf(name)
            return B[name]

        hT_b = [[bf(f"hT{i}_{k}") for k in range(KC)] for i in range(2)]
        pb_b = [bf(f"pb{i}") for i in range(3)]
        ws_b = [bf(f"ws{i}") for i in range(NSLOT)]

        P.op("sp", lambda e: e.dma_start(out=cst_t[:], in_=cst), writes=[bf("cst")], chan="cst")
        P.op("dve", lambda e: e.tensor_copy(out=ident[:], in_=cst_t[:, C_IDENT:C_IDENT + 128]), reads=[bf("cst")], writes=[bf("ident")])
        P.op("dve", lambda e: e.tensor_copy(out=perm[:], in_=cst_t[:, C_PERM:C_PERM + 128]), reads=[bf("cst")], writes=[bf("perm")])
        P.op("dve", lambda e: e.tensor_copy(out=ones[:], in_=cst_t[:, C_ONES:C_ONES + 128]), reads=[bf("cst")], writes=[bf("ones")])
        cb = bf("cst")

        def vec(l, which, kc):
            c = C_VECS + l * 96 + which * 16 + kc
            return cst_t[:, c:c + 1]

        def fvec(kc):
            c = C_VECS + DEPTH * 96 + kc
            return cst_t[:, c:c + 1]

        epsc = cst_t[:, C_EPS:C_EPS + 1]

        def cast_piece(l_, i_):
            r0 = (l_ * NPIECE + i_) * 128
            P.op("gq", lambda e: e.dma_start(out=wbfs[l_][i_ * 128:(i_ + 1) * 128, :], in_=wall[r0:r0 + 128, :]),
                 writes=[bf(f"wbf{l_}_{i_}")], chan="cv")

        for i_ in range(NPIECE):
            cast_piece(0, i_)

        seq = []
        for l_ in range(nlayers):
            for t in range(NT):
                for i in range(NPIECE):
                    seq.append((wbfs[l_][i * 128:(i + 1) * 128, :], bf(f"wbf{l_}_{i}")))
        wstate = {"seq": seq, "issued": 0, "used": 0}

        def w_issue_upto(n):
            while wstate["issued"] < min(n, len(wstate["seq"])):
                i = wstate["issued"]
                src, srcb = wstate["seq"][i]
                s = i % NSLOT
                P.op("sp", lambda e, s=s, src=src: e.dma_start(out=wslot[s][:], in_=src),
                     reads=[srcb], writes=[ws_b[s]], chan=f"ws{s}")
                wstate["issued"] += 1
                lcur, rem = divmod(i, NT * NPIECE)
                if lcur + 1 < nlayers and rem % NT == 0:
                    cast_piece(lcur + 1, rem // NT)

        def w_next():
            i = wstate["used"]
            w_issue_upto(i + NSLOT)
            wstate["used"] += 1
            s = i % NSLOT
            return wslot[s], ws_b[s]

        pstate = {"i": 0}

        def proj(rhs_tile, rhs_bufs):
            wt, wb = w_next()
            i = pstate["i"] % 3
            pstate["i"] += 1
            bank = pbank[i]

            def fn(e, wt=wt, bank=bank):
                for kc in range(KC):
                    ins = e.matmul(bank[:, 0:T], wt[:, kc * 128:(kc + 1) * 128], rhs_tile[:, kc, :],
                                   start=(kc == 0), stop=(kc == KC - 1))
                return ins
            P.op("pe", fn, reads=[wb] + rhs_bufs, writes=[pb_b[i]])
            return bank, pb_b[i]

        tiles = [(l, t) for l in range(nlayers) for t in range(NT)]
        NTILE = len(tiles)

        def src_of(l):
            return (hin, "hin") if l == 0 else (hbuf, "hbuf")

        def dst_of(l):
            return (hout, "hout") if l == nlayers - 1 else (hbuf, "hbuf")

        def load_h(i):
            l, t = tiles[i]
            src, name = src_of(l)
            b = i % 2
            P.op("gq", lambda e: e.dma_start(out=hT[b][:].rearrange("p k j -> p (k j)"), in_=src[t * 128:(t + 1) * 128, :]),
                 reads=[bf(f"{name}{t}")], writes=hT_b[b], chan=f"hT{b}")

        def load_rot(i):
            l, t = tiles[i]
            P.op("gq", lambda e: e.dma_start(out=rott[:].rearrange("p k j -> p (k j)"), in_=rot[t * 128:(t + 1) * 128, :]),
                 writes=[bf("rott")], chan="rott")

        def rms_stats(b):
            hTi, hTb = hT[b], hT_b[b]
            for kc in range(KC):
                s = kc % 2
                P.op("act", lambda e, kc=kc, s=s: e.activation(out=sq[s][:], in_=hTi[:, kc, :], func=AF.Square),
                     reads=[hTb[kc]], writes=[bf(f"sq{s}")])
                P.op("pe", lambda e, kc=kc, s=s: e.matmul(bankM[:, 0:T], ones[:], sq[s][:], start=(kc == 0), stop=(kc == KC - 1)),
                     reads=[bf(f"sq{s}"), bf("ones")], writes=[bf("bM")])
            P.op("act", lambda e: e.activation(out=rstd[:], in_=bankM[:, 0:T], func=AF.Sqrt, scale=1.0 / D, bias=epsc),
                 reads=[bf("bM"), cb], writes=[bf("rstd")])
            P.op("dve", lambda e: e.reciprocal(out=rstd[:], in_=rstd[:]), reads=[bf("rstd")], writes=[bf("rstd")])

        def make_u(i):
            l, t = tiles[i]
            b = i % 2
            hTi, hTb = hT[b], hT_b[b]
            rms_stats(b)
            for kc in range(KC):
                P.op("dve", lambda e, kc=kc: e.scalar_tensor_tensor(out=uT[:, kc, :], in0=hTi[:, kc, :], scalar=vec(l, 0, kc),
                                                                    in1=rstd[:], op0=ALU.mult, op1=ALU.mult),
                     reads=[hTb[kc], bf("rstd"), cb], writes=[bf("uT")])

        def rot_copy(bank, bankb, i2):
            xr = xraw[i2]
            P.op("act", lambda e: e.copy(out=xr[:], in_=bank[:, 0:T]), reads=[bankb], writes=[bf(f"xraw{i2}")])

        def rot_finish(i2, dst, dstb):
            xr = xraw[i2]
            xb = bf(f"xraw{i2}")
            P.op("pe", lambda e: e.matmul(bankM[:, 0:T], perm[:], xr[:], start=True, stop=True),
                 reads=[xb, bf("perm")], writes=[bf("bM")])
            P.op("dve", lambda e: e.tensor_tensor(out=t1[:], in0=xr[:], in1=rott[:, 0, :], op=ALU.mult),
                 reads=[xb, bf("rott")], writes=[bf("t1")])
            P.op("dve", lambda e: e.tensor_tensor(out=t2[:], in0=bankM[:, 0:T], in1=rott[:, 1, :], op=ALU.mult),
                 reads=[bf("bM"), bf("rott")], writes=[bf("t2")])
            P.op("dve", lambda e: e.tensor_tensor(out=dst[:], in0=t1[:], in1=t2[:], op=ALU.add),
                 reads=[bf("t1"), bf("t2")], writes=[dstb])

        def k_trans(h):
            hb = h % 2

            def trk(e):
                for c in range(NCH):
                    ins = e.transpose(bankT[:, c * 128:(c + 1) * 128], krot[hb][:, c * 128:(c + 1) * 128], ident[:])
                return ins
            P.op("pe", trk, reads=[bf(f"krot{hb}"), bf("ident")], writes=[bf("bT")])
            P.op("dve", lambda e: e.tensor_scalar_mul(out=ktok[hb][:].rearrange("p c d -> p (c d)"), in0=bankT[:, 0:NCH * 128],
                                                      scalar1=cst_t[:, C_KDEC + h:C_KDEC + h + 1]),
                 reads=[bf("bT"), cb], writes=[bf(f"ktok{hb}")])

        def proj_v(h, i):
            bank, bb = proj(uT, [bf("uT")])
            P.op("act", lambda e: e.copy(out=vT[i][:], in_=bank[:, 0:T]), reads=[bb], writes=[bf(f"vT{i}")])

        def v_trans(h):
            hb = h % 2

            def trv(e):
                for c in range(NCH):
                    for i in range(2):
                        o0 = (c * 2 + i) * 128
                        ins = e.transpose(bankT[:, o0:o0 + 128], vT[i][:, c * 128:(c + 1) * 128], ident[:])
                return ins
            P.op("pe", trv, reads=[bf("vT0"), bf("vT1"), bf("ident")], writes=[bf("bT")])
            P.op("act", lambda e: e.copy(out=vtok[hb][:].rearrange("p c d -> p (c d)"), in_=bankT[:, 0:NCH * 256]),
                 reads=[bf("bT")], writes=[bf(f"vtok{hb}")])

        def ret_S(h):
            hb = h % 2

            def fs(e):
                for c in range(NCH):
                    cs = slice(c * 128, (c + 1) * 128)
                    ins = e.matmul(bankS[:, cs], krot[hb][:, cs], qrot[hb][:, cs], start=True, stop=True)
                return ins
            P.op("pe", fs, reads=[bf(f"krot{hb}"), bf(f"qrot{hb}")], writes=[bf("bS")])
            for c in range(NCH):
                P.op("dve", lambda e, c=c: e.tensor_tensor(out=Pm[c][:], in0=bankS[:, c * 128:(c + 1) * 128],
                                                           in1=cst_t[:, C_MASK + h * 128:C_MASK + (h + 1) * 128], op=ALU.mult),
                     reads=[bf("bS"), cb], writes=[bf(f"Pm{c}")])

        def ret_chunk(h, c):
            hb = h % 2
            pi = c % 2
            cs = slice(c * 128, (c + 1) * 128)
            P.op("dve", lambda e: e.tensor_tensor(out=qd[pi][:], in0=qrot[hb][:, cs],
                                                  in1=cst_t[:, C_QDEC + h * 128:C_QDEC + (h + 1) * 128], op=ALU.mult),
                 reads=[bf(f"qrot{hb}"), cb], writes=[bf(f"qd{pi}")])

            def fo(e):
                e.matmul(bankO[:, 0:256], Pm[c][:], vtok[hb][:, c, :], start=True, stop=False)
                return e.matmul(bankO[:, 0:256], qd[pi][:], state_bf[:, h, :], start=False, stop=True)
            P.op("pe", fo, reads=[bf(f"Pm{c}"), bf(f"qd{pi}"), bf(f"vtok{hb}"), bf(f"stb{h}")], writes=[bf("bO")])
            P.op("pe", lambda e: e.matmul(bankK[:, 0:256], ktok[hb][:, c, :], vtok[hb][:, c, :], start=True, stop=True),
                 reads=[bf(f"ktok{hb}"), bf(f"vtok{hb}")], writes=[bf("bK")])
            P.op("dve", lambda e: e.scalar_tensor_tensor(out=state_f[:, h, :], in0=state_f[:, h, :], scalar=float(GAMMA[h] ** 128),
                                                         in1=bankK[:, 0:256], op0=ALU.mult, op1=ALU.add),
                 reads=[bf("bK"), bf(f"stf{h}")], writes=[bf(f"stf{h}")])
            P.op("act", lambda e: e.copy(out=state_bf[:, h, :], in_=state_f[:, h, :]),
                 reads=[bf(f"stf{h}")], writes=[bf(f"stb{h}")])
            P.op("dve", lambda e: e.bn_stats(out=st6[:], in_=bankO[:, 0:256]), reads=[bf("bO")], writes=[bf("st6")])
            P.op("dve", lambda e: e.bn_aggr(out=mv[:], in_=st6[:]), reads=[bf("st6")], writes=[bf("mv")])
            P.op("act", lambda e: e.activation(out=grs[:], in_=mv[:, 1:2], func=AF.Sqrt, scale=1.0, bias=epsc),
                 reads=[bf("mv"), cb], writes=[bf("grs")])
            P.op("dve", lambda e: e.reciprocal(out=grs[:], in_=grs[:]), reads=[bf("grs")], writes=[bf("grs")])
            P.op("dve", lambda e: e.tensor_scalar(out=yn[:, c, h * 256:(h + 1) * 256], in0=bankO[:, 0:256],
                                                  scalar1=mv[:, 0:1], scalar2=grs[:, 0:1],
                                                  op0=ALU.subtract, op1=ALU.mult),
                 reads=[bf("bO"), bf("mv"), bf("grs")], writes=[bf("yn")])

        def yret_cc(l, cc):
            s = cc % 2
            bank, bb = proj(uT, [bf("uT")])
            P.op("act", lambda e: e.activation(out=sg[s][:], in_=bank[:, 0:T], func=AF.Silu), reads=[bb], writes=[bf(f"sg{s}")])

            def try_(e):
                for c in range(NCH):
                    ins = e.transpose(bankT[:, c * 128:(c + 1) * 128], yn[:, c, cc * 128:(cc + 1) * 128], ident[:])
                return ins
            P.op("pe", try_, reads=[bf("yn"), bf("ident")], writes=[bf("bT")])
            P.op("dve", lambda e: e.scalar_tensor_tensor(out=yretT[:, cc, :], in0=bankT[:, 0:T], scalar=vec(l, 1, cc),
                                                         in1=sg[s][:], op0=ALU.mult, op1=ALU.mult),
                 reads=[bf("bT"), bf(f"sg{s}"), cb], writes=[bf("yretT")])

        def conv_cc(l, cc):
            s = cc % 2
            cb_ = bf(f"cxp{s}")
            bank, bb = proj(uT, [bf("uT")])
            P.op("act", lambda e, bank=bank: e.copy(out=cxs[:], in_=bank[:, 0:T]), reads=[bb], writes=[bf("cxs")])
            bank, bb = proj(uT, [bf("uT")])
            P.op("dve", lambda e, bank=bank: e.tensor_tensor(out=cxp[s][:, 2:T + 2], in0=bank[:, 0:T], in1=cxs[:], op=ALU.mult),
                 reads=[bb, bf("cxs")], writes=[cb_])
            P.op("dve", lambda e: e.tensor_copy(out=cxp[s][:, 0:2], in_=halo[:, cc, :]), reads=[bf("halo")], writes=[cb_])
            P.op("dve", lambda e: e.tensor_copy(out=halo[:, cc, :], in_=cxp[s][:, T:T + 2]), reads=[cb_], writes=[bf("halo")])
            P.op("dve", lambda e: e.tensor_scalar(out=zt[:], in0=cxp[s][:, 2:T + 2], scalar1=vec(l, 4, cc), scalar2=vec(l, 5, cc),
                                                  op0=ALU.mult, op1=ALU.add),
                 reads=[cb_, cb], writes=[bf("zt")])
            P.op("dve", lambda e: e.scalar_tensor_tensor(out=zt[:], in0=cxp[s][:, 1:T + 1], scalar=vec(l, 3, cc), in1=zt[:],
                                                         op0=ALU.mult, op1=ALU.add),
                 reads=[cb_, cb, bf("zt")], writes=[bf("zt")])
            P.op("dve", lambda e: e.scalar_tensor_tensor(out=zt[:], in0=cxp[s][:, 0:T], scalar=vec(l, 2, cc), in1=zt[:],
                                                         op0=ALU.mult, op1=ALU.add),
                 reads=[cb_, cb, bf("zt")], writes=[bf("zt")])
            bank, bb = proj(uT, [bf("uT")])
            P.op("dve", lambda e, bank=bank: e.tensor_tensor(out=zt[:], in0=bank[:, 0:T], in1=zt[:], op=ALU.mult),
                 reads=[bb, bf("zt")], writes=[bf("zt")])
            bank, bb = proj(uT, [bf("uT")])
            P.op("act", lambda e, bank=bank: e.activation(out=sgc[:], in_=bank[:, 0:T], func=AF.Silu), reads=[bb], writes=[bf("sgc")])
            P.op("dve", lambda e: e.tensor_tensor(out=yconvT[:, cc, :], in0=zt[:], in1=sgc[:], op=ALU.mult),
                 reads=[bf("zt"), bf("sgc")], writes=[bf("yconvT")])

        def merge_dd(dd):
            bank, bb = proj(uT, [bf("uT")])
            P.op("act", lambda e, bank=bank: e.activation(out=smr[:], in_=bank[:, 0:T], func=AF.Sigmoid), reads=[bb], writes=[bf("smr")])
            bank, bb = proj(uT, [bf("uT")])
            P.op("act", lambda e, bank=bank: e.activation(out=smc[:], in_=bank[:, 0:T], func=AF.Sigmoid), reads=[bb], writes=[bf("smc")])
            bank, bb = proj(yretT, [bf("yretT")])
            P.op("dve", lambda e, bank=bank: e.tensor_tensor(out=t1[:], in0=bank[:, 0:T], in1=smr[:], op=ALU.mult),
                 reads=[bb, bf("smr")], writes=[bf("t1")])
            bank, bb = proj(yconvT, [bf("yconvT")])
            P.op("dve", lambda e, bank=bank: e.tensor_tensor(out=t2[:], in0=bank[:, 0:T], in1=smc[:], op=ALU.mult),
                 reads=[bb, bf("smc")], writes=[bf("t2")])
            P.op("dve", lambda e: e.tensor_tensor(out=mergedT[:, dd, :], in0=t1[:], in1=t2[:], op=ALU.add),
                 reads=[bf("t1"), bf("t2")], writes=[bf("mergedT")])

        for i in range(NTILE):
            l, t = tiles[i]
            b = i % 2
            hTi, hTb = hT[b], hT_b[b]
            if t == 0:
                P.op("dve", lambda e: e.memset(halo[:].rearrange("p k j -> p (k j)"), 0.0), writes=[bf("halo")])
                P.op("dve", lambda e: e.memset(state_f[:].rearrange("p h v -> p (h v)"), 0.0), writes=[bf(f"stf{h}") for h in range(H)])
                P.op("dve", lambda e: e.memset(state_bf[:].rearrange("p h v -> p (h v)"), 0.0), writes=[bf(f"stb{h}") for h in range(H)])
            if i == 0:
                load_h(0)
                load_rot(0)
                make_u(0)
            for h in range(H):
                hb = h % 2
                bank, bb = proj(uT, [bf("uT")])
                rot_copy(bank, bb, 0)
                if h >= 1:
                    v_trans(h - 1)
                    ret_S(h - 1)
                bank, bb = proj(uT, [bf("uT")])
                rot_copy(bank, bb, 1)
                rot_finish(0, qrot[hb], bf(f"qrot{hb}"))
                if h >= 1:
                    ret_chunk(h - 1, 0)
                proj_v(h, 0)
                rot_finish(1, krot[hb], bf(f"krot{hb}"))
                if h >= 1:
                    ret_chunk(h - 1, 1)
                proj_v(h, 1)
                k_trans(h)
                if h >= 1:
                    ret_chunk(h - 1, 2)
            v_trans(H - 1)
            if i + 1 < NTILE:
                load_h(i + 1)
                load_rot(i + 1)
            yret_cc(l, 0)
            ret_S(H - 1)
            yret_cc(l, 1)
            ret_chunk(H - 1, 0)
            yret_cc(l, 2)
            ret_chunk(H - 1, 1)
            yret_cc(l, 3)
            ret_chunk(H - 1, 2)
            for cc in range(4, KC):
                yret_cc(l, cc)
            for cc in range(KC):
                conv_cc(l, cc)
            for dd in range(KC):
                merge_dd(dd)
            if i + 1 < NTILE:
                make_u(i + 1)
            for dd in range(KC):
                bank, bb = proj(mergedT, [bf("mergedT")])
                P.op("dve", lambda e, bank=bank, dd=dd, hTi=hTi: e.tensor_tensor(out=hTi[:, dd, :], in0=hTi[:, dd, :], in1=bank[:, 0:T], op=ALU.add),
                     reads=[bb, hTb[dd]], writes=[hTb[dd]])
            if l == nlayers - 1:
                rms_stats(b)
                for kc in range(KC):
                    P.op("dve", lambda e, kc=kc, hTi=hTi: e.scalar_tensor_tensor(out=hTi[:, kc, :], in0=hTi[:, kc, :], scalar=fvec(kc),
                                                                               in1=rstd[:], op0=ALU.mult, op1=ALU.mult),
                         reads=[hTb[kc], bf("rstd"), cb], writes=[hTb[kc]])
            dst, dname = dst_of(l)
            P.op("gq", lambda e, t=t, dst=dst, hTi=hTi: e.dma_start(out=dst[t * 128:(t + 1) * 128, :], in_=hTi[:].rearrange("p k j -> p (k j)")),
                 reads=hTb, writes=[bf(f"{dname}{t}")], chan=f"hT{b}")

        assert wstate["used"] == len(wstate["seq"]), (wstate["used"], len(wstate["seq"]))
        P.finalize()

        plan = P.sem_plan()
        sems = {key: [es.enter_context(nc.semaphore(f"s_{key[0]}_{key[1]}_{i}")) for i in range(n)] for key, n in plan.items()}
        block = es.enter_context(nc.Block())

        @block.tensor
        def _(e):
            P.emit("pe", e, sems)

        @block.scalar
        def _(e):
            P.emit("act", e, sems)

        @block.vector
        def _(e):
            P.emit("dve", e, sems)

        @block.gpsimd
        def _(e):
            P.emit("gq", e, sems)

        @block.sync
        def _(e):
            P.emit("sp", e, sems)
            P.final_waits(e, sems)
    return nc


def _piece_chunk_order():
    q0, k0, v0, gr0, cx0, cp0, co0, gc0, mr0, mc0 = [x // 128 for x in
                                                      (0, 1024, 2048, 4096, 6144, 8192, 10240, 12288, 14336, 16384)]
    wb0_0 = 18432 // 128
    wb1_0 = wb0_0 + 16
    wo_0 = wb1_0 + 16
    order = []
    for h in range(H):
        order += [q0 + h, k0 + h, v0 + 2 * h, v0 + 2 * h + 1]
    for cc in range(KC):
        order += [gr0 + cc]
    for cc in range(KC):
        order += [cx0 + cc, cp0 + cc, co0 + cc, gc0 + cc]
    for dd in range(KC):
        order += [mr0 + dd, mc0 + dd, wb0_0 + dd, wb1_0 + dd]
    for dd in range(KC):
        order += [wo_0 + dd]
    assert len(order) == NPIECE
    return order


def _layer_weights(w_in, w_branch, w_out, l):
    order = np.array(_piece_chunk_order())
    wall = np.concatenate([w_in[l], w_branch[l, 0], w_branch[l, 1], w_out[l]], axis=1)
    w4 = wall.reshape(KC, 128, NPIECE, 128)[:, :, order, :]
    wp = np.ascontiguousarray(w4.transpose(2, 1, 0, 3)).reshape(NPIECE * 128, 2048)
    return wp


def _tile_layout(hloc):
    a = hloc.reshape(NT, T, KC, 128).transpose(0, 3, 2, 1)
    return np.ascontiguousarray(a).reshape(NT * 128, KC * T)


def _untile(ht):
    a = ht.reshape(NT, 128, KC, T).transpose(0, 3, 2, 1)
    return np.ascontiguousarray(a).reshape(NT * T, D)


def _consts(core, norm_g, gn_g, conv_w, conv_b, final_norm_g):
    b, s = (core, 0) if SOLO else divmod(core, 4)
    c = np.zeros((128, NCONST), np.float32)
    idx = np.arange(128, dtype=np.float64)
    scale = 128.0 ** -0.5
    for h in range(H):
        g = GAMMA[h]
        diff = idx[None, :] - idx[:, None]
        m = np.where(diff >= 0, g ** np.maximum(diff, 0.0), 0.0) * scale
        c[:, C_MASK + h * 128:C_MASK + (h + 1) * 128] = m
        c[:, C_QDEC + h * 128:C_QDEC + (h + 1) * 128] = (g ** (idx + 1.0))[None, :] * scale
        c[:, C_KDEC + h] = g ** (127.0 - idx)
    c[:, C_IDENT:C_IDENT + 128] = np.eye(128)
    pm = np.zeros((128, 128))
    for m_ in range(128):
        pm[(m_ + 64) % 128, m_] = 1.0
    c[:, C_PERM:C_PERM + 128] = pm
    c[:, C_ONES:C_ONES + 128] = 1.0
    for l in range(DEPTH):
        base = C_VECS + l * 96
        c[:, base + 0:base + 16] = norm_g[l].reshape(KC, 128).T
        c[:, base + 16:base + 32] = gn_g[l].reshape(KC, 128).T
        for k in range(3):
            c[:, base + 32 + 16 * k:base + 48 + 16 * k] = conv_w[l, k].reshape(KC, 128).T
        c[:, base + 80:base + 96] = conv_b[l].reshape(KC, 128).T
    c[:, C_VECS + DEPTH * 96:C_VECS + DEPTH * 96 + 16] = final_norm_g.reshape(KC, 128).T
    for j in range(0 if SOLO else NCORE):
        bj, sj = divmod(j, 4)
        if bj == b and sj < s:
            for h in range(H):
                c[:, C_COEF + j * 8 + h] = GAMMA[h] ** (4096.0 * (s - 1 - sj))
        if bj == b and sj == s - 1:
            c[:, C_HCOEF + j] = 1.0
    c[:, C_EPS] = EPS
    c[:, C_SELF + core] = 1.0
    return c


def _rot_tables(core):
    b, s = (core, 0) if SOLO else divmod(core, 4)
    tau = np.arange(NT * T)
    pos = np.where(tau >= 128, 16 + s * 4096 + (tau - 128), np.maximum(tau - 112, 0)).astype(np.float32)
    inv_freq = (np.float32(10000.0) ** (-(np.arange(64, dtype=np.float32) / np.float32(64)))).astype(np.float32)
    ang = (pos[:, None] * inv_freq[None, :]).astype(np.float32)
    cos = np.cos(ang).astype(np.float32)
    sin = np.sin(ang).astype(np.float32)
    cosT = np.concatenate([cos, cos], axis=1).T
    sinT = np.concatenate([-sin, sin], axis=1).T
    r = np.stack([cosT.reshape(128, NT, T), sinT.reshape(128, NT, T)], axis=2)
    return np.ascontiguousarray(r.transpose(1, 0, 2, 3)).reshape(NT * 128, 2 * T)


_PROG_CACHE = {}


def _prog():
    if "p" not in _PROG_CACHE:
        _PROG_CACHE["p"] = build_program()
    return _PROG_CACHE["p"]


def kernel(x, meta_tokens, norm_g, w_in, conv_w, conv_b, gn_g, w_branch, w_out, final_norm_g):
    x = np.asarray(x, np.float32)
    meta_tokens = np.asarray(meta_tokens, np.float32)
    norm_g = np.asarray(norm_g, np.float32)
    w_in = np.asarray(w_in, np.float32)
    conv_w = np.asarray(conv_w, np.float32)
    conv_b = np.asarray(conv_b, np.float32)
    gn_g = np.asarray(gn_g, np.float32)
    w_branch = np.asarray(w_branch, np.float32)
    w_out = np.asarray(w_out, np.float32)
    final_norm_g = np.asarray(final_norm_g, np.float32)

    cores = list(range(NCORE_USED))
    nreal = (NT * T - 128)
    hs = []
    for core in cores:
        b, s = (core, 0) if SOLO else divmod(core, 4)
        hloc = np.zeros((NT * T, D), np.float32)
        if s == 0:
            hloc[112:128] = meta_tokens
        hloc[128:] = x[b, s * nreal:(s + 1) * nreal]
        hs.append(_tile_layout(hloc))
    rots = [_rot_tables(c) for c in cores]
    csts = [_consts(c, norm_g, gn_g, conv_w, conv_b, final_norm_g) for c in cores]
    wall = np.empty((DEPTH * NPIECE * 128, 2048), np.float32)
    for l in range(DEPTH):
        wall[l * NPIECE * 128:(l + 1) * NPIECE * 128] = _layer_weights(w_in, w_branch, w_out, l)

    nc = _prog()
    res = run_bass_kernel_spmd(nc, [{"hin": hs[c], "rot": rots[c], "cst": csts[c], "wall": wall} for c in cores], core_ids=cores)
    out = np.zeros((2, 16384, D), np.float32)
    for core in cores:
        b, s = (core, 0) if SOLO else divmod(core, 4)
        out[b, s * nreal:(s + 1) * nreal] = _untile(res.results[core]["hout"])[128:]
    return out
```

```python
import numpy as np
import ml_dtypes
from contextlib import ExitStack
import concourse.bass as bass
import concourse.mybir as mybir
from concourse.bass_utils import run_bass_kernel_spmd

F32 = mybir.dt.float32
BF16 = mybir.dt.bfloat16
AF = mybir.ActivationFunctionType
ALU = mybir.AluOpType

D = 2048
KC = 16
T = 384
NCH = 3
SOLO = True
NT = 43 if SOLO else 11
NCORE_USED = 2 if SOLO else 8
H = 8
DEPTH = 4
NCORE = 8
EPS = 1e-6
NPIECE = 192
EW = 2048 + 32
NSLOT = 8
SAME_ENG_SYNC = True

C_MASK = 0
C_QDEC = C_MASK + 1024
C_KDEC = C_QDEC + 1024
C_IDENT = C_KDEC + 8
C_PERM = C_IDENT + 128
C_ONES = C_PERM + 128
C_VECS = C_ONES + 128
C_COEF = C_VECS + DEPTH * 96 + 16
C_HCOEF = C_COEF + 64
C_EPS = C_HCOEF + 8
C_SELF = C_EPS + 1
NCONST = C_SELF + 8

GAMMA = [1.0 - 2.0 ** (-5.0 - h) for h in range(H)]


class Buf:
    __slots__ = ("name", "w", "r")

    def __init__(self, name):
        self.name = name
        self.w = None
        self.r = []


class Op:
    __slots__ = ("eng", "fn", "deps", "token", "signal", "seq", "is_dma", "chan", "waits", "sigval", "inc")


class Prog:
    ENGS = ("pe", "act", "dve", "sp", "gq")

    def __init__(self):
        self.ops = {e: [] for e in self.ENGS}
        self.chan_count = {}
        self.chan_last = {}

    def op(self, eng, fn, reads=(), writes=(), chan=None, inc=16):
        o = Op()
        o.inc = inc
        o.eng = eng
        o.fn = fn
        o.is_dma = chan is not None
        o.chan = chan
        o.seq = len(self.ops[eng])
        o.signal = False
        o.token = None
        o.sigval = None
        deps = []
        for b in reads:
            if b.w is not None:
                deps.append((b.w, 0))
        for b in writes:
            if b.w is not None:
                deps.append((b.w, 1))
            for r in b.r:
                deps.append((r, 2))
        if o.is_dma:
            k = self.chan_count.get(chan, 0) + 1
            self.chan_count[chan] = k
            o.token = (chan, k)
            prev = self.chan_last.get(chan)
            if prev is not None:
                deps.append((prev, 0))
            self.chan_last[chan] = o
        o.deps = deps
        for b in reads:
            b.r.append(o)
        for b in writes:
            b.w = o
            b.r = []
        self.ops[eng].append(o)
        return o

    def finalize(self):
        for eng in self.ENGS:
            waited = {}
            for o in self.ops[eng]:
                need = {}
                for (p, kind) in o.deps:
                    if p is o:
                        continue
                    if p.is_dma:
                        key = ("c", p.chan)
                        val = p.token[1]
                    else:
                        if p.eng == eng:
                            if eng == "pe" or not SAME_ENG_SYNC or kind == 2:
                                continue
                        key = ("e", p.eng)
                        val = p.seq
                    if waited.get(key, -1) >= val:
                        continue
                    cur = need.get(key)
                    if cur is None or cur[0] < val:
                        need[key] = (val, p)
                o.waits = []
                for key, (val, p) in need.items():
                    waited[key] = val
                    if not p.is_dma:
                        p.signal = True
                    o.waits.append(p)
        for eng in ("pe", "act", "dve"):
            cnt = 0
            for o in self.ops[eng]:
                if o.signal:
                    cnt += 1
                    o.sigval = cnt

    LIM = 16384

    def sem_plan(self):
        plan = {}
        for eng in ("pe", "act", "dve"):
            n = max([o.sigval or 0 for o in self.ops[eng]] + [0])
            plan[("e", eng)] = max(1, -(-n // self.LIM))
        for c, k in self.chan_count.items():
            plan[("c", c)] = max(1, -(-k // (self.LIM // 16)))
        return plan

    def _loc(self, ordinal, per, inc):
        return (ordinal - 1) // per, ((ordinal - 1) % per + 1) * inc

    def emit(self, eng, e, sems):
        per_d = self.LIM // 16
        for o in self.ops[eng]:
            for p in o.waits:
                if p.is_dma:
                    b, v = self._loc(p.token[1], per_d, p.inc)
                    e.wait_ge(sems[("c", p.chan)][b], v)
                else:
                    b, v = self._loc(p.sigval, self.LIM, 1)
                    e.wait_ge(sems[("e", p.eng)][b], v)
            ins = o.fn(e)
            if o.is_dma:
                b, v = self._loc(o.token[1], per_d, o.inc)
                ins.then_inc(sems[("c", o.chan)][b], o.inc)
            elif o.signal:
                b, v = self._loc(o.sigval, self.LIM, 1)
                ins.then_inc(sems[("e", o.eng)][b], 1)

    def final_waits(self, e, sems):
        per_d = self.LIM // 16
        for c, o in self.chan_last.items():
            b, v = self._loc(o.token[1], per_d, o.inc)
            e.wait_ge(sems[("c", c)][b], v)


def build_program(nlayers=DEPTH):
    nc = bass.Bass("TRN2", target_bir_lowering=False)
    P = Prog()
    hin = nc.dram_tensor("hin", [NT * 128, KC * T], F32, kind="ExternalInput").ap()
    rot = nc.dram_tensor("rot", [NT * 128, 2 * T], F32, kind="ExternalInput").ap()
    cst = nc.dram_tensor("cst", [128, NCONST], F32, kind="ExternalInput").ap()
    wall = nc.dram_tensor("wall", [DEPTH * NPIECE * 128, 2048], F32, kind="ExternalInput").ap()
    hout = nc.dram_tensor("hout", [NT * 128, KC * T], F32, kind="ExternalOutput").ap()
    hbuf = nc.dram_tensor("hbuf", [NT * 128, KC * T], F32).ap()
    wbfs = [nc.dram_tensor(f"wbf{l}", [NPIECE * 128, 2048], BF16).ap() for l in range(DEPTH)]

    es = ExitStack()
    with es:
        def sb(name, shape, dt):
            return es.enter_context(nc.sbuf_tensor(name, shape, dt))

        hT = [sb(f"hT{i}", [128, KC, T], F32) for i in range(2)]
        uT = sb("uT", [128, KC, T], BF16)
        sq = [sb(f"sq{i}", [128, T], BF16) for i in range(2)]
        rstd = sb("rstd", [128, T], F32)
        rott = sb("rott", [128, 2, T], F32)
        xraw = [sb(f"xraw{i}", [128, T], BF16) for i in range(2)]
        t1 = sb("t1", [128, T], F32)
        t2 = sb("t2", [128, T], F32)
        qrot = [sb(f"qrot{i}", [128, T], BF16) for i in range(2)]
        krot = [sb(f"krot{i}", [128, T], BF16) for i in range(2)]
        ktok = [sb(f"ktok{i}", [128, NCH, 128], BF16) for i in range(2)]
        vT = [sb(f"vT{i}", [128, T], BF16) for i in range(2)]
        vtok = [sb(f"vtok{i}", [128, NCH, 256], BF16) for i in range(2)]
        Pm = [sb(f"Pm{i}", [128, 128], BF16) for i in range(NCH)]
        qd = [sb(f"qd{i}", [128, 128], BF16) for i in range(NCH)]
        state_f = sb("state_f", [128, H, 256], F32)
        state_bf = sb("state_bf", [128, H, 256], BF16)
        yn = sb("yn", [128, NCH, D], BF16)
        sg = [sb(f"sg{i}", [128, T], BF16) for i in range(2)]
        yretT = sb("yretT", [128, KC, T], BF16)
        yconvT = sb("yconvT", [128, KC, T], BF16)
        mergedT = sb("mergedT", [128, KC, T], BF16)
        cxs = sb("cxs", [128, T], F32)
        cxp = [sb(f"cxp{i}", [128, T + 2], F32) for i in range(2)]
        zt = sb("zt", [128, T], F32)
        sgc = sb("sgc", [128, T], F32)
        smr = sb("smr", [128, T], F32)
        smc = sb("smc", [128, T], F32)
        halo = sb("halo", [128, KC, 2], F32)
        st6 = sb("st6", [128, 6], F32)
        mv = sb("mv", [128, 2], F32)
        grs = sb("grs", [128, 1], F32)
        wslot = [sb(f"wslot{i}", [128, 2048], BF16) for i in range(NSLOT)]
        cst_t = sb("cst_t", [128, NCONST], F32)
        ident = sb("ident", [128, 128], BF16)
        perm = sb("perm", [128, 128], BF16)
        ones = sb("ones", [128, 128], BF16)

        pbank = [es.enter_context(nc.psum_tensor(f"pb{i}", [128, 512], F32)) for i in range(3)]
        bankM = es.enter_context(nc.psum_tensor("bM", [128, 512], F32))
        bankS = es.enter_context(nc.psum_tensor("bS", [128, 512], F32))
        bankO = es.enter_context(nc.psum_tensor("bO", [128, 512], F32))
        bankK = es.enter_context(nc.psum_tensor("bK", [128, 512], F32))
        bankT = es.enter_context(nc.psum_tensor("bT", [128, 1024], BF16))

        B = {}

        def bf(name):
            if name not in B:
                B[name] = Buf(name)
            return B[name]

        hT_b = [[bf(f"hT{i}_{k}") for k in range(KC)] for i in range(2)]
        pb_b = [bf(f"pb{i}") for i in range(3)]
        ws_b = [bf(f"ws{i}") for i in range(NSLOT)]

        P.op("sp", lambda e: e.dma_start(out=cst_t[:], in_=cst), writes=[bf("cst")], chan="cst")
        P.op("dve", lambda e: e.tensor_copy(out=ident[:], in_=cst_t[:, C_IDENT:C_IDENT + 128]), reads=[bf("cst")], writes=[bf("ident")])
        P.op("dve", lambda e: e.tensor_copy(out=perm[:], in_=cst_t[:, C_PERM:C_PERM + 128]), reads=[bf("cst")], writes=[bf("perm")])
        P.op("dve", lambda e: e.tensor_copy(out=ones[:], in_=cst_t[:, C_ONES:C_ONES + 128]), reads=[bf("cst")], writes=[bf("ones")])
        cb = bf("cst")

        def vec(l, which, kc):
            c = C_VECS + l * 96 + which * 16 + kc
            return cst_t[:, c:c + 1]

        def fvec(kc):
            c = C_VECS + DEPTH * 96 + kc
            return cst_t[:, c:c + 1]

        epsc = cst_t[:, C_EPS:C_EPS + 1]

        def cast_piece(l_, i_):
            r0 = (l_ * NPIECE + i_) * 128
            P.op("gq", lambda e: e.dma_start(out=wbfs[l_][i_ * 128:(i_ + 1) * 128, :], in_=wall[r0:r0 + 128, :]),
                 writes=[bf(f"wbf{l_}_{i_}")], chan="cv")

        for i_ in range(NPIECE):
            cast_piece(0, i_)

        seq = []
        for l_ in range(nlayers):
            for t in range(NT):
                for i in range(NPIECE):
                    seq.append((wbfs[l_][i * 128:(i + 1) * 128, :], bf(f"wbf{l_}_{i}")))
        wstate = {"seq": seq, "issued": 0, "used": 0}

        def w_issue_upto(n):
            while wstate["issued"] < min(n, len(wstate["seq"])):
                i = wstate["issued"]
                src, srcb = wstate["seq"][i]
                s = i % NSLOT
                P.op("sp", lambda e, s=s, src=src: e.dma_start(out=wslot[s][:], in_=src),
                     reads=[srcb], writes=[ws_b[s]], chan=f"ws{s}")
                wstate["issued"] += 1
                lcur, rem = divmod(i, NT * NPIECE)
                if lcur + 1 < nlayers and rem % NT == 0:
                    cast_piece(lcur + 1, rem // NT)

        def w_next():
            i = wstate["used"]
            w_issue_upto(i + NSLOT)
            wstate["used"] += 1
            s = i % NSLOT
            return wslot[s], ws_b[s]

        pstate = {"i": 0}

        def proj(rhs_tile, rhs_bufs):
            wt, wb = w_next()
            i = pstate["i"] % 3
            pstate["i"] += 1
            bank = pbank[i]

            def fn(e, wt=wt, bank=bank):
                for kc in range(KC):
                    ins = e.matmul(bank[:, 0:T], wt[:, kc * 128:(kc + 1) * 128], rhs_tile[:, kc, :],
                                   start=(kc == 0), stop=(kc == KC - 1))
                return ins
            P.op("pe", fn, reads=[wb] + rhs_bufs, writes=[pb_b[i]])
            return bank, pb_b[i]

        tiles = [(l, t) for l in range(nlayers) for t in range(NT)]
        NTILE = len(tiles)

        def src_of(l):
            return (hin, "hin") if l == 0 else (hbuf, "hbuf")

        def dst_of(l):
            return (hout, "hout") if l == nlayers - 1 else (hbuf, "hbuf")

        def load_h(i):
            l, t = tiles[i]
            src, name = src_of(l)
            b = i % 2
            P.op("gq", lambda e: e.dma_start(out=hT[b][:].rearrange("p k j -> p (k j)"), in_=src[t * 128:(t + 1) * 128, :]),
                 reads=[bf(f"{name}{t}")], writes=hT_b[b], chan=f"hT{b}")

        def load_rot(i):
            l, t = tiles[i]
            P.op("gq", lambda e: e.dma_start(out=rott[:].rearrange("p k j -> p (k j)"), in_=rot[t * 128:(t + 1) * 128, :]),
                 writes=[bf("rott")], chan="rott")

        def rms_stats(b):
            hTi, hTb = hT[b], hT_b[b]
            for kc in range(KC):
                s = kc % 2
                P.op("act", lambda e, kc=kc, s=s: e.activation(out=sq[s][:], in_=hTi[:, kc, :], func=AF.Square),
                     reads=[hTb[kc]], writes=[bf(f"sq{s}")])
                P.op("pe", lambda e, kc=kc, s=s: e.matmul(bankM[:, 0:T], ones[:], sq[s][:], start=(kc == 0), stop=(kc == KC - 1)),
                     reads=[bf(f"sq{s}"), bf("ones")], writes=[bf("bM")])
            P.op("act", lambda e: e.activation(out=rstd[:], in_=bankM[:, 0:T], func=AF.Sqrt, scale=1.0 / D, bias=epsc),
                 reads=[bf("bM"), cb], writes=[bf("rstd")])
            P.op("dve", lambda e: e.reciprocal(out=rstd[:], in_=rstd[:]), reads=[bf("rstd")], writes=[bf("rstd")])

        def make_u(i):
            l, t = tiles[i]
            b = i % 2
            hTi, hTb = hT[b], hT_b[b]
            rms_stats(b)
            for kc in range(KC):
                P.op("dve", lambda e, kc=kc: e.scalar_tensor_tensor(out=uT[:, kc, :], in0=hTi[:, kc, :], scalar=vec(l, 0, kc),
                                                                    in1=rstd[:], op0=ALU.mult, op1=ALU.mult),
                     reads=[hTb[kc], bf("rstd"), cb], writes=[bf("uT")])

        def rot_copy(bank, bankb, i2):
            xr = xraw[i2]
            P.op("act", lambda e: e.copy(out=xr[:], in_=bank[:, 0:T]), reads=[bankb], writes=[bf(f"xraw{i2}")])

        def rot_finish(i2, dst, dstb):
            xr = xraw[i2]
            xb = bf(f"xraw{i2}")
            P.op("pe", lambda e: e.matmul(bankM[:, 0:T], perm[:], xr[:], start=True, stop=True),
                 reads=[xb, bf("perm")], writes=[bf("bM")])
            P.op("dve", lambda e: e.tensor_tensor(out=t1[:], in0=xr[:], in1=rott[:, 0, :], op=ALU.mult),
                 reads=[xb, bf("rott")], writes=[bf("t1")])
            P.op("dve", lambda e: e.tensor_tensor(out=t2[:], in0=bankM[:, 0:T], in1=rott[:, 1, :], op=ALU.mult),
                 reads=[bf("bM"), bf("rott")], writes=[bf("t2")])
            P.op("dve", lambda e: e.tensor_tensor(out=dst[:], in0=t1[:], in1=t2[:], op=ALU.add),
                 reads=[bf("t1"), bf("t2")], writes=[dstb])

        def k_trans(h):
            hb = h % 2

            def trk(e):
                for c in range(NCH):
                    ins = e.transpose(bankT[:, c * 128:(c + 1) * 128], krot[hb][:, c * 128:(c + 1) * 128], ident[:])
                return ins
            P.op("pe", trk, reads=[bf(f"krot{hb}"), bf("ident")], writes=[bf("bT")])
            P.op("dve", lambda e: e.tensor_scalar_mul(out=ktok[hb][:].rearrange("p c d -> p (c d)"), in0=bankT[:, 0:NCH * 128],
                                                      scalar1=cst_t[:, C_KDEC + h:C_KDEC + h + 1]),
                 reads=[bf("bT"), cb], writes=[bf(f"ktok{hb}")])

        def proj_v(h, i):
            bank, bb = proj(uT, [bf("uT")])
            P.op("act", lambda e: e.copy(out=vT[i][:], in_=bank[:, 0:T]), reads=[bb], writes=[bf(f"vT{i}")])

        def v_trans(h):
            hb = h % 2

            def trv(e):
                for c in range(NCH):
                    for i in range(2):
                        o0 = (c * 2 + i) * 128
                        ins = e.transpose(bankT[:, o0:o0 + 128], vT[i][:, c * 128:(c + 1) * 128], ident[:])
                return ins
            P.op("pe", trv, reads=[bf("vT0"), bf("vT1"), bf("ident")], writes=[bf("bT")])
            P.op("act", lambda e: e.copy(out=vtok[hb][:].rearrange("p c d -> p (c d)"), in_=bankT[:, 0:NCH * 256]),
                 reads=[bf("bT")], writes=[bf(f"vtok{hb}")])

        ostate = {"i": 0}

        def ret_S(h):
            hb = h % 2

            def fs(e):
                for c in range(NCH):
                    cs = slice(c * 128, (c + 1) * 128)
                    ins = e.matmul(bankS[:, cs], krot[hb][:, cs], qrot[hb][:, cs], start=True, stop=True)
                return ins
            P.op("pe", fs, reads=[bf(f"krot{hb}"), bf(f"qrot{hb}")], writes=[bf("bS")])
            for c in range(NCH):
                P.op("dve", lambda e, c=c: e.tensor_tensor(out=Pm[c][:], in0=bankS[:, c * 128:(c + 1) * 128],
                                                           in1=cst_t[:, C_MASK + h * 128:C_MASK + (h + 1) * 128], op=ALU.mult),
                     reads=[bf("bS"), cb], writes=[bf(f"Pm{c}")])
                P.op("dve", lambda e, c=c: e.tensor_tensor(out=qd[c][:], in0=qrot[hb][:, c * 128:(c + 1) * 128],
                                                           in1=cst_t[:, C_QDEC + h * 128:C_QDEC + (h + 1) * 128], op=ALU.mult),
                     reads=[bf(f"qrot{hb}"), cb], writes=[bf(f"qd{c}")])

        def ret_chunk(h, c):
            hb = h % 2
            osl = slice(0, 256)
            bo, bk = bf("bO"), bf("bK")

            def fo(e):
                e.matmul(bankO[:, osl], Pm[c][:], vtok[hb][:, c, :], start=True, stop=False)
                return e.matmul(bankO[:, osl], qd[c][:], state_bf[:, h, :], start=False, stop=True)
            P.op("pe", fo, reads=[bf(f"Pm{c}"), bf(f"qd{c}"), bf(f"vtok{hb}"), bf(f"stb{h}")], writes=[bo])
            P.op("pe", lambda e: e.matmul(bankK[:, osl], ktok[hb][:, c, :], vtok[hb][:, c, :], start=True, stop=True),
                 reads=[bf(f"ktok{hb}"), bf(f"vtok{hb}")], writes=[bk])
            P.op("dve", lambda e: e.scalar_tensor_tensor(out=state_f[:, h, :], in0=state_f[:, h, :], scalar=float(GAMMA[h] ** 128),
                                                         in1=bankK[:, osl], op0=ALU.mult, op1=ALU.add),
                 reads=[bk, bf(f"stf{h}")], writes=[bf(f"stf{h}")])
            P.op("act", lambda e: e.copy(out=state_bf[:, h, :], in_=state_f[:, h, :]),
                 reads=[bf(f"stf{h}")], writes=[bf(f"stb{h}")])
            P.op("dve", lambda e: e.bn_stats(out=st6[:], in_=bankO[:, osl]), reads=[bo], writes=[bf("st6")])
            P.op("dve", lambda e: e.bn_aggr(out=mv[:], in_=st6[:]), reads=[bf("st6")], writes=[bf("mv")])
            P.op("act", lambda e: e.activation(out=grs[:], in_=mv[:, 1:2], func=AF.Sqrt, scale=1.0, bias=epsc),
                 reads=[bf("mv"), cb], writes=[bf("grs")])
            P.op("dve", lambda e: e.reciprocal(out=grs[:], in_=grs[:]), reads=[bf("grs")], writes=[bf("grs")])
            P.op("dve", lambda e: e.tensor_scalar(out=yn[:, c, h * 256:(h + 1) * 256], in0=bankO[:, osl],
                                                  scalar1=mv[:, 0:1], scalar2=grs[:, 0:1],
                                                  op0=ALU.subtract, op1=ALU.mult),
                 reads=[bo, bf("mv"), bf("grs")], writes=[bf("yn")])

        def yret_cc(l, cc):
            s = cc % 2
            bank, bb = proj(uT, [bf("uT")])
            P.op("act", lambda e: e.activation(out=sg[s][:], in_=bank[:, 0:T], func=AF.Silu), reads=[bb], writes=[bf(f"sg{s}")])

            def try_(e):
                for c in range(NCH):
                    ins = e.transpose(bankT[:, c * 128:(c + 1) * 128], yn[:, c, cc * 128:(cc + 1) * 128], ident[:])
                return ins
            P.op("pe", try_, reads=[bf("yn"), bf("ident")], writes=[bf("bT")])
            P.op("dve", lambda e: e.scalar_tensor_tensor(out=yretT[:, cc, :], in0=bankT[:, 0:T], scalar=vec(l, 1, cc),
                                                         in1=sg[s][:], op0=ALU.mult, op1=ALU.mult),
                 reads=[bf("bT"), bf(f"sg{s}"), cb], writes=[bf("yretT")])

        def conv_cc(l, cc):
            s = cc % 2
            cb_ = bf(f"cxp{s}")
            bank, bb = proj(uT, [bf("uT")])
            P.op("act", lambda e, bank=bank: e.copy(out=cxs[:], in_=bank[:, 0:T]), reads=[bb], writes=[bf("cxs")])
            bank, bb = proj(uT, [bf("uT")])
            P.op("dve", lambda e, bank=bank: e.tensor_tensor(out=cxp[s][:, 2:T + 2], in0=bank[:, 0:T], in1=cxs[:], op=ALU.mult),
                 reads=[bb, bf("cxs")], writes=[cb_])
            P.op("dve", lambda e: e.tensor_copy(out=cxp[s][:, 0:2], in_=halo[:, cc, :]), reads=[bf("halo")], writes=[cb_])
            P.op("dve", lambda e: e.tensor_copy(out=halo[:, cc, :], in_=cxp[s][:, T:T + 2]), reads=[cb_], writes=[bf("halo")])
            P.op("dve", lambda e: e.tensor_scalar(out=zt[:], in0=cxp[s][:, 2:T + 2], scalar1=vec(l, 4, cc), scalar2=vec(l, 5, cc),
                                                  op0=ALU.mult, op1=ALU.add),
                 reads=[cb_, cb], writes=[bf("zt")])
            P.op("dve", lambda e: e.scalar_tensor_tensor(out=zt[:], in0=cxp[s][:, 1:T + 1], scalar=vec(l, 3, cc), in1=zt[:],
                                                         op0=ALU.mult, op1=ALU.add),
                 reads=[cb_, cb, bf("zt")], writes=[bf("zt")])
            P.op("dve", lambda e: e.scalar_tensor_tensor(out=zt[:], in0=cxp[s][:, 0:T], scalar=vec(l, 2, cc), in1=zt[:],
                                                         op0=ALU.mult, op1=ALU.add),
                 reads=[cb_, cb, bf("zt")], writes=[bf("zt")])
            bank, bb = proj(uT, [bf("uT")])
            P.op("dve", lambda e, bank=bank: e.tensor_tensor(out=zt[:], in0=bank[:, 0:T], in1=zt[:], op=ALU.mult),
                 reads=[bb, bf("zt")], writes=[bf("zt")])
            bank, bb = proj(uT, [bf("uT")])
            P.op("act", lambda e, bank=bank: e.activation(out=sgc[:], in_=bank[:, 0:T], func=AF.Silu), reads=[bb], writes=[bf("sgc")])
            P.op("dve", lambda e: e.tensor_tensor(out=yconvT[:, cc, :], in0=zt[:], in1=sgc[:], op=ALU.mult),
                 reads=[bf("zt"), bf("sgc")], writes=[bf("yconvT")])

        def merge_dd(dd):
            bank, bb = proj(uT, [bf("uT")])
            P.op("act", lambda e, bank=bank: e.activation(out=smr[:], in_=bank[:, 0:T], func=AF.Sigmoid), reads=[bb], writes=[bf("smr")])
            bank, bb = proj(uT, [bf("uT")])
            P.op("act", lambda e, bank=bank: e.activation(out=smc[:], in_=bank[:, 0:T], func=AF.Sigmoid), reads=[bb], writes=[bf("smc")])
            bank, bb = proj(yretT, [bf("yretT")])
            P.op("dve", lambda e, bank=bank: e.tensor_tensor(out=t1[:], in0=bank[:, 0:T], in1=smr[:], op=ALU.mult),
                 reads=[bb, bf("smr")], writes=[bf("t1")])
            bank, bb = proj(yconvT, [bf("yconvT")])
            P.op("dve", lambda e, bank=bank: e.tensor_tensor(out=t2[:], in0=bank[:, 0:T], in1=smc[:], op=ALU.mult),
                 reads=[bb, bf("smc")], writes=[bf("t2")])
            P.op("dve", lambda e: e.tensor_tensor(out=mergedT[:, dd, :], in0=t1[:], in1=t2[:], op=ALU.add),
                 reads=[bf("t1"), bf("t2")], writes=[bf("mergedT")])

        for i in range(NTILE):
            l, t = tiles[i]
            b = i % 2
            hTi, hTb = hT[b], hT_b[b]
            if t == 0:
                P.op("dve", lambda e: e.memset(halo[:].rearrange("p k j -> p (k j)"), 0.0), writes=[bf("halo")])
                P.op("dve", lambda e: e.memset(state_f[:].rearrange("p h v -> p (h v)"), 0.0), writes=[bf(f"stf{h}") for h in range(H)])
                P.op("dve", lambda e: e.memset(state_bf[:].rearrange("p h v -> p (h v)"), 0.0), writes=[bf(f"stb{h}") for h in range(H)])
            if i == 0:
                load_h(0)
                load_rot(0)
                make_u(0)
            for h in range(H):
                hb = h % 2
                bank, bb = proj(uT, [bf("uT")])
                rot_copy(bank, bb, 0)
                if h >= 1:
                    v_trans(h - 1)
                    ret_S(h - 1)
                bank, bb = proj(uT, [bf("uT")])
                rot_copy(bank, bb, 1)
                rot_finish(0, qrot[hb], bf(f"qrot{hb}"))
                if h >= 1:
                    ret_chunk(h - 1, 0)
                proj_v(h, 0)
                rot_finish(1, krot[hb], bf(f"krot{hb}"))
                if h >= 1:
                    ret_chunk(h - 1, 1)
                proj_v(h, 1)
                k_trans(h)
                if h >= 1:
                    ret_chunk(h - 1, 2)
            v_trans(H - 1)
            if i + 1 < NTILE:
                load_h(i + 1)
                load_rot(i + 1)
            yret_cc(l, 0)
            ret_S(H - 1)
            yret_cc(l, 1)
            ret_chunk(H - 1, 0)
            yret_cc(l, 2)
            ret_chunk(H - 1, 1)
            yret_cc(l, 3)
            ret_chunk(H - 1, 2)
            for cc in range(4, KC):
                yret_cc(l, cc)
            for cc in range(KC):
                conv_cc(l, cc)
            for dd in range(KC):
                merge_dd(dd)
            if i + 1 < NTILE:
                make_u(i + 1)
            for dd in range(KC):
                bank, bb = proj(mergedT, [bf("mergedT")])
                P.op("dve", lambda e, bank=bank, dd=dd, hTi=hTi: e.tensor_tensor(out=hTi[:, dd, :], in0=hTi[:, dd, :], in1=bank[:, 0:T], op=ALU.add),
                     reads=[bb, hTb[dd]], writes=[hTb[dd]])
            if l == nlayers - 1:
                rms_stats(b)
                for kc in range(KC):
                    P.op("dve", lambda e, kc=kc, hTi=hTi: e.scalar_tensor_tensor(out=hTi[:, kc, :], in0=hTi[:, kc, :], scalar=fvec(kc),
                                                                               in1=rstd[:], op0=ALU.mult, op1=ALU.mult),
                         reads=[hTb[kc], bf("rstd"), cb], writes=[hTb[kc]])
            dst, dname = dst_of(l)
            P.op("gq", lambda e, t=t, dst=dst, hTi=hTi: e.dma_start(out=dst[t * 128:(t + 1) * 128, :], in_=hTi[:].rearrange("p k j -> p (k j)")),
                 reads=hTb, writes=[bf(f"{dname}{t}")], chan=f"hT{b}")

        assert wstate["used"] == len(wstate["seq"]), (wstate["used"], len(wstate["seq"]))
        P.finalize()

        plan = P.sem_plan()
        sems = {key: [es.enter_context(nc.semaphore(f"s_{key[0]}_{key[1]}_{i}")) for i in range(n)] for key, n in plan.items()}
        block = es.enter_context(nc.Block())

        @block.tensor
        def _(e):
            P.emit("pe", e, sems)

        @block.scalar
        def _(e):
            P.emit("act", e, sems)

        @block.vector
        def _(e):
            P.emit("dve", e, sems)

        @block.gpsimd
        def _(e):
            P.emit("gq", e, sems)

        @block.sync
        def _(e):
            P.emit("sp", e, sems)
            P.final_waits(e, sems)
    return nc


def _piece_chunk_order():
    q0, k0, v0, gr0, cx0, cp0, co0, gc0, mr0, mc0 = [x // 128 for x in
                                                      (0, 1024, 2048, 4096, 6144, 8192, 10240, 12288, 14336, 16384)]
    wb0_0 = 18432 // 128
    wb1_0 = wb0_0 + 16
    wo_0 = wb1_0 + 16
    order = []
    for h in range(H):
        order += [q0 + h, k0 + h, v0 + 2 * h, v0 + 2 * h + 1]
    for cc in range(KC):
        order += [gr0 + cc]
    for cc in range(KC):
        order += [cx0 + cc, cp0 + cc, co0 + cc, gc0 + cc]
    for dd in range(KC):
        order += [mr0 + dd, mc0 + dd, wb0_0 + dd, wb1_0 + dd]
    for dd in range(KC):
        order += [wo_0 + dd]
    assert len(order) == NPIECE
    return order


def _layer_weights(w_in, w_branch, w_out, l):
    order = np.array(_piece_chunk_order())
    wall = np.concatenate([w_in[l], w_branch[l, 0], w_branch[l, 1], w_out[l]], axis=1)
    w4 = wall.reshape(KC, 128, NPIECE, 128)[:, :, order, :]
    wp = np.ascontiguousarray(w4.transpose(2, 1, 0, 3)).reshape(NPIECE * 128, 2048)
    return wp


def _tile_layout(hloc):
    a = hloc.reshape(NT, T, KC, 128).transpose(0, 3, 2, 1)
    return np.ascontiguousarray(a).reshape(NT * 128, KC * T)


def _untile(ht):
    a = ht.reshape(NT, 128, KC, T).transpose(0, 3, 2, 1)
    return np.ascontiguousarray(a).reshape(NT * T, D)


def _consts(core, norm_g, gn_g, conv_w, conv_b, final_norm_g):
    b, s = (core, 0) if SOLO else divmod(core, 4)
    c = np.zeros((128, NCONST), np.float32)
    idx = np.arange(128, dtype=np.float64)
    scale = 128.0 ** -0.5
    for h in range(H):
        g = GAMMA[h]
        diff = idx[None, :] - idx[:, None]
        m = np.where(diff >= 0, g ** np.maximum(diff, 0.0), 0.0) * scale
        c[:, C_MASK + h * 128:C_MASK + (h + 1) * 128] = m
        c[:, C_QDEC + h * 128:C_QDEC + (h + 1) * 128] = (g ** (idx + 1.0))[None, :] * scale
        c[:, C_KDEC + h] = g ** (127.0 - idx)
    c[:, C_IDENT:C_IDENT + 128] = np.eye(128)
    pm = np.zeros((128, 128))
    for m_ in range(128):
        pm[(m_ + 64) % 128, m_] = 1.0
    c[:, C_PERM:C_PERM + 128] = pm
    c[:, C_ONES:C_ONES + 128] = 1.0
    for l in range(DEPTH):
        base = C_VECS + l * 96
        c[:, base + 0:base + 16] = norm_g[l].reshape(KC, 128).T
        c[:, base + 16:base + 32] = gn_g[l].reshape(KC, 128).T
        for k in range(3):
            c[:, base + 32 + 16 * k:base + 48 + 16 * k] = conv_w[l, k].reshape(KC, 128).T
        c[:, base + 80:base + 96] = conv_b[l].reshape(KC, 128).T
    c[:, C_VECS + DEPTH * 96:C_VECS + DEPTH * 96 + 16] = final_norm_g.reshape(KC, 128).T
    for j in range(0 if SOLO else NCORE):
        bj, sj = divmod(j, 4)
        if bj == b and sj < s:
            for h in range(H):
                c[:, C_COEF + j * 8 + h] = GAMMA[h] ** (4096.0 * (s - 1 - sj))
        if bj == b and sj == s - 1:
            c[:, C_HCOEF + j] = 1.0
    c[:, C_EPS] = EPS
    c[:, C_SELF + core] = 1.0
    return c


def _rot_tables(core):
    b, s = (core, 0) if SOLO else divmod(core, 4)
    tau = np.arange(NT * T)
    pos = np.where(tau >= 128, 16 + s * 4096 + (tau - 128), np.maximum(tau - 112, 0)).astype(np.float32)
    inv_freq = (np.float32(10000.0) ** (-(np.arange(64, dtype=np.float32) / np.float32(64)))).astype(np.float32)
    ang = (pos[:, None] * inv_freq[None, :]).astype(np.float32)
    cos = np.cos(ang).astype(np.float32)
    sin = np.sin(ang).astype(np.float32)
    cosT = np.concatenate([cos, cos], axis=1).T
    sinT = np.concatenate([-sin, sin], axis=1).T
    r = np.stack([cosT.reshape(128, NT, T), sinT.reshape(128, NT, T)], axis=2)
    return np.ascontiguousarray(r.transpose(1, 0, 2, 3)).reshape(NT * 128, 2 * T)


_PROG_CACHE = {}


def _prog():
    if "p" not in _PROG_CACHE:
        _PROG_CACHE["p"] = build_program()
    return _PROG_CACHE["p"]


def kernel(x, meta_tokens, norm_g, w_in, conv_w, conv_b, gn_g, w_branch, w_out, final_norm_g):
    x = np.asarray(x, np.float32)
    meta_tokens = np.asarray(meta_tokens, np.float32)
    norm_g = np.asarray(norm_g, np.float32)
    w_in = np.asarray(w_in, np.float32)
    conv_w = np.asarray(conv_w, np.float32)
    conv_b = np.asarray(conv_b, np.float32)
    gn_g = np.asarray(gn_g, np.float32)
    w_branch = np.asarray(w_branch, np.float32)
    w_out = np.asarray(w_out, np.float32)
    final_norm_g = np.asarray(final_norm_g, np.float32)

    cores = list(range(NCORE_USED))
    nreal = (NT * T - 128)
    hs = []
    for core in cores:
        b, s = (core, 0) if SOLO else divmod(core, 4)
        hloc = np.zeros((NT * T, D), np.float32)
        if s == 0:
            hloc[112:128] = meta_tokens
        hloc[128:] = x[b, s * nreal:(s + 1) * nreal]
        hs.append(_tile_layout(hloc))
    rots = [_rot_tables(c) for c in cores]
    csts = [_consts(c, norm_g, gn_g, conv_w, conv_b, final_norm_g) for c in cores]
    wall = np.empty((DEPTH * NPIECE * 128, 2048), np.float32)
    for l in range(DEPTH):
        wall[l * NPIECE * 128:(l + 1) * NPIECE * 128] = _layer_weights(w_in, w_branch, w_out, l)

    nc = _prog()
    res = run_bass_kernel_spmd(nc, [{"hin": hs[c], "rot": rots[c], "cst": csts[c], "wall": wall} for c in cores], core_ids=cores)
    out = np.zeros((2, 16384, D), np.float32)
    for core in cores:
        b, s = (core, 0) if SOLO else divmod(core, 4)
        out[b, s * nreal:(s + 1) * nreal] = _untile(res.results[core]["hout"])[128:]
    return out
```

```python
import numpy as np
import ml_dtypes
from contextlib import ExitStack
import concourse.bass as bass
import concourse.mybir as mybir
from concourse.bass_utils import run_bass_kernel_spmd

F32 = mybir.dt.float32
BF16 = mybir.dt.bfloat16
AF = mybir.ActivationFunctionType
ALU = mybir.AluOpType

D = 2048
KC = 16
T = 384
NCH = 3
SOLO = True
NT = 43 if SOLO else 11
NCORE_USED = 2 if SOLO else 8
H = 8
DEPTH = 4
NCORE = 8
EPS = 1e-6
NPIECE = 192
EW = 2048 + 32
NSLOT = 8
SAME_ENG_SYNC = True

C_MASK = 0
C_QDEC = C_MASK + 1024
C_KDEC = C_QDEC + 1024
C_IDENT = C_KDEC + 8
C_PERM = C_IDENT + 128
C_ONES = C_PERM + 128
C_VECS = C_ONES + 128
C_COEF = C_VECS + DEPTH * 96 + 16
C_HCOEF = C_COEF + 64
C_EPS = C_HCOEF + 8
C_SELF = C_EPS + 1
NCONST = C_SELF + 8

GAMMA = [1.0 - 2.0 ** (-5.0 - h) for h in range(H)]


class Buf:
    __slots__ = ("name", "w", "r")

    def __init__(self, name):
        self.name = name
        self.w = None
        self.r = []


class Op:
    __slots__ = ("eng", "fn", "deps", "token", "signal", "seq", "is_dma", "chan", "waits", "sigval", "inc")


class Prog:
    ENGS = ("pe", "act", "dve", "sp", "gq")

    def __init__(self):
        self.ops = {e: [] for e in self.ENGS}
        self.chan_count = {}
        self.chan_last = {}

    def op(self, eng, fn, reads=(), writes=(), chan=None, inc=16):
        o = Op()
        o.inc = inc
        o.eng = eng
        o.fn = fn
        o.is_dma = chan is not None
        o.chan = chan
        o.seq = len(self.ops[eng])
        o.signal = False
        o.token = None
        o.sigval = None
        deps = []
        for b in reads:
            if b.w is not None:
                deps.append((b.w, 0))
        for b in writes:
            if b.w is not None:
                deps.append((b.w, 1))
            for r in b.r:
                deps.append((r, 2))
        if o.is_dma:
            k = self.chan_count.get(chan, 0) + 1
            self.chan_count[chan] = k
            o.token = (chan, k)
            prev = self.chan_last.get(chan)
            if prev is not None:
                deps.append((prev, 0))
            self.chan_last[chan] = o
        o.deps = deps
        for b in reads:
            b.r.append(o)
        for b in writes:
            b.w = o
            b.r = []
        self.ops[eng].append(o)
        return o

    def finalize(self):
        for eng in self.ENGS:
            waited = {}
            for o in self.ops[eng]:
                need = {}
                for (p, kind) in o.deps:
                    if p is o:
                        continue
                    if p.is_dma:
                        key = ("c", p.chan)
                        val = p.token[1]
                    else:
                        if p.eng == eng:
                            if eng == "pe" or not SAME_ENG_SYNC or kind == 2:
                                continue
                        key = ("e", p.eng)
                        val = p.seq
                    if waited.get(key, -1) >= val:
                        continue
                    cur = need.get(key)
                    if cur is None or cur[0] < val:
                        need[key] = (val, p)
                o.waits = []
                for key, (val, p) in need.items():
                    waited[key] = val
                    if not p.is_dma:
                        p.signal = True
                    o.waits.append(p)
        for eng in ("pe", "act", "dve"):
            cnt = 0
            for o in self.ops[eng]:
                if o.signal:
                    cnt += 1
                    o.sigval = cnt

    LIM = 16384

    def sem_plan(self):
        plan = {}
        for eng in ("pe", "act", "dve"):
            n = max([o.sigval or 0 for o in self.ops[eng]] + [0])
            plan[("e", eng)] = max(1, -(-n // self.LIM))
        for c, k in self.chan_count.items():
            plan[("c", c)] = max(1, -(-k // (self.LIM // 16)))
        return plan

    def _loc(self, ordinal, per, inc):
        return (ordinal - 1) // per, ((ordinal - 1) % per + 1) * inc

    def emit(self, eng, e, sems):
        per_d = self.LIM // 16
        for o in self.ops[eng]:
            for p in o.waits:
                if p.is_dma:
                    b, v = self._loc(p.token[1], per_d, p.inc)
                    e.wait_ge(sems[("c", p.chan)][b], v)
                else:
                    b, v = self._loc(p.sigval, self.LIM, 1)
                    e.wait_ge(sems[("e", p.eng)][b], v)
            ins = o.fn(e)
            if o.is_dma:
                b, v = self._loc(o.token[1], per_d, o.inc)
                ins.then_inc(sems[("c", o.chan)][b], o.inc)
            elif o.signal:
                b, v = self._loc(o.sigval, self.LIM, 1)
                ins.then_inc(sems[("e", o.eng)][b], 1)

    def final_waits(self, e, sems):
        per_d = self.LIM // 16
        for c, o in self.chan_last.items():
            b, v = self._loc(o.token[1], per_d, o.inc)
            e.wait_ge(sems[("c", c)][b], v)


def build_program(nlayers=DEPTH):
    nc = bass.Bass("TRN2", target_bir_lowering=False)
    P = Prog()
    hin = nc.dram_tensor("hin", [NT * 128, KC * T], F32, kind="ExternalInput").ap()
    rot = nc.dram_tensor("rot", [NT * 128, 2 * T], F32, kind="ExternalInput").ap()
    cst = nc.dram_tensor("cst", [128, NCONST], F32, kind="ExternalInput").ap()
    wall = nc.dram_tensor("wall", [DEPTH * NPIECE * 128, 2048], F32, kind="ExternalInput").ap()
    hout = nc.dram_tensor("hout", [NT * 128, KC * T], F32, kind="ExternalOutput").ap()
    hbuf = nc.dram_tensor("hbuf", [NT * 128, KC * T], F32).ap()
    wbfs = [nc.dram_tensor(f"wbf{l}", [NPIECE * 128, 2048], BF16).ap() for l in range(DEPTH)]

    es = ExitStack()
    with es:
        def sb(name, shape, dt):
            return es.enter_context(nc.sbuf_tensor(name, shape, dt))

        hT = [sb(f"hT{i}", [128, KC, T], F32) for i in range(2)]
        uT = sb("uT", [128, KC, T], BF16)
        sq = [sb(f"sq{i}", [128, T], BF16) for i in range(2)]
        rstd = sb("rstd", [128, T], F32)
        rott = sb("rott", [128, 2, T], F32)
        xraw = [sb(f"xraw{i}", [128, T], BF16) for i in range(2)]
        t1 = sb("t1", [128, T], F32)
        t2 = sb("t2", [128, T], F32)
        qrot = [sb(f"qrot{i}", [128, T], BF16) for i in range(2)]
        krot = [sb(f"krot{i}", [128, T], BF16) for i in range(2)]
        ktok = [sb(f"ktok{i}", [128, NCH, 128], BF16) for i in range(2)]
        vT = [sb(f"vT{i}", [128, T], BF16) for i in range(2)]
        vtok = [sb(f"vtok{i}", [128, NCH, 256], BF16) for i in range(2)]
        Pm = [sb(f"Pm{i}", [128, 128], BF16) for i in range(NCH)]
        qd = [sb(f"qd{i}", [128, 128], BF16) for i in range(NCH)]
        state_f = sb("state_f", [128, H, 256], F32)
        state_bf = sb("state_bf", [128, H, 256], BF16)
        yn = sb("yn", [128, NCH, D], BF16)
        sg = [sb(f"sg{i}", [128, T], BF16) for i in range(2)]
        yretT = sb("yretT", [128, KC, T], BF16)
        yconvT = sb("yconvT", [128, KC, T], BF16)
        mergedT = sb("mergedT", [128, KC, T], BF16)
        cxs = sb("cxs", [128, T], F32)
        cxp = [sb(f"cxp{i}", [128, T + 2], F32) for i in range(2)]
        zt = sb("zt", [128, T], F32)
        sgc = sb("sgc", [128, T], F32)
        smr = sb("smr", [128, T], F32)
        smc = sb("smc", [128, T], F32)
        halo = sb("halo", [128, KC, 2], F32)
        st6 = sb("st6", [128, 6], F32)
        osb = sb("osb", [128, 256], F32)
        mv = sb("mv", [128, 2], F32)
        grs = sb("grs", [128, 1], F32)
        wslot = [sb(f"wslot{i}", [128, 2048], BF16) for i in range(NSLOT)]
        cst_t = sb("cst_t", [128, NCONST], F32)
        ident = sb("ident", [128, 128], BF16)
        perm = sb("perm", [128, 128], BF16)
        ones = sb("ones", [128, 128], BF16)

        pbank = [es.enter_context(nc.psum_tensor(f"pb{i}", [128, 512], F32)) for i in range(3)]
        bankM = es.enter_context(nc.psum_tensor("bM", [128, 512], F32))
        bankS = es.enter_context(nc.psum_tensor("bS", [128, 512], F32))
        bankO = es.enter_context(nc.psum_tensor("bO", [128, 512], F32))
        bankK = es.enter_context(nc.psum_tensor("bK", [128, 512], F32))
        bankT = es.enter_context(nc.psum_tensor("bT", [128, 1024], BF16))

        B = {}

        def bf(name):
            if name not in B:
                B[name] = Buf(name)
            return B[name]

        hT_b = [[bf(f"hT{i}_{k}") for k in range(KC)] for i in range(2)]
        pb_b = [bf(f"pb{i}") for i in range(3)]
        ws_b = [bf(f"ws{i}") for i in range(NSLOT)]

        P.op("sp", lambda e: e.dma_start(out=cst_t[:], in_=cst), writes=[bf("cst")], chan="cst")
        P.op("dve", lambda e: e.tensor_copy(out=ident[:], in_=cst_t[:, C_IDENT:C_IDENT + 128]), reads=[bf("cst")], writes=[bf("ident")])
        P.op("dve", lambda e: e.tensor_copy(out=perm[:], in_=cst_t[:, C_PERM:C_PERM + 128]), reads=[bf("cst")], writes=[bf("perm")])
        P.op("dve", lambda e: e.tensor_copy(out=ones[:], in_=cst_t[:, C_ONES:C_ONES + 128]), reads=[bf("cst")], writes=[bf("ones")])
        cb = bf("cst")

        def vec(l, which, kc):
            c = C_VECS + l * 96 + which * 16 + kc
            return cst_t[:, c:c + 1]

        def fvec(kc):
            c = C_VECS + DEPTH * 96 + kc
            return cst_t[:, c:c + 1]

        epsc = cst_t[:, C_EPS:C_EPS + 1]

        def cast_piece(l_, i_):
            r0 = (l_ * NPIECE + i_) * 128
            P.op("gq", lambda e: e.dma_start(out=wbfs[l_][i_ * 128:(i_ + 1) * 128, :], in_=wall[r0:r0 + 128, :]),
                 writes=[bf(f"wbf{l_}_{i_}")], chan=f"cv{i_ % 3}")

        for i_ in range(NPIECE):
            cast_piece(0, i_)

        seq = []
        for l_ in range(nlayers):
            for t in range(NT):
                for i in range(NPIECE):
                    seq.append((wbfs[l_][i * 128:(i + 1) * 128, :], bf(f"wbf{l_}_{i}")))
        wstate = {"seq": seq, "issued": 0, "used": 0}

        def w_issue_upto(n):
            while wstate["issued"] < min(n, len(wstate["seq"])):
                i = wstate["issued"]
                src, srcb = wstate["seq"][i]
                s = i % NSLOT
                P.op("sp", lambda e, s=s, src=src: e.dma_start(out=wslot[s][:], in_=src),
                     reads=[srcb], writes=[ws_b[s]], chan=f"ws{s}")
                wstate["issued"] += 1
                lcur, rem = divmod(i, NT * NPIECE)
                if lcur + 1 < nlayers and rem % NT == 0:
                    cast_piece(lcur + 1, rem // NT)

        def w_next():
            i = wstate["used"]
            w_issue_upto(i + NSLOT)
            wstate["used"] += 1
            s = i % NSLOT
            return wslot[s], ws_b[s]

        pstate = {"i": 0}

        def proj(rhs_tile, rhs_bufs):
            wt, wb = w_next()
            i = pstate["i"] % 3
            pstate["i"] += 1
            bank = pbank[i]

            def fn(e, wt=wt, bank=bank):
                for kc in range(KC):
                    ins = e.matmul(bank[:, 0:T], wt[:, kc * 128:(kc + 1) * 128], rhs_tile[:, kc, :],
                                   start=(kc == 0), stop=(kc == KC - 1))
                return ins
            P.op("pe", fn, reads=[wb] + rhs_bufs, writes=[pb_b[i]])
            return bank, pb_b[i]

        tiles = [(l, t) for l in range(nlayers) for t in range(NT)]
        NTILE = len(tiles)

        def src_of(l):
            return (hin, "hin") if l == 0 else (hbuf, "hbuf")

        def dst_of(l):
            return (hout, "hout") if l == nlayers - 1 else (hbuf, "hbuf")

        def load_h(i):
            l, t = tiles[i]
            src, name = src_of(l)
            b = i % 2
            P.op("gq", lambda e: e.dma_start(out=hT[b][:].rearrange("p k j -> p (k j)"), in_=src[t * 128:(t + 1) * 128, :]),
                 reads=[bf(f"{name}{t}")], writes=hT_b[b], chan=f"hT{b}")

        def load_rot(i):
            l, t = tiles[i]
            P.op("gq", lambda e: e.dma_start(out=rott[:].rearrange("p k j -> p (k j)"), in_=rot[t * 128:(t + 1) * 128, :]),
                 writes=[bf("rott")], chan="rott")

        def rms_stats(b):
            hTi, hTb = hT[b], hT_b[b]
            for kc in range(KC):
                s = kc % 2
                P.op("act", lambda e, kc=kc, s=s: e.activation(out=sq[s][:], in_=hTi[:, kc, :], func=AF.Square),
                     reads=[hTb[kc]], writes=[bf(f"sq{s}")])
                P.op("pe", lambda e, kc=kc, s=s: e.matmul(bankM[:, 0:T], ones[:], sq[s][:], start=(kc == 0), stop=(kc == KC - 1)),
                     reads=[bf(f"sq{s}"), bf("ones")], writes=[bf("bM")])
            P.op("act", lambda e: e.activation(out=rstd[:], in_=bankM[:, 0:T], func=AF.Sqrt, scale=1.0 / D, bias=epsc),
                 reads=[bf("bM"), cb], writes=[bf("rstd")])
            P.op("dve", lambda e: e.reciprocal(out=rstd[:], in_=rstd[:]), reads=[bf("rstd")], writes=[bf("rstd")])

        def make_u(i):
            l, t = tiles[i]
            b = i % 2
            hTi, hTb = hT[b], hT_b[b]
            rms_stats(b)
            for kc in range(KC):
                P.op("dve", lambda e, kc=kc: e.scalar_tensor_tensor(out=uT[:, kc, :], in0=hTi[:, kc, :], scalar=vec(l, 0, kc),
                                                                    in1=rstd[:], op0=ALU.mult, op1=ALU.mult),
                     reads=[hTb[kc], bf("rstd"), cb], writes=[bf("uT")])

        def rot_copy(bank, bankb, i2):
            xr = xraw[i2]
            P.op("act", lambda e: e.copy(out=xr[:], in_=bank[:, 0:T]), reads=[bankb], writes=[bf(f"xraw{i2}")])

        def rot_finish(i2, dst, dstb):
            xr = xraw[i2]
            xb = bf(f"xraw{i2}")
            P.op("pe", lambda e: e.matmul(bankM[:, 0:T], perm[:], xr[:], start=True, stop=True),
                 reads=[xb, bf("perm")], writes=[bf("bM")])
            P.op("dve", lambda e: e.tensor_tensor(out=t1[:], in0=xr[:], in1=rott[:, 0, :], op=ALU.mult),
                 reads=[xb, bf("rott")], writes=[bf("t1")])
            P.op("dve", lambda e: e.tensor_tensor(out=t2[:], in0=bankM[:, 0:T], in1=rott[:, 1, :], op=ALU.mult),
                 reads=[bf("bM"), bf("rott")], writes=[bf("t2")])
            P.op("dve", lambda e: e.tensor_tensor(out=dst[:], in0=t1[:], in1=t2[:], op=ALU.add),
                 reads=[bf("t1"), bf("t2")], writes=[dstb])

        def k_trans(h):
            hb = h % 2

            def trk(e):
                for c in range(NCH):
                    ins = e.transpose(bankT[:, c * 128:(c + 1) * 128], krot[hb][:, c * 128:(c + 1) * 128], ident[:])
                return ins
            P.op("pe", trk, reads=[bf(f"krot{hb}"), bf("ident")], writes=[bf("bT")])
            P.op("dve", lambda e: e.tensor_scalar_mul(out=ktok[hb][:].rearrange("p c d -> p (c d)"), in0=bankT[:, 0:NCH * 128],
                                                      scalar1=cst_t[:, C_KDEC + h:C_KDEC + h + 1]),
                 reads=[bf("bT"), cb], writes=[bf(f"ktok{hb}")])

        def proj_v(h, i):
            bank, bb = proj(uT, [bf("uT")])
            P.op("act", lambda e: e.copy(out=vT[i][:], in_=bank[:, 0:T]), reads=[bb], writes=[bf(f"vT{i}")])

        def v_trans(h):
            hb = h % 2

            def trv(e):
                for c in range(NCH):
                    for i in range(2):
                        o0 = (c * 2 + i) * 128
                        ins = e.transpose(bankT[:, o0:o0 + 128], vT[i][:, c * 128:(c + 1) * 128], ident[:])
                return ins
            P.op("pe", trv, reads=[bf("vT0"), bf("vT1"), bf("ident")], writes=[bf("bT")])
            P.op("act", lambda e: e.copy(out=vtok[hb][:].rearrange("p c d -> p (c d)"), in_=bankT[:, 0:NCH * 256]),
                 reads=[bf("bT")], writes=[bf(f"vtok{hb}")])

        ostate = {"i": 0}

        def ret_S(h):
            hb = h % 2

            def fs(e):
                for c in range(NCH):
                    cs = slice(c * 128, (c + 1) * 128)
                    ins = e.matmul(bankS[:, cs], krot[hb][:, cs], qrot[hb][:, cs], start=True, stop=True)
                return ins
            P.op("pe", fs, reads=[bf(f"krot{hb}"), bf(f"qrot{hb}")], writes=[bf("bS")])
            for c in range(NCH):
                P.op("dve", lambda e, c=c: e.tensor_tensor(out=Pm[c][:], in0=bankS[:, c * 128:(c + 1) * 128],
                                                           in1=cst_t[:, C_MASK + h * 128:C_MASK + (h + 1) * 128], op=ALU.mult),
                     reads=[bf("bS"), cb], writes=[bf(f"Pm{c}")])
                P.op("dve", lambda e, c=c: e.tensor_tensor(out=qd[c][:], in0=qrot[hb][:, c * 128:(c + 1) * 128],
                                                           in1=cst_t[:, C_QDEC + h * 128:C_QDEC + (h + 1) * 128], op=ALU.mult),
                     reads=[bf(f"qrot{hb}"), cb], writes=[bf(f"qd{c}")])

        def ret_chunk(h, c):
            hb = h % 2
            osl = slice(0, 256)
            bo, bk = bf("bO"), bf("bK")

            def fo(e):
                e.matmul(bankO[:, osl], Pm[c][:], vtok[hb][:, c, :], start=True, stop=False)
                return e.matmul(bankO[:, osl], qd[c][:], state_bf[:, h, :], start=False, stop=True)
            P.op("pe", fo, reads=[bf(f"Pm{c}"), bf(f"qd{c}"), bf(f"vtok{hb}"), bf(f"stb{h}")], writes=[bo])
            P.op("pe", lambda e: e.matmul(bankK[:, osl], ktok[hb][:, c, :], vtok[hb][:, c, :], start=True, stop=True),
                 reads=[bf(f"ktok{hb}"), bf(f"vtok{hb}")], writes=[bk])
            P.op("dve", lambda e: e.scalar_tensor_tensor(out=state_f[:, h, :], in0=state_f[:, h, :], scalar=float(GAMMA[h] ** 128),
                                                         in1=bankK[:, osl], op0=ALU.mult, op1=ALU.add),
                 reads=[bk, bf(f"stf{h}")], writes=[bf(f"stf{h}")])
            P.op("act", lambda e: e.copy(out=state_bf[:, h, :], in_=state_f[:, h, :]),
                 reads=[bf(f"stf{h}")], writes=[bf(f"stb{h}")])
            P.op("dve", lambda e: e.tensor_copy(out=osb[:], in_=bankO[:, osl]), reads=[bo], writes=[bf("osb")])
            P.op("dve", lambda e: e.bn_stats(out=st6[:], in_=osb[:]), reads=[bf("osb")], writes=[bf("st6")])
            P.op("dve", lambda e: e.bn_aggr(out=mv[:], in_=st6[:]), reads=[bf("st6")], writes=[bf("mv")])
            P.op("act", lambda e: e.activation(out=grs[:], in_=mv[:, 1:2], func=AF.Sqrt, scale=1.0, bias=epsc),
                 reads=[bf("mv"), cb], writes=[bf("grs")])
            P.op("dve", lambda e: e.reciprocal(out=grs[:], in_=grs[:]), reads=[bf("grs")], writes=[bf("grs")])
            P.op("dve", lambda e: e.tensor_scalar(out=yn[:, c, h * 256:(h + 1) * 256], in0=osb[:],
                                                  scalar1=mv[:, 0:1], scalar2=grs[:, 0:1],
                                                  op0=ALU.subtract, op1=ALU.mult),
                 reads=[bf("osb"), bf("mv"), bf("grs")], writes=[bf("yn")])

        def yret_cc(l, cc):
            s = cc % 2
            bank, bb = proj(uT, [bf("uT")])
            P.op("act", lambda e: e.activation(out=sg[s][:], in_=bank[:, 0:T], func=AF.Silu), reads=[bb], writes=[bf(f"sg{s}")])

            def try_(e):
                for c in range(NCH):
                    ins = e.transpose(bankT[:, c * 128:(c + 1) * 128], yn[:, c, cc * 128:(cc + 1) * 128], ident[:])
                return ins
            P.op("pe", try_, reads=[bf("yn"), bf("ident")], writes=[bf("bT")])
            P.op("dve", lambda e: e.scalar_tensor_tensor(out=yretT[:, cc, :], in0=bankT[:, 0:T], scalar=vec(l, 1, cc),
                                                         in1=sg[s][:], op0=ALU.mult, op1=ALU.mult),
                 reads=[bf("bT"), bf(f"sg{s}"), cb], writes=[bf("yretT")])

        def conv_cc(l, cc):
            s = cc % 2
            cb_ = bf(f"cxp{s}")
            bank, bb = proj(uT, [bf("uT")])
            P.op("act", lambda e, bank=bank: e.copy(out=cxs[:], in_=bank[:, 0:T]), reads=[bb], writes=[bf("cxs")])
            bank, bb = proj(uT, [bf("uT")])
            P.op("dve", lambda e, bank=bank: e.tensor_tensor(out=cxp[s][:, 2:T + 2], in0=bank[:, 0:T], in1=cxs[:], op=ALU.mult),
                 reads=[bb, bf("cxs")], writes=[cb_])
            P.op("dve", lambda e: e.tensor_copy(out=cxp[s][:, 0:2], in_=halo[:, cc, :]), reads=[bf("halo")], writes=[cb_])
            P.op("dve", lambda e: e.tensor_copy(out=halo[:, cc, :], in_=cxp[s][:, T:T + 2]), reads=[cb_], writes=[bf("halo")])
            P.op("dve", lambda e: e.tensor_scalar(out=zt[:], in0=cxp[s][:, 2:T + 2], scalar1=vec(l, 4, cc), scalar2=vec(l, 5, cc),
                                                  op0=ALU.mult, op1=ALU.add),
                 reads=[cb_, cb], writes=[bf("zt")])
            P.op("dve", lambda e: e.scalar_tensor_tensor(out=zt[:], in0=cxp[s][:, 1:T + 1], scalar=vec(l, 3, cc), in1=zt[:],
                                                         op0=ALU.mult, op1=ALU.add),
                 reads=[cb_, cb, bf("zt")], writes=[bf("zt")])
            P.op("dve", lambda e: e.scalar_tensor_tensor(out=zt[:], in0=cxp[s][:, 0:T], scalar=vec(l, 2, cc), in1=zt[:],
                                                         op0=ALU.mult, op1=ALU.add),
                 reads=[cb_, cb, bf("zt")], writes=[bf("zt")])
            bank, bb = proj(uT, [bf("uT")])
            P.op("dve", lambda e, bank=bank: e.tensor_tensor(out=zt[:], in0=bank[:, 0:T], in1=zt[:], op=ALU.mult),
                 reads=[bb, bf("zt")], writes=[bf("zt")])
            bank, bb = proj(uT, [bf("uT")])
            P.op("act", lambda e, bank=bank: e.activation(out=sgc[:], in_=bank[:, 0:T], func=AF.Silu), reads=[bb], writes=[bf("sgc")])
            P.op("dve", lambda e: e.tensor_tensor(out=yconvT[:, cc, :], in0=zt[:], in1=sgc[:], op=ALU.mult),
                 reads=[bf("zt"), bf("sgc")], writes=[bf("yconvT")])

        def merge_dd(dd):
            bank, bb = proj(uT, [bf("uT")])
            P.op("act", lambda e, bank=bank: e.activation(out=smr[:], in_=bank[:, 0:T], func=AF.Sigmoid), reads=[bb], writes=[bf("smr")])
            bank, bb = proj(uT, [bf("uT")])
            P.op("act", lambda e, bank=bank: e.activation(out=smc[:], in_=bank[:, 0:T], func=AF.Sigmoid), reads=[bb], writes=[bf("smc")])
            bank, bb = proj(yretT, [bf("yretT")])
            P.op("dve", lambda e, bank=bank: e.tensor_tensor(out=t1[:], in0=bank[:, 0:T], in1=smr[:], op=ALU.mult),
                 reads=[bb, bf("smr")], writes=[bf("t1")])
            bank, bb = proj(yconvT, [bf("yconvT")])
            P.op("dve", lambda e, bank=bank: e.tensor_tensor(out=t2[:], in0=bank[:, 0:T], in1=smc[:], op=ALU.mult),
                 reads=[bb, bf("smc")], writes=[bf("t2")])
            P.op("dve", lambda e: e.tensor_tensor(out=mergedT[:, dd, :], in0=t1[:], in1=t2[:], op=ALU.add),
                 reads=[bf("t1"), bf("t2")], writes=[bf("mergedT")])

        for i in range(NTILE):
            l, t = tiles[i]
            b = i % 2
            hTi, hTb = hT[b], hT_b[b]
            if t == 0:
                P.op("dve", lambda e: e.memset(halo[:].rearrange("p k j -> p (k j)"), 0.0), writes=[bf("halo")])
                P.op("dve", lambda e: e.memset(state_f[:].rearrange("p h v -> p (h v)"), 0.0), writes=[bf(f"stf{h}") for h in range(H)])
                P.op("dve", lambda e: e.memset(state_bf[:].rearrange("p h v -> p (h v)"), 0.0), writes=[bf(f"stb{h}") for h in range(H)])
            if i == 0:
                load_h(0)
                load_rot(0)
                make_u(0)
            for h in range(H):
                hb = h % 2
                bank, bb = proj(uT, [bf("uT")])
                rot_copy(bank, bb, 0)
                if h >= 1:
                    v_trans(h - 1)
                    ret_S(h - 1)
                bank, bb = proj(uT, [bf("uT")])
                rot_copy(bank, bb, 1)
                rot_finish(0, qrot[hb], bf(f"qrot{hb}"))
                if h >= 1:
                    ret_chunk(h - 1, 0)
                proj_v(h, 0)
                rot_finish(1, krot[hb], bf(f"krot{hb}"))
                if h >= 1:
                    ret_chunk(h - 1, 1)
                proj_v(h, 1)
                k_trans(h)
                if h >= 1:
                    ret_chunk(h - 1, 2)
            v_trans(H - 1)
            if i + 1 < NTILE:
                load_h(i + 1)
                load_rot(i + 1)
            yret_cc(l, 0)
            ret_S(H - 1)
            yret_cc(l, 1)
            ret_chunk(H - 1, 0)
            yret_cc(l, 2)
            ret_chunk(H - 1, 1)
            yret_cc(l, 3)
            ret_chunk(H - 1, 2)
            for cc in range(4, KC):
                yret_cc(l, cc)
            for cc in range(KC):
                conv_cc(l, cc)
            for dd in range(KC):
                merge_dd(dd)
            if i + 1 < NTILE:
                make_u(i + 1)
            for dd in range(KC):
                bank, bb = proj(mergedT, [bf("mergedT")])
                P.op("dve", lambda e, bank=bank, dd=dd, hTi=hTi: e.tensor_tensor(out=hTi[:, dd, :], in0=hTi[:, dd, :], in1=bank[:, 0:T], op=ALU.add),
                     reads=[bb, hTb[dd]], writes=[hTb[dd]])
            if l == nlayers - 1:
                rms_stats(b)
                for kc in range(KC):
                    P.op("dve", lambda e, kc=kc, hTi=hTi: e.scalar_tensor_tensor(out=hTi[:, kc, :], in0=hTi[:, kc, :], scalar=fvec(kc),
                                                                               in1=rstd[:], op0=ALU.mult, op1=ALU.mult),
                         reads=[hTb[kc], bf("rstd"), cb], writes=[hTb[kc]])
            dst, dname = dst_of(l)
            P.op("gq", lambda e, t=t, dst=dst, hTi=hTi: e.dma_start(out=dst[t * 128:(t + 1) * 128, :], in_=hTi[:].rearrange("p k j -> p (k j)")),
                 reads=hTb, writes=[bf(f"{dname}{t}")], chan=f"hT{b}")

        assert wstate["used"] == len(wstate["seq"]), (wstate["used"], len(wstate["seq"]))
        P.finalize()

        plan = P.sem_plan()
        sems = {key: [es.enter_context(nc.semaphore(f"s_{key[0]}_{key[1]}_{i}")) for i in range(n)] for key, n in plan.items()}
        block = es.enter_context(nc.Block())

        @block.tensor
        def _(e):
            P.emit("pe", e, sems)

        @block.scalar
        def _(e):
            P.emit("act", e, sems)

        @block.vector
        def _(e):
            P.emit("dve", e, sems)

        @block.gpsimd
        def _(e):
            P.emit("gq", e, sems)

        @block.sync
        def _(e):
            P.emit("sp", e, sems)
            P.final_waits(e, sems)
    return nc


def _piece_chunk_order():
    q0, k0, v0, gr0, cx0, cp0, co0, gc0, mr0, mc0 = [x // 128 for x in
                                                      (0, 1024, 2048, 4096, 6144, 8192, 10240, 12288, 14336, 16384)]
    wb0_0 = 18432 // 128
    wb1_0 = wb0_0 + 16
    wo_0 = wb1_0 + 16
    order = []
    for h in range(H):
        order += [q0 + h, k0 + h, v0 + 2 * h, v0 + 2 * h + 1]
    for cc in range(KC):
        order += [gr0 + cc]
    for cc in range(KC):
        order += [cx0 + cc, cp0 + cc, co0 + cc, gc0 + cc]
    for dd in range(KC):
        order += [mr0 + dd, mc0 + dd, wb0_0 + dd, wb1_0 + dd]
    for dd in range(KC):
        order += [wo_0 + dd]
    assert len(order) == NPIECE
    return order


def _layer_weights(w_in, w_branch, w_out, l):
    order = np.array(_piece_chunk_order())
    wall = np.concatenate([w_in[l], w_branch[l, 0], w_branch[l, 1], w_out[l]], axis=1)
    w4 = wall.reshape(KC, 128, NPIECE, 128)[:, :, order, :]
    wp = np.ascontiguousarray(w4.transpose(2, 1, 0, 3)).reshape(NPIECE * 128, 2048)
    return wp


def _tile_layout(hloc):
    a = hloc.reshape(NT, T, KC, 128).transpose(0, 3, 2, 1)
    return np.ascontiguousarray(a).reshape(NT * 128, KC * T)


def _untile(ht):
    a = ht.reshape(NT, 128, KC, T).transpose(0, 3, 2, 1)
    return np.ascontiguousarray(a).reshape(NT * T, D)


def _consts(core, norm_g, gn_g, conv_w, conv_b, final_norm_g):
    b, s = (core, 0) if SOLO else divmod(core, 4)
    c = np.zeros((128, NCONST), np.float32)
    idx = np.arange(128, dtype=np.float64)
    scale = 128.0 ** -0.5
    for h in range(H):
        g = GAMMA[h]
        diff = idx[None, :] - idx[:, None]
        m = np.where(diff >= 0, g ** np.maximum(diff, 0.0), 0.0) * scale
        c[:, C_MASK + h * 128:C_MASK + (h + 1) * 128] = m
        c[:, C_QDEC + h * 128:C_QDEC + (h + 1) * 128] = (g ** (idx + 1.0))[None, :] * scale
        c[:, C_KDEC + h] = g ** (127.0 - idx)
    c[:, C_IDENT:C_IDENT + 128] = np.eye(128)
    pm = np.zeros((128, 128))
    for m_ in range(128):
        pm[(m_ + 64) % 128, m_] = 1.0
    c[:, C_PERM:C_PERM + 128] = pm
    c[:, C_ONES:C_ONES + 128] = 1.0
    for l in range(DEPTH):
        base = C_VECS + l * 96
        c[:, base + 0:base + 16] = norm_g[l].reshape(KC, 128).T
        c[:, base + 16:base + 32] = gn_g[l].reshape(KC, 128).T
        for k in range(3):
            c[:, base + 32 + 16 * k:base + 48 + 16 * k] = conv_w[l, k].reshape(KC, 128).T
        c[:, base + 80:base + 96] = conv_b[l].reshape(KC, 128).T
    c[:, C_VECS + DEPTH * 96:C_VECS + DEPTH * 96 + 16] = final_norm_g.reshape(KC, 128).T
    for j in range(0 if SOLO else NCORE):
        bj, sj = divmod(j, 4)
        if bj == b and sj < s:
            for h in range(H):
                c[:, C_COEF + j * 8 + h] = GAMMA[h] ** (4096.0 * (s - 1 - sj))
        if bj == b and sj == s - 1:
            c[:, C_HCOEF + j] = 1.0
    c[:, C_EPS] = EPS
    c[:, C_SELF + core] = 1.0
    return c


def _rot_tables(core):
    b, s = (core, 0) if SOLO else divmod(core, 4)
    tau = np.arange(NT * T)
    pos = np.where(tau >= 128, 16 + s * 4096 + (tau - 128), np.maximum(tau - 112, 0)).astype(np.float32)
    inv_freq = (np.float32(10000.0) ** (-(np.arange(64, dtype=np.float32) / np.float32(64)))).astype(np.float32)
    ang = (pos[:, None] * inv_freq[None, :]).astype(np.float32)
    cos = np.cos(ang).astype(np.float32)
    sin = np.sin(ang).astype(np.float32)
    cosT = np.concatenate([cos, cos], axis=1).T
    sinT = np.concatenate([-sin, sin], axis=1).T
    r = np.stack([cosT.reshape(128, NT, T), sinT.reshape(128, NT, T)], axis=2)
    return np.ascontiguousarray(r.transpose(1, 0, 2, 3)).reshape(NT * 128, 2 * T)


_PROG_CACHE = {}


def _prog():
    if "p" not in _PROG_CACHE:
        _PROG_CACHE["p"] = build_program()
    return _PROG_CACHE["p"]


def kernel(x, meta_tokens, norm_g, w_in, conv_w, conv_b, gn_g, w_branch, w_out, final_norm_g):
    x = np.asarray(x, np.float32)
    meta_tokens = np.asarray(meta_tokens, np.float32)
    norm_g = np.asarray(norm_g, np.float32)
    w_in = np.asarray(w_in, np.float32)
    conv_w = np.asarray(conv_w, np.float32)
    conv_b = np.asarray(conv_b, np.float32)
    gn_g = np.asarray(gn_g, np.float32)
    w_branch = np.asarray(w_branch, np.float32)
    w_out = np.asarray(w_out, np.float32)
    final_norm_g = np.asarray(final_norm_g, np.float32)

    cores = list(range(NCORE_USED))
    nreal = (NT * T - 128)
    hs = []
    for core in cores:
        b, s = (core, 0) if SOLO else divmod(core, 4)
        hloc = np.zeros((NT * T, D), np.float32)
        if s == 0:
            hloc[112:128] = meta_tokens
        hloc[128:] = x[b, s * nreal:(s + 1) * nreal]
        hs.append(_tile_layout(hloc))
    rots = [_rot_tables(c) for c in cores]
    csts = [_consts(c, norm_g, gn_g, conv_w, conv_b, final_norm_g) for c in cores]
    wall = np.empty((DEPTH * NPIECE * 128, 2048), np.float32)
    for l in range(DEPTH):
        wall[l * NPIECE * 128:(l + 1) * NPIECE * 128] = _layer_weights(w_in, w_branch, w_out, l)

    nc = _prog()
    res = run_bass_kernel_spmd(nc, [{"hin": hs[c], "rot": rots[c], "cst": csts[c], "wall": wall} for c in cores], core_ids=cores)
    out = np.zeros((2, 16384, D), np.float32)
    for core in cores:
        b, s = (core, 0) if SOLO else divmod(core, 4)
        out[b, s * nreal:(s + 1) * nreal] = _untile(res.results[core]["hout"])[128:]
    return out
```

```python
import numpy as np
import ml_dtypes
from contextlib import ExitStack
import concourse.bass as bass
import concourse.mybir as mybir
from concourse.bass_utils import run_bass_kernel_spmd

F32 = mybir.dt.float32
BF16 = mybir.dt.bfloat16
AF = mybir.ActivationFunctionType
ALU = mybir.AluOpType

D = 2048
KC = 16
T = 384
NCH = 3
SOLO = True
NT = 43 if SOLO else 11
NCORE_USED = 2 if SOLO else 8
H = 8
DEPTH = 4
NCORE = 8
EPS = 1e-6
NPIECE = 192
EW = 2048 + 32
NSLOT = 7
SAME_ENG_SYNC = True

C_MASK = 0
C_QDEC = C_MASK + 1024
C_KDEC = C_QDEC + 1024
C_IDENT = C_KDEC + 8
C_PERM = C_IDENT + 128
C_ONES = C_PERM + 128
C_VECS = C_ONES + 128
C_COEF = C_VECS + DEPTH * 96 + 16
C_HCOEF = C_COEF + 64
C_EPS = C_HCOEF + 8
C_SELF = C_EPS + 1
NCONST = C_SELF + 8

GAMMA = [1.0 - 2.0 ** (-5.0 - h) for h in range(H)]


class Buf:
    __slots__ = ("name", "w", "r")

    def __init__(self, name):
        self.name = name
        self.w = None
        self.r = []


class Op:
    __slots__ = ("eng", "fn", "deps", "token", "signal", "seq", "is_dma", "chan", "waits", "sigval", "inc")


class Prog:
    ENGS = ("pe", "act", "dve", "sp", "gq")

    def __init__(self):
        self.ops = {e: [] for e in self.ENGS}
        self.chan_count = {}
        self.chan_last = {}

    def op(self, eng, fn, reads=(), writes=(), chan=None, inc=16):
        o = Op()
        o.inc = inc
        o.eng = eng
        o.fn = fn
        o.is_dma = chan is not None
        o.chan = chan
        o.seq = len(self.ops[eng])
        o.signal = False
        o.token = None
        o.sigval = None
        deps = []
        for b in reads:
            if b.w is not None:
                deps.append((b.w, 0))
        for b in writes:
            if b.w is not None:
                deps.append((b.w, 1))
            for r in b.r:
                deps.append((r, 2))
        if o.is_dma:
            k = self.chan_count.get(chan, 0) + 1
            self.chan_count[chan] = k
            o.token = (chan, k)
            prev = self.chan_last.get(chan)
            if prev is not None:
                deps.append((prev, 0))
            self.chan_last[chan] = o
        o.deps = deps
        for b in reads:
            b.r.append(o)
        for b in writes:
            b.w = o
            b.r = []
        self.ops[eng].append(o)
        return o

    def finalize(self):
        for eng in self.ENGS:
            waited = {}
            for o in self.ops[eng]:
                need = {}
                for (p, kind) in o.deps:
                    if p is o:
                        continue
                    if p.is_dma:
                        key = ("c", p.chan)
                        val = p.token[1]
                    else:
                        if p.eng == eng:
                            if eng == "pe" or not SAME_ENG_SYNC or kind == 2:
                                continue
                        key = ("e", p.eng)
                        val = p.seq
                    if waited.get(key, -1) >= val:
                        continue
                    cur = need.get(key)
                    if cur is None or cur[0] < val:
                        need[key] = (val, p)
                o.waits = []
                for key, (val, p) in need.items():
                    waited[key] = val
                    if not p.is_dma:
                        p.signal = True
                    o.waits.append(p)
        for eng in ("pe", "act", "dve"):
            cnt = 0
            for o in self.ops[eng]:
                if o.signal:
                    cnt += 1
                    o.sigval = cnt

    LIM = 16384

    def sem_plan(self):
        plan = {}
        for eng in ("pe", "act", "dve"):
            n = max([o.sigval or 0 for o in self.ops[eng]] + [0])
            plan[("e", eng)] = max(1, -(-n // self.LIM))
        for c, k in self.chan_count.items():
            plan[("c", c)] = max(1, -(-k // (self.LIM // 16)))
        return plan

    def _loc(self, ordinal, per, inc):
        return (ordinal - 1) // per, ((ordinal - 1) % per + 1) * inc

    def emit(self, eng, e, sems):
        per_d = self.LIM // 16
        for o in self.ops[eng]:
            for p in o.waits:
                if p.is_dma:
                    b, v = self._loc(p.token[1], per_d, p.inc)
                    e.wait_ge(sems[("c", p.chan)][b], v)
                else:
                    b, v = self._loc(p.sigval, self.LIM, 1)
                    e.wait_ge(sems[("e", p.eng)][b], v)
            ins = o.fn(e)
            if o.is_dma:
                b, v = self._loc(o.token[1], per_d, o.inc)
                ins.then_inc(sems[("c", o.chan)][b], o.inc)
            elif o.signal:
                b, v = self._loc(o.sigval, self.LIM, 1)
                ins.then_inc(sems[("e", o.eng)][b], 1)

    def final_waits(self, e, sems):
        per_d = self.LIM // 16
        for c, o in self.chan_last.items():
            b, v = self._loc(o.token[1], per_d, o.inc)
            e.wait_ge(sems[("c", c)][b], v)


def build_program(nlayers=DEPTH):
    nc = bass.Bass("TRN2", target_bir_lowering=False)
    P = Prog()
    hin = nc.dram_tensor("hin", [NT * 128, KC * T], F32, kind="ExternalInput").ap()
    rot = nc.dram_tensor("rot", [NT * 128, 2 * T], F32, kind="ExternalInput").ap()
    cst = nc.dram_tensor("cst", [128, NCONST], F32, kind="ExternalInput").ap()
    wall = nc.dram_tensor("wall", [DEPTH * NPIECE * 128, 2048], F32, kind="ExternalInput").ap()
    hout = nc.dram_tensor("hout", [NT * 128, KC * T], F32, kind="ExternalOutput").ap()
    hbuf = nc.dram_tensor("hbuf", [NT * 128, KC * T], F32).ap()
    wbfs = [nc.dram_tensor(f"wbf{l}", [NPIECE * 128, 2048], BF16).ap() for l in range(DEPTH)]

    es = ExitStack()
    with es:
        def sb(name, shape, dt):
            return es.enter_context(nc.sbuf_tensor(name, shape, dt))

        hT = [sb(f"hT{i}", [128, KC, T], F32) for i in range(2)]
        uT = sb("uT", [128, KC, T], BF16)
        sq = [sb(f"sq{i}", [128, T], BF16) for i in range(KC)]
        rstd = sb("rstd", [128, T], F32)
        rott = sb("rott", [128, 2, T], F32)
        xraw = [sb(f"xraw{i}", [128, T], BF16) for i in range(2)]
        t1 = sb("t1", [128, T], F32)
        t2 = sb("t2", [128, T], F32)
        qrot = [sb(f"qrot{i}", [128, T], BF16) for i in range(2)]
        krot = [sb(f"krot{i}", [128, T], BF16) for i in range(2)]
        ktok = [sb(f"ktok{i}", [128, NCH, 128], BF16) for i in range(2)]
        vT = [sb(f"vT{i}", [128, T], BF16) for i in range(2)]
        vtok = [sb(f"vtok{i}", [128, NCH, 256], BF16) for i in range(2)]
        Pm = [sb(f"Pm{i}", [128, 128], BF16) for i in range(NCH)]
        qd = [sb(f"qd{i}", [128, 128], BF16) for i in range(NCH)]
        state_f = sb("state_f", [128, H, 256], F32)
        state_bf = sb("state_bf", [128, H, 256], BF16)
        yn = sb("yn", [128, NCH, D], BF16)
        sg = [sb(f"sg{i}", [128, T], BF16) for i in range(2)]
        yretT = sb("yretT", [128, KC, T], BF16)
        yconvT = sb("yconvT", [128, KC, T], BF16)
        mergedT = sb("mergedT", [128, KC, T], BF16)
        cxs = sb("cxs", [128, T], F32)
        cxp = [sb(f"cxp{i}", [128, T + 2], F32) for i in range(2)]
        zt = sb("zt", [128, T], F32)
        sgc = sb("sgc", [128, T], F32)
        smr = sb("smr", [128, T], F32)
        smc = sb("smc", [128, T], F32)
        halo = sb("halo", [128, KC, 2], F32)
        st6 = sb("st6", [128, 6], F32)
        osb = sb("osb", [128, 256], F32)
        mv = sb("mv", [128, 2], F32)
        grs = sb("grs", [128, 1], F32)
        wslot = [sb(f"wslot{i}", [128, 2048], BF16) for i in range(NSLOT)]
        cst_t = sb("cst_t", [128, NCONST], F32)
        ident = sb("ident", [128, 128], BF16)
        perm = sb("perm", [128, 128], BF16)
        ones = sb("ones", [128, 128], BF16)

        pbank = [es.enter_context(nc.psum_tensor(f"pb{i}", [128, 512], F32)) for i in range(3)]
        bankM = es.enter_context(nc.psum_tensor("bM", [128, 512], F32))
        bankS = es.enter_context(nc.psum_tensor("bS", [128, 512], F32))
        bankO = es.enter_context(nc.psum_tensor("bO", [128, 512], F32))
        bankK = es.enter_context(nc.psum_tensor("bK", [128, 512], F32))
        bankT = es.enter_context(nc.psum_tensor("bT", [128, 1024], BF16))

        B = {}

        def bf(name):
            if name not in B:
                B[name] = Buf(name)
            return B[name]

        hT_b = [[bf(f"hT{i}_{k}") for k in range(KC)] for i in range(2)]
        pb_b = [bf(f"pb{i}") for i in range(3)]
        ws_b = [bf(f"ws{i}") for i in range(NSLOT)]

        P.op("sp", lambda e: e.dma_start(out=cst_t[:], in_=cst), writes=[bf("cst")], chan="cst")
        P.op("dve", lambda e: e.tensor_copy(out=ident[:], in_=cst_t[:, C_IDENT:C_IDENT + 128]), reads=[bf("cst")], writes=[bf("ident")])
        P.op("dve", lambda e: e.tensor_copy(out=perm[:], in_=cst_t[:, C_PERM:C_PERM + 128]), reads=[bf("cst")], writes=[bf("perm")])
        P.op("dve", lambda e: e.tensor_copy(out=ones[:], in_=cst_t[:, C_ONES:C_ONES + 128]), reads=[bf("cst")], writes=[bf("ones")])
        cb = bf("cst")

        def vec(l, which, kc):
            c = C_VECS + l * 96 + which * 16 + kc
            return cst_t[:, c:c + 1]

        def fvec(kc):
            c = C_VECS + DEPTH * 96 + kc
            return cst_t[:, c:c + 1]

        epsc = cst_t[:, C_EPS:C_EPS + 1]

        def cast_piece(l_, i_):
            r0 = (l_ * NPIECE + i_) * 128
            P.op("gq", lambda e: e.dma_start(out=wbfs[l_][i_ * 128:(i_ + 1) * 128, :], in_=wall[r0:r0 + 128, :]),
                 writes=[bf(f"wbf{l_}_{i_}")], chan=f"cv{i_ % 3}")

        for i_ in range(NPIECE):
            cast_piece(0, i_)

        seq = []
        for l_ in range(nlayers):
            for t in range(NT):
                for i in range(NPIECE):
                    seq.append((wbfs[l_][i * 128:(i + 1) * 128, :], bf(f"wbf{l_}_{i}")))
        wstate = {"seq": seq, "issued": 0, "used": 0}

        def w_issue_upto(n):
            while wstate["issued"] < min(n, len(wstate["seq"])):
                i = wstate["issued"]
                src, srcb = wstate["seq"][i]
                s = i % NSLOT
                P.op("sp", lambda e, s=s, src=src: e.dma_start(out=wslot[s][:], in_=src),
                     reads=[srcb], writes=[ws_b[s]], chan=f"ws{s}")
                wstate["issued"] += 1
                lcur, rem = divmod(i, NT * NPIECE)
                if lcur + 1 < nlayers and rem % NT == 0:
                    cast_piece(lcur + 1, rem // NT)

        def w_next():
            i = wstate["used"]
            w_issue_upto(i + NSLOT)
            wstate["used"] += 1
            s = i % NSLOT
            return wslot[s], ws_b[s]

        pstate = {"i": 0}

        def proj(rhs_tile, rhs_bufs):
            wt, wb = w_next()
            i = pstate["i"] % 3
            pstate["i"] += 1
            bank = pbank[i]

            def fn(e, wt=wt, bank=bank):
                for kc in range(KC):
                    ins = e.matmul(bank[:, 0:T], wt[:, kc * 128:(kc + 1) * 128], rhs_tile[:, kc, :],
                                   start=(kc == 0), stop=(kc == KC - 1))
                return ins
            P.op("pe", fn, reads=[wb] + rhs_bufs, writes=[pb_b[i]])
            return bank, pb_b[i]

        tiles = [(l, t) for l in range(nlayers) for t in range(NT)]
        NTILE = len(tiles)

        def src_of(l):
            return (hin, "hin") if l == 0 else (hbuf, "hbuf")

        def dst_of(l):
            return (hout, "hout") if l == nlayers - 1 else (hbuf, "hbuf")

        def load_h(i):
            l, t = tiles[i]
            src, name = src_of(l)
            b = i % 2
            P.op("gq", lambda e: e.dma_start(out=hT[b][:].rearrange("p k j -> p (k j)"), in_=src[t * 128:(t + 1) * 128, :]),
                 reads=[bf(f"{name}{t}")], writes=hT_b[b], chan=f"hT{b}")

        def load_rot(i):
            l, t = tiles[i]
            P.op("gq", lambda e: e.dma_start(out=rott[:].rearrange("p k j -> p (k j)"), in_=rot[t * 128:(t + 1) * 128, :]),
                 writes=[bf("rott")], chan="rott")

        def rms_squares(b):
            hTi, hTb = hT[b], hT_b[b]
            for kc in range(KC):
                P.op("act", lambda e, kc=kc: e.activation(out=sq[kc][:], in_=hTi[:, kc, :], func=AF.Square),
                     reads=[hTb[kc]], writes=[bf(f"sq{kc}")])

        def rms_reduce():
            for kc in range(KC):
                P.op("pe", lambda e, kc=kc: e.matmul(bankM[:, 0:T], ones[:], sq[kc][:], start=(kc == 0), stop=(kc == KC - 1)),
                     reads=[bf(f"sq{kc}"), bf("ones")], writes=[bf("bM")])
            P.op("act", lambda e: e.activation(out=rstd[:], in_=bankM[:, 0:T], func=AF.Sqrt, scale=1.0 / D, bias=epsc),
                 reads=[bf("bM"), cb], writes=[bf("rstd")])
            P.op("dve", lambda e: e.reciprocal(out=rstd[:], in_=rstd[:]), reads=[bf("rstd")], writes=[bf("rstd")])

        def rms_stats(b):
            rms_squares(b)
            rms_reduce()

        def write_u(i):
            l, t = tiles[i]
            b = i % 2
            hTi, hTb = hT[b], hT_b[b]
            for kc in range(KC):
                P.op("dve", lambda e, kc=kc: e.scalar_tensor_tensor(out=uT[:, kc, :], in0=hTi[:, kc, :], scalar=vec(l, 0, kc),
                                                                    in1=rstd[:], op0=ALU.mult, op1=ALU.mult),
                     reads=[hTb[kc], bf("rstd"), cb], writes=[bf("uT")])

        def make_u(i):
            rms_stats(i % 2)
            write_u(i)

        def rot_copy(bank, bankb, i2):
            xr = xraw[i2]
            P.op("act", lambda e: e.copy(out=xr[:], in_=bank[:, 0:T]), reads=[bankb], writes=[bf(f"xraw{i2}")])

        def rot_finish(i2, dst, dstb):
            xr = xraw[i2]
            xb = bf(f"xraw{i2}")
            P.op("pe", lambda e: e.matmul(bankM[:, 0:T], perm[:], xr[:], start=True, stop=True),
                 reads=[xb, bf("perm")], writes=[bf("bM")])
            P.op("dve", lambda e: e.tensor_tensor(out=t1[:], in0=xr[:], in1=rott[:, 0, :], op=ALU.mult),
                 reads=[xb, bf("rott")], writes=[bf("t1")])
            P.op("dve", lambda e: e.tensor_tensor(out=t2[:], in0=bankM[:, 0:T], in1=rott[:, 1, :], op=ALU.mult),
                 reads=[bf("bM"), bf("rott")], writes=[bf("t2")])
            P.op("dve", lambda e: e.tensor_tensor(out=dst[:], in0=t1[:], in1=t2[:], op=ALU.add),
                 reads=[bf("t1"), bf("t2")], writes=[dstb])

        def k_trans(h):
            hb = h % 2

            def trk(e):
                for c in range(NCH):
                    ins = e.transpose(bankT[:, c * 128:(c + 1) * 128], krot[hb][:, c * 128:(c + 1) * 128], ident[:])
                return ins
            P.op("pe", trk, reads=[bf(f"krot{hb}"), bf("ident")], writes=[bf("bT")])
            P.op("dve", lambda e: e.tensor_scalar_mul(out=ktok[hb][:].rearrange("p c d -> p (c d)"), in0=bankT[:, 0:NCH * 128],
                                                      scalar1=cst_t[:, C_KDEC + h:C_KDEC + h + 1]),
                 reads=[bf("bT"), cb], writes=[bf(f"ktok{hb}")])

        def proj_v(h, i):
            bank, bb = proj(uT, [bf("uT")])
            P.op("act", lambda e: e.copy(out=vT[i][:], in_=bank[:, 0:T]), reads=[bb], writes=[bf(f"vT{i}")])

        def v_trans(h):
            hb = h % 2

            def trv(e):
                for c in range(NCH):
                    for i in range(2):
                        o0 = (c * 2 + i) * 128
                        ins = e.transpose(bankT[:, o0:o0 + 128], vT[i][:, c * 128:(c + 1) * 128], ident[:])
                return ins
            P.op("pe", trv, reads=[bf("vT0"), bf("vT1"), bf("ident")], writes=[bf("bT")])
            P.op("act", lambda e: e.copy(out=vtok[hb][:].rearrange("p c d -> p (c d)"), in_=bankT[:, 0:NCH * 256]),
                 reads=[bf("bT")], writes=[bf(f"vtok{hb}")])

        ostate = {"i": 0}

        def ret_S(h):
            hb = h % 2

            def fs(e):
                for c in range(NCH):
                    cs = slice(c * 128, (c + 1) * 128)
                    ins = e.matmul(bankS[:, cs], krot[hb][:, cs], qrot[hb][:, cs], start=True, stop=True)
                return ins
            P.op("pe", fs, reads=[bf(f"krot{hb}"), bf(f"qrot{hb}")], writes=[bf("bS")])
            for c in range(NCH):
                P.op("dve", lambda e, c=c: e.tensor_tensor(out=Pm[c][:], in0=bankS[:, c * 128:(c + 1) * 128],
                                                           in1=cst_t[:, C_MASK + h * 128:C_MASK + (h + 1) * 128], op=ALU.mult),
                     reads=[bf("bS"), cb], writes=[bf(f"Pm{c}")])
                P.op("dve", lambda e, c=c: e.tensor_tensor(out=qd[c][:], in0=qrot[hb][:, c * 128:(c + 1) * 128],
                                                           in1=cst_t[:, C_QDEC + h * 128:C_QDEC + (h + 1) * 128], op=ALU.mult),
                     reads=[bf(f"qrot{hb}"), cb], writes=[bf(f"qd{c}")])

        def ret_chunk(h, c):
            hb = h % 2
            osl = slice(0, 256)
            bo, bk = bf("bO"), bf("bK")

            def fo(e):
                e.matmul(bankO[:, osl], Pm[c][:], vtok[hb][:, c, :], start=True, stop=False)
                return e.matmul(bankO[:, osl], qd[c][:], state_bf[:, h, :], start=False, stop=True)
            P.op("pe", fo, reads=[bf(f"Pm{c}"), bf(f"qd{c}"), bf(f"vtok{hb}"), bf(f"stb{h}")], writes=[bo])
            P.op("pe", lambda e: e.matmul(bankK[:, osl], ktok[hb][:, c, :], vtok[hb][:, c, :], start=True, stop=True),
                 reads=[bf(f"ktok{hb}"), bf(f"vtok{hb}")], writes=[bk])
            P.op("dve", lambda e: e.scalar_tensor_tensor(out=state_f[:, h, :], in0=state_f[:, h, :], scalar=float(GAMMA[h] ** 128),
                                                         in1=bankK[:, osl], op0=ALU.mult, op1=ALU.add),
                 reads=[bk, bf(f"stf{h}")], writes=[bf(f"stf{h}")])
            P.op("act", lambda e: e.copy(out=state_bf[:, h, :], in_=state_f[:, h, :]),
                 reads=[bf(f"stf{h}")], writes=[bf(f"stb{h}")])
            P.op("dve", lambda e: e.tensor_copy(out=osb[:], in_=bankO[:, osl]), reads=[bo], writes=[bf("osb")])
            P.op("dve", lambda e: e.bn_stats(out=st6[:], in_=osb[:]), reads=[bf("osb")], writes=[bf("st6")])
            P.op("dve", lambda e: e.bn_aggr(out=mv[:], in_=st6[:]), reads=[bf("st6")], writes=[bf("mv")])
            P.op("act", lambda e: e.activation(out=grs[:], in_=mv[:, 1:2], func=AF.Sqrt, scale=1.0, bias=epsc),
                 reads=[bf("mv"), cb], writes=[bf("grs")])
            P.op("dve", lambda e: e.reciprocal(out=grs[:], in_=grs[:]), reads=[bf("grs")], writes=[bf("grs")])
            P.op("dve", lambda e: e.tensor_scalar(out=yn[:, c, h * 256:(h + 1) * 256], in0=osb[:],
                                                  scalar1=mv[:, 0:1], scalar2=grs[:, 0:1],
                                                  op0=ALU.subtract, op1=ALU.mult),
                 reads=[bf("osb"), bf("mv"), bf("grs")], writes=[bf("yn")])

        def yret_cc(l, cc):
            s = cc % 2
            bank, bb = proj(uT, [bf("uT")])
            P.op("act", lambda e: e.activation(out=sg[s][:], in_=bank[:, 0:T], func=AF.Silu), reads=[bb], writes=[bf(f"sg{s}")])

            def try_(e):
                for c in range(NCH):
                    ins = e.transpose(bankT[:, c * 128:(c + 1) * 128], yn[:, c, cc * 128:(cc + 1) * 128], ident[:])
                return ins
            P.op("pe", try_, reads=[bf("yn"), bf("ident")], writes=[bf("bT")])
            P.op("dve", lambda e: e.scalar_tensor_tensor(out=yretT[:, cc, :], in0=bankT[:, 0:T], scalar=vec(l, 1, cc),
                                                         in1=sg[s][:], op0=ALU.mult, op1=ALU.mult),
                 reads=[bf("bT"), bf(f"sg{s}"), cb], writes=[bf("yretT")])

        def conv_cc(l, cc):
            s = cc % 2
            cb_ = bf(f"cxp{s}")
            bank, bb = proj(uT, [bf("uT")])
            P.op("act", lambda e, bank=bank: e.copy(out=cxs[:], in_=bank[:, 0:T]), reads=[bb], writes=[bf("cxs")])
            bank, bb = proj(uT, [bf("uT")])
            P.op("dve", lambda e, bank=bank: e.tensor_tensor(out=cxp[s][:, 2:T + 2], in0=bank[:, 0:T], in1=cxs[:], op=ALU.mult),
                 reads=[bb, bf("cxs")], writes=[cb_])
            P.op("dve", lambda e: e.tensor_copy(out=cxp[s][:, 0:2], in_=halo[:, cc, :]), reads=[bf("halo")], writes=[cb_])
            P.op("dve", lambda e: e.tensor_copy(out=halo[:, cc, :], in_=cxp[s][:, T:T + 2]), reads=[cb_], writes=[bf("halo")])
            P.op("dve", lambda e: e.tensor_scalar(out=zt[:], in0=cxp[s][:, 2:T + 2], scalar1=vec(l, 4, cc), scalar2=vec(l, 5, cc),
                                                  op0=ALU.mult, op1=ALU.add),
                 reads=[cb_, cb], writes=[bf("zt")])
            P.op("dve", lambda e: e.scalar_tensor_tensor(out=zt[:], in0=cxp[s][:, 1:T + 1], scalar=vec(l, 3, cc), in1=zt[:],
                                                         op0=ALU.mult, op1=ALU.add),
                 reads=[cb_, cb, bf("zt")], writes=[bf("zt")])
            P.op("dve", lambda e: e.scalar_tensor_tensor(out=zt[:], in0=cxp[s][:, 0:T], scalar=vec(l, 2, cc), in1=zt[:],
                                                         op0=ALU.mult, op1=ALU.add),
                 reads=[cb_, cb, bf("zt")], writes=[bf("zt")])
            bank, bb = proj(uT, [bf("uT")])
            P.op("dve", lambda e, bank=bank: e.tensor_tensor(out=zt[:], in0=bank[:, 0:T], in1=zt[:], op=ALU.mult),
                 reads=[bb, bf("zt")], writes=[bf("zt")])
            bank, bb = proj(uT, [bf("uT")])
            P.op("act", lambda e, bank=bank: e.activation(out=sgc[:], in_=bank[:, 0:T], func=AF.Silu), reads=[bb], writes=[bf("sgc")])
            P.op("dve", lambda e: e.tensor_tensor(out=yconvT[:, cc, :], in0=zt[:], in1=sgc[:], op=ALU.mult),
                 reads=[bf("zt"), bf("sgc")], writes=[bf("yconvT")])

        def merge_dd(dd):
            bank, bb = proj(uT, [bf("uT")])
            P.op("act", lambda e, bank=bank: e.activation(out=smr[:], in_=bank[:, 0:T], func=AF.Sigmoid), reads=[bb], writes=[bf("smr")])
            bank, bb = proj(uT, [bf("uT")])
            P.op("act", lambda e, bank=bank: e.activation(out=smc[:], in_=bank[:, 0:T], func=AF.Sigmoid), reads=[bb], writes=[bf("smc")])
            bank, bb = proj(yretT, [bf("yretT")])
            P.op("dve", lambda e, bank=bank: e.tensor_tensor(out=t1[:], in0=bank[:, 0:T], in1=smr[:], op=ALU.mult),
                 reads=[bb, bf("smr")], writes=[bf("t1")])
            bank, bb = proj(yconvT, [bf("yconvT")])
            P.op("dve", lambda e, bank=bank: e.tensor_tensor(out=t2[:], in0=bank[:, 0:T], in1=smc[:], op=ALU.mult),
                 reads=[bb, bf("smc")], writes=[bf("t2")])
            P.op("dve", lambda e: e.tensor_tensor(out=mergedT[:, dd, :], in0=t1[:], in1=t2[:], op=ALU.add),
                 reads=[bf("t1"), bf("t2")], writes=[bf("mergedT")])

        for i in range(NTILE):
            l, t = tiles[i]
            b = i % 2
            hTi, hTb = hT[b], hT_b[b]
            if t == 0:
                P.op("dve", lambda e: e.memset(halo[:].rearrange("p k j -> p (k j)"), 0.0), writes=[bf("halo")])
                P.op("dve", lambda e: e.memset(state_f[:].rearrange("p h v -> p (h v)"), 0.0), writes=[bf(f"stf{h}") for h in range(H)])
                P.op("dve", lambda e: e.memset(state_bf[:].rearrange("p h v -> p (h v)"), 0.0), writes=[bf(f"stb{h}") for h in range(H)])
            if i == 0:
                load_h(0)
                load_rot(0)
                make_u(0)
            for h in range(H):
                hb = h % 2
                bank, bb = proj(uT, [bf("uT")])
                rot_copy(bank, bb, 0)
                if h >= 1:
                    v_trans(h - 1)
                    ret_S(h - 1)
                bank, bb = proj(uT, [bf("uT")])
                rot_copy(bank, bb, 1)
                rot_finish(0, qrot[hb], bf(f"qrot{hb}"))
                if h >= 1:
                    ret_chunk(h - 1, 0)
                proj_v(h, 0)
                rot_finish(1, krot[hb], bf(f"krot{hb}"))
                if h >= 1:
                    ret_chunk(h - 1, 1)
                proj_v(h, 1)
                k_trans(h)
                if h >= 1:
                    ret_chunk(h - 1, 2)
            v_trans(H - 1)
            if i + 1 < NTILE:
                load_h(i + 1)
                load_rot(i + 1)
            yret_cc(l, 0)
            ret_S(H - 1)
            yret_cc(l, 1)
            ret_chunk(H - 1, 0)
            yret_cc(l, 2)
            ret_chunk(H - 1, 1)
            yret_cc(l, 3)
            ret_chunk(H - 1, 2)
            for cc in range(4, KC):
                yret_cc(l, cc)
            for cc in range(KC):
                conv_cc(l, cc)
            for dd in range(KC):
                merge_dd(dd)
                if i + 1 < NTILE and dd == 11:
                    rms_squares((i + 1) % 2)
                if i + 1 < NTILE and dd == 13:
                    rms_reduce()
            if i + 1 < NTILE:
                write_u(i + 1)
            for dd in range(KC):
                bank, bb = proj(mergedT, [bf("mergedT")])
                P.op("dve", lambda e, bank=bank, dd=dd, hTi=hTi: e.tensor_tensor(out=hTi[:, dd, :], in0=hTi[:, dd, :], in1=bank[:, 0:T], op=ALU.add),
                     reads=[bb, hTb[dd]], writes=[hTb[dd]])
            if l == nlayers - 1:
                rms_stats(b)
                for kc in range(KC):
                    P.op("dve", lambda e, kc=kc, hTi=hTi: e.scalar_tensor_tensor(out=hTi[:, kc, :], in0=hTi[:, kc, :], scalar=fvec(kc),
                                                                               in1=rstd[:], op0=ALU.mult, op1=ALU.mult),
                         reads=[hTb[kc], bf("rstd"), cb], writes=[hTb[kc]])
            dst, dname = dst_of(l)
            P.op("gq", lambda e, t=t, dst=dst, hTi=hTi: e.dma_start(out=dst[t * 128:(t + 1) * 128, :], in_=hTi[:].rearrange("p k j -> p (k j)")),
                 reads=hTb, writes=[bf(f"{dname}{t}")], chan=f"hT{b}")

        assert wstate["used"] == len(wstate["seq"]), (wstate["used"], len(wstate["seq"]))
        P.finalize()

        plan = P.sem_plan()
        sems = {key: [es.enter_context(nc.semaphore(f"s_{key[0]}_{key[1]}_{i}")) for i in range(n)] for key, n in plan.items()}
        block = es.enter_context(nc.Block())

        @block.tensor
        def _(e):
            P.emit("pe", e, sems)

        @block.scalar
        def _(e):
            P.emit("act", e, sems)

        @block.vector
        def _(e):
            P.emit("dve", e, sems)

        @block.gpsimd
        def _(e):
            P.emit("gq", e, sems)

        @block.sync
        def _(e):
            P.emit("sp", e, sems)
            P.final_waits(e, sems)
    return nc


def _piece_chunk_order():
    q0, k0, v0, gr0, cx0, cp0, co0, gc0, mr0, mc0 = [x // 128 for x in
                                                      (0, 1024, 2048, 4096, 6144, 8192, 10240, 12288, 14336, 16384)]
    wb0_0 = 18432 // 128
    wb1_0 = wb0_0 + 16
    wo_0 = wb1_0 + 16
    order = []
    for h in range(H):
        order += [q0 + h, k0 + h, v0 + 2 * h, v0 + 2 * h + 1]
    for cc in range(KC):
        order += [gr0 + cc]
    for cc in range(KC):
        order += [cx0 + cc, cp0 + cc, co0 + cc, gc0 + cc]
    for dd in range(KC):
        order += [mr0 + dd, mc0 + dd, wb0_0 + dd, wb1_0 + dd]
    for dd in range(KC):
        order += [wo_0 + dd]
    assert len(order) == NPIECE
    return order


def _layer_weights(w_in, w_branch, w_out, l):
    order = np.array(_piece_chunk_order())
    wall = np.concatenate([w_in[l], w_branch[l, 0], w_branch[l, 1], w_out[l]], axis=1)
    w4 = wall.reshape(KC, 128, NPIECE, 128)[:, :, order, :]
    wp = np.ascontiguousarray(w4.transpose(2, 1, 0, 3)).reshape(NPIECE * 128, 2048)
    return wp


def _tile_layout(hloc):
    a = hloc.reshape(NT, T, KC, 128).transpose(0, 3, 2, 1)
    return np.ascontiguousarray(a).reshape(NT * 128, KC * T)


def _untile(ht):
    a = ht.reshape(NT, 128, KC, T).transpose(0, 3, 2, 1)
    return np.ascontiguousarray(a).reshape(NT * T, D)


def _consts(core, norm_g, gn_g, conv_w, conv_b, final_norm_g):
    b, s = (core, 0) if SOLO else divmod(core, 4)
    c = np.zeros((128, NCONST), np.float32)
    idx = np.arange(128, dtype=np.float64)
    scale = 128.0 ** -0.5
    for h in range(H):
        g = GAMMA[h]
        diff = idx[None, :] - idx[:, None]
        m = np.where(diff >= 0, g ** np.maximum(diff, 0.0), 0.0) * scale
        c[:, C_MASK + h * 128:C_MASK + (h + 1) * 128] = m
        c[:, C_QDEC + h * 128:C_QDEC + (h + 1) * 128] = (g ** (idx + 1.0))[None, :] * scale
        c[:, C_KDEC + h] = g ** (127.0 - idx)
    c[:, C_IDENT:C_IDENT + 128] = np.eye(128)
    pm = np.zeros((128, 128))
    for m_ in range(128):
        pm[(m_ + 64) % 128, m_] = 1.0
    c[:, C_PERM:C_PERM + 128] = pm
    c[:, C_ONES:C_ONES + 128] = 1.0
    for l in range(DEPTH):
        base = C_VECS + l * 96
        c[:, base + 0:base + 16] = norm_g[l].reshape(KC, 128).T
        c[:, base + 16:base + 32] = gn_g[l].reshape(KC, 128).T
        for k in range(3):
            c[:, base + 32 + 16 * k:base + 48 + 16 * k] = conv_w[l, k].reshape(KC, 128).T
        c[:, base + 80:base + 96] = conv_b[l].reshape(KC, 128).T
    c[:, C_VECS + DEPTH * 96:C_VECS + DEPTH * 96 + 16] = final_norm_g.reshape(KC, 128).T
    for j in range(0 if SOLO else NCORE):
        bj, sj = divmod(j, 4)
        if bj == b and sj < s:
            for h in range(H):
                c[:, C_COEF + j * 8 + h] = GAMMA[h] ** (4096.0 * (s - 1 - sj))
        if bj == b and sj == s - 1:
            c[:, C_HCOEF + j] = 1.0
    c[:, C_EPS] = EPS
    c[:, C_SELF + core] = 1.0
    return c


def _rot_tables(core):
    b, s = (core, 0) if SOLO else divmod(core, 4)
    tau = np.arange(NT * T)
    pos = np.where(tau >= 128, 16 + s * 4096 + (tau - 128), np.maximum(tau - 112, 0)).astype(np.float32)
    inv_freq = (np.float32(10000.0) ** (-(np.arange(64, dtype=np.float32) / np.float32(64)))).astype(np.float32)
    ang = (pos[:, None] * inv_freq[None, :]).astype(np.float32)
    cos = np.cos(ang).astype(np.float32)
    sin = np.sin(ang).astype(np.float32)
    cosT = np.concatenate([cos, cos], axis=1).T
    sinT = np.concatenate([-sin, sin], axis=1).T
    r = np.stack([cosT.reshape(128, NT, T), sinT.reshape(128, NT, T)], axis=2)
    return np.ascontiguousarray(r.transpose(1, 0, 2, 3)).reshape(NT * 128, 2 * T)


_PROG_CACHE = {}


def _prog():
    if "p" not in _PROG_CACHE:
        _PROG_CACHE["p"] = build_program()
    return _PROG_CACHE["p"]


def kernel(x, meta_tokens, norm_g, w_in, conv_w, conv_b, gn_g, w_branch, w_out, final_norm_g):
    x = np.asarray(x, np.float32)
    meta_tokens = np.asarray(meta_tokens, np.float32)
    norm_g = np.asarray(norm_g, np.float32)
    w_in = np.asarray(w_in, np.float32)
    conv_w = np.asarray(conv_w, np.float32)
    conv_b = np.asarray(conv_b, np.float32)
    gn_g = np.asarray(gn_g, np.float32)
    w_branch = np.asarray(w_branch, np.float32)
    w_out = np.asarray(w_out, np.float32)
    final_norm_g = np.asarray(final_norm_g, np.float32)

    cores = list(range(NCORE_USED))
    nreal = (NT * T - 128)
    hs = []
    for core in cores:
        b, s = (core, 0) if SOLO else divmod(core, 4)
        hloc = np.zeros((NT * T, D), np.float32)
        if s == 0:
            hloc[112:128] = meta_tokens
        hloc[128:] = x[b, s * nreal:(s + 1) * nreal]
        hs.append(_tile_layout(hloc))
    rots = [_rot_tables(c) for c in cores]
    csts = [_consts(c, norm_g, gn_g, conv_w, conv_b, final_norm_g) for c in cores]
    wall = np.empty((DEPTH * NPIECE * 128, 2048), np.float32)
    for l in range(DEPTH):
        wall[l * NPIECE * 128:(l + 1) * NPIECE * 128] = _layer_weights(w_in, w_branch, w_out, l)

    nc = _prog()
    res = run_bass_kernel_spmd(nc, [{"hin": hs[c], "rot": rots[c], "cst": csts[c], "wall": wall} for c in cores], core_ids=cores)
    out = np.zeros((2, 16384, D), np.float32)
    for core in cores:
        b, s = (core, 0) if SOLO else divmod(core, 4)
        out[b, s * nreal:(s + 1) * nreal] = _untile(res.results[core]["hout"])[128:]
    return out
```
